# Optimizing a Trainium2 kernel written in Bass

```python
import math
import jax, jax.numpy as jnp
from jax import lax
import numpy as np

D_MODEL = 2048
BATCH = 2
SEQ = 16384
DEPTH = 1

HEAD_DIM = 64
N_Q_HEADS = (3 * D_MODEL // 4) // HEAD_DIM
N_KV_HEADS = max(1, N_Q_HEADS // 8)
Q_PER_KV = N_Q_HEADS // N_KV_HEADS
ATTN_WIDTH = N_Q_HEADS * HEAD_DIM
KV_WIDTH = N_KV_HEADS * HEAD_DIM
WINDOW = 128
BLOCK = 128
SSM_GROUP = 16
SSM_WIDTH = D_MODEL - ATTN_WIDTH
N_SSM_GROUPS = SSM_WIDTH // SSM_GROUP
STATE = 64
SSM_CHUNK = 128
DT_MIN = 1e-3
DT_MAX = 1e-1
MIX_WIDTH = ATTN_WIDTH + SSM_WIDTH
IN_WIDTH = 2 * ATTN_WIDTH + 2 * KV_WIDTH + 2 * SSM_WIDTH
SPLITS = (ATTN_WIDTH,
          ATTN_WIDTH + KV_WIDTH,
          ATTN_WIDTH + 2 * KV_WIDTH,
          2 * ATTN_WIDTH + 2 * KV_WIDTH,
          2 * ATTN_WIDTH + 2 * KV_WIDTH + SSM_WIDTH)
EPS = 1e-5

kernel_name = 'hybrid_swa_sink_s5_parallel_heads'


def rms_norm(x, gain):
    xf = x.astype(jnp.float32)
    var = jnp.mean(xf * xf, axis=-1, keepdims=True)
    return (xf * lax.rsqrt(var + EPS) * gain.astype(jnp.float32)).astype(x.dtype)


def sliding_window_attention(q, k, v, sinks):
    b, s, _ = q.shape
    nb = s // BLOCK
    q = q.reshape(b, nb, BLOCK, N_KV_HEADS, Q_PER_KV, HEAD_DIM) * (HEAD_DIM ** -0.5)
    k = k.reshape(b, nb, BLOCK, N_KV_HEADS, HEAD_DIM)
    v = v.reshape(b, nb, BLOCK, N_KV_HEADS, HEAD_DIM)
    pad = ((0, 0), (1, 0), (0, 0), (0, 0), (0, 0))
    k_band = jnp.concatenate([jnp.pad(k[:, :-1], pad), k], axis=2)
    v_band = jnp.concatenate([jnp.pad(v[:, :-1], pad), v], axis=2)
    qi = jnp.arange(BLOCK)[:, None]
    kj = jnp.arange(2 * BLOCK)[None, :]
    rel = kj - BLOCK - qi
    band = (rel <= 0) & (rel > -WINDOW)
    sink_logits = sinks.astype(jnp.float32).reshape(N_KV_HEADS, Q_PER_KV)

    def block_fn(args):
        qb, kb, vb, blk = args
        mask = band & ((blk > 0) | (kj >= BLOCK))
        scores = jnp.einsum('bqhgd,bkhd->bhgqk', qb.astype(jnp.float32), kb.astype(jnp.float32))
        scores = jnp.where(mask, scores, -jnp.inf)
        sink = jnp.broadcast_to(sink_logits[None, :, :, None, None], scores.shape[:-1] + (1,))
        probs = jax.nn.softmax(jnp.concatenate([scores, sink], axis=-1), axis=-1)[..., :-1]
        return jnp.einsum('bhgqk,bkhd->bqhgd', probs.astype(vb.dtype), vb)

    out = lax.map(block_fn, (jnp.moveaxis(q, 1, 0), jnp.moveaxis(k_band, 1, 0),
                             jnp.moveaxis(v_band, 1, 0), jnp.arange(nb)))
    return jnp.moveaxis(out, 0, 1).reshape(b, s, ATTN_WIDTH)


def _complex_affine_combine(e1, e2):
    a1r, a1i, b1r, b1i = e1
    a2r, a2i, b2r, b2i = e2
    return (a2r * a1r - a2i * a1i,
            a2r * a1i + a2i * a1r,
            a2r * b1r - a2i * b1i + b2r,
            a2r * b1i + a2i * b1r + b2i)


def s5_ssm(u, lam_re, lam_im, log_step, b_re, b_im, c_re, c_im, d_skip):
    f32 = jnp.float32
    bsz, s, _ = u.shape
    uf = u.astype(f32).reshape(bsz, s, N_SSM_GROUPS, SSM_GROUP)
    lr, li = lam_re.astype(f32), lam_im.astype(f32)
    step = jnp.exp(log_step.astype(f32))[:, None]
    decay = jnp.exp(lr * step)
    abar_re, abar_im = decay * jnp.cos(li * step), decay * jnp.sin(li * step)
    den = lr * lr + li * li
    nr, ni = abar_re - 1.0, abar_im
    coef_re = (nr * lr + ni * li) / den
    coef_im = (ni * lr - nr * li) / den
    br, bi = b_re.astype(f32), b_im.astype(f32)
    bbar_re = coef_re[..., None] * br - coef_im[..., None] * bi
    bbar_im = coef_re[..., None] * bi + coef_im[..., None] * br
    cr, ci = c_re.astype(f32), c_im.astype(f32)
    nc = s // SSM_CHUNK
    u_chunks = jnp.moveaxis(uf.reshape(bsz, nc, SSM_CHUNK, N_SSM_GROUPS, SSM_GROUP), 1, 0)
    a_re = jnp.broadcast_to(abar_re, (bsz, SSM_CHUNK, N_SSM_GROUPS, STATE))
    a_im = jnp.broadcast_to(abar_im, (bsz, SSM_CHUNK, N_SSM_GROUPS, STATE))

    def chunk_fn(carry, uc):
        h0r, h0i = carry
        bu_re = jnp.einsum('btgc,gpc->btgp', uc, bbar_re)
        bu_im = jnp.einsum('btgc,gpc->btgp', uc, bbar_im)
        acc_r, acc_i, hr, hi = lax.associative_scan(
            _complex_affine_combine, (a_re, a_im, bu_re, bu_im), axis=1)
        hr, hi = (hr + acc_r * h0r[:, None] - acc_i * h0i[:, None],
                  hi + acc_r * h0i[:, None] + acc_i * h0r[:, None])
        y = jnp.einsum('btgp,gcp->btgc', hr, cr) - jnp.einsum('btgp,gcp->btgc', hi, ci)
        return (hr[:, -1], hi[:, -1]), y

    init = (jnp.zeros((bsz, N_SSM_GROUPS, STATE), f32), jnp.zeros((bsz, N_SSM_GROUPS, STATE), f32))
    _, y = lax.scan(chunk_fn, init, u_chunks)
    y = jnp.moveaxis(y, 0, 1).reshape(bsz, s, N_SSM_GROUPS, SSM_GROUP)
    y = y + d_skip.astype(f32).reshape(N_SSM_GROUPS, SSM_GROUP) * uf
    return y.reshape(bsz, s, SSM_WIDTH).astype(u.dtype)


def hybrid_layer(x, c, w_ada, b_ada, norm_gain, w_in, b_in, attn_sinks, attn_out_gain,
                 lam_re, lam_im, log_step, b_re, b_im, c_re, c_im, d_skip,
                 glu_w, glu_b, ssm_out_gain, w_out):
    mod = jax.nn.silu(c.astype(jnp.float32)) @ w_ada.astype(jnp.float32) + b_ada.astype(jnp.float32)
    shift, scale, gate = jnp.split(mod, 3, axis=-1)
    h = (rms_norm(x, norm_gain) * (1.0 + scale[:, None]) + shift[:, None]).astype(x.dtype)
    proj = h @ w_in + b_in
    q, k, v, z_attn, u_ssm, z_ssm = jnp.split(proj, SPLITS, axis=-1)
    attn = sliding_window_attention(q, k, v, attn_sinks)
    attn = rms_norm(attn, attn_out_gain) * jax.nn.silu(z_attn)
    ssm = jax.nn.gelu(s5_ssm(u_ssm, lam_re, lam_im, log_step, b_re, b_im, c_re, c_im, d_skip), approximate=False)
    ssm = ssm * jax.nn.sigmoid(ssm @ glu_w + glu_b)
    ssm = rms_norm(ssm, ssm_out_gain) * jax.nn.silu(z_ssm)
    mixed = jnp.concatenate([attn, ssm], axis=-1)
    out = mixed @ w_out
    return (x + gate[:, None].astype(x.dtype) * out).astype(x.dtype)


def setup_inputs(seed: int = 0) -> dict:
    key = jax.random.key(seed)
    ks = jax.random.split(key, 24)
    f32 = jnp.float32
    L, D, G, P = DEPTH, D_MODEL, N_SSM_GROUPS, STATE
    nrm = lambda k, shp, s: jax.random.normal(k, shp, f32) * s
    lam_im = jnp.broadcast_to(math.pi * jnp.arange(P, dtype=f32), (L, G, P))
    return {
        'x': nrm(ks[0], (BATCH, SEQ, D), 1.0),
        'c': nrm(ks[1], (BATCH, D), 1.0),
        'w_ada': nrm(ks[2], (L, D, 3 * D), 0.5 * D ** -0.5),
        'b_ada': nrm(ks[3], (L, 3 * D), 0.01),
        'norm_gain': 1.0 + nrm(ks[4], (L, D), 0.01),
        'w_in': nrm(ks[5], (L, D, IN_WIDTH), D ** -0.5),
        'b_in': nrm(ks[6], (L, IN_WIDTH), 0.01),
        'attn_sinks': nrm(ks[7], (L, N_Q_HEADS), 1.0),
        'attn_out_gain': 1.0 + nrm(ks[8], (L, ATTN_WIDTH), 0.01),
        'ssm_lambda_re': -0.5 + nrm(ks[9], (L, G, P), 0.01),
        'ssm_lambda_im': lam_im + 0.0 * nrm(ks[10], (L, G, P), 1.0) if False else lam_im + nrm(ks[10], (L, G, P), 0.01),
        'ssm_log_step': jax.random.uniform(ks[11], (L, G), f32, math.log(DT_MIN), math.log(DT_MAX)),
        'ssm_b_re': nrm(ks[12], (L, G, P, SSM_GROUP), (2 * SSM_GROUP) ** -0.5),
        'ssm_b_im': nrm(ks[13], (L, G, P, SSM_GROUP), (2 * SSM_GROUP) ** -0.5),
        'ssm_c_re': nrm(ks[14], (L, G, SSM_GROUP, P), P ** -0.5),
        'ssm_c_im': nrm(ks[15], (L, G, SSM_GROUP, P), P ** -0.5),
        'ssm_d': nrm(ks[16], (L, SSM_WIDTH), 1.0),
        'glu_w': nrm(ks[17], (L, SSM_WIDTH, SSM_WIDTH), SSM_WIDTH ** -0.5),
        'glu_b': nrm(ks[18], (L, SSM_WIDTH), 0.01),
        'ssm_out_gain': 1.0 + nrm(ks[19], (L, SSM_WIDTH), 0.01),
        'w_out': nrm(ks[20], (L, MIX_WIDTH, D), MIX_WIDTH ** -0.5),
        'final_gain': 1.0 + nrm(ks[21], (D,), 0.01),
    }


def reference(x, c, w_ada, b_ada, norm_gain, w_in, b_in, attn_sinks, attn_out_gain,
              ssm_lambda_re, ssm_lambda_im, ssm_log_step, ssm_b_re, ssm_b_im,
              ssm_c_re, ssm_c_im, ssm_d, glu_w, glu_b, ssm_out_gain, w_out, final_gain):
    h = x
    for l in range(DEPTH):
        h = hybrid_layer(h, c, w_ada[l], b_ada[l], norm_gain[l], w_in[l], b_in[l],
                         attn_sinks[l], attn_out_gain[l],
                         ssm_lambda_re[l], ssm_lambda_im[l], ssm_log_step[l],
                         ssm_b_re[l], ssm_b_im[l], ssm_c_re[l], ssm_c_im[l], ssm_d[l],
                         glu_w[l], glu_b[l], ssm_out_gain[l], w_out[l])
    return rms_norm(h, final_gain)
```

```python
import math
from contextlib import ExitStack
import numpy as np
import concourse.bass as bass
import concourse.mybir as mybir
from concourse.bass_utils import run_bass_kernel_spmd

F32 = mybir.dt.float32
BF16 = mybir.dt.bfloat16
I32 = mybir.dt.int32
AF = mybir.ActivationFunctionType
ALU = mybir.AluOpType
AX = mybir.AxisListType

NCORES = 8
D = 2048
T_CORE = 4096
NT = 32
ST = 4
NST = NT // ST
NPRE_ST = 24
EPS = 1e-5
SEM_CAP = 20000
TWO_PI = 2.0 * math.pi


class Buf:
    __slots__ = ("name", "w", "r")

    def __init__(self, name):
        self.name = name
        self.w = {}
        self.r = {}


class _Op:
    __slots__ = ("eng", "fn", "deps", "dma", "sig", "idx", "signal", "presem")

    def __init__(self, eng, fn, dma):
        self.eng = eng
        self.fn = fn
        self.dma = dma
        self.deps = []
        self.sig = None
        self.signal = False
        self.presem = None


class _Rec:
    def __init__(self):
        self.call = None

    def __getattr__(self, name):
        def f(*a, **kw):
            self.call = (name, a, kw)
            return self
        return f


class Plan:
    ENGS = ("sync", "scalar", "vector", "gpsimd", "tensor")

    def __init__(self):
        self.streams = {e: [] for e in self.ENGS}
        self.all_ops = []
        self.out_dmas = []

    def op(self, eng, fn, reads=(), writes=(), dma=False, out=False):
        rec = _Rec()
        fn(rec)
        o = _Op(eng, rec.call, dma)
        o.idx = len(self.all_ops)
        deps = set()
        for b in reads:
            deps.update(b.w.values())
        for b in writes:
            deps.update(b.w.values())
            deps.update(b.r.values())
        key = ("dma", o.idx) if dma else eng
        fdeps = []
        for d in deps:
            dop = self.all_ops[d]
            if (not dop.dma) and dop.eng == eng and (not dma) and eng == "tensor":
                continue
            fdeps.append(d)
        o.deps = fdeps
        for d in fdeps:
            self.all_ops[d].signal = True
        for b in reads:
            b.r[key] = o.idx
        for b in writes:
            b.w = {key: o.idx}
            b.r = {}
        self.streams[eng].append(o)
        self.all_ops.append(o)
        if out:
            self.out_dmas.append(o.idx)
        return o

    def finalize(self, n_dma_sems):
        if self.out_dmas:
            fin = _Op("sync", None, False)
            fin.idx = len(self.all_ops)
            fin.deps = list(self.out_dmas)
            for d in fin.deps:
                self.all_ops[d].signal = True
            self.streams["sync"].append(fin)
            self.all_ops.append(fin)
        cnt = {e: [0, 0] for e in self.ENGS}
        dma_vals = [0] * n_dma_sems
        dma_used = [False] * n_dma_sems
        rr = 0
        for o in self.all_ops:
            if o.dma:
                s = rr
                rr = (rr + 1) % n_dma_sems
                if dma_used[s]:
                    o.presem = ("dma", s, dma_vals[s])
                dma_vals[s] += 16
                dma_used[s] = True
                o.sig = ("dma", s, dma_vals[s])
            elif o.signal:
                c = cnt[o.eng]
                if c[1] >= SEM_CAP:
                    c[0] += 1
                    c[1] = 0
                c[1] += 1
                o.sig = (o.eng, c[0], c[1])
        return {e: cnt[e][0] + 1 for e in self.ENGS}

    def run_stream(self, eng_name, e, sems):
        seen = {}
        for o in self.streams[eng_name]:
            waits = {}
            if o.presem is not None:
                waits[o.presem[:2]] = o.presem[2]
            for d in o.deps:
                sg = self.all_ops[d].sig
                k = sg[:2]
                if waits.get(k, 0) < sg[2]:
                    waits[k] = sg[2]
            for k, v in waits.items():
                if seen.get(k, 0) >= v:
                    continue
                seen[k] = v
                e.wait_ge(sems[k[0]][k[1]], v)
            if o.fn is None:
                continue
            name, a, kw = o.fn
            ins = getattr(e, name)(*a, **kw)
            if o.sig is not None:
                if o.dma:
                    ins.then_inc(sems["dma"][o.sig[1]], 16)
                else:
                    ins.then_inc(sems[o.sig[0]][o.sig[1]], 1)


def build_program(nst=NST, debug=False, carry=True):
    nc = bass.Bass("TRN2", target_bir_lowering=False)
    P = Plan()
    nrows = 128 + nst * ST * 128

    def din(name, shape, dt=F32):
        return nc.dram_tensor(name, list(shape), dt, kind="ExternalInput").ap()

    x_d = din("x", [128 + T_CORE, D])
    vecsA_d = din("vecsA", [84, 128])
    vecsB_d = din("vecsB", [67, 128])
    wada_d = din("w_ada", [D, 3 * D])
    bada_d = din("b_ada", [1, 3 * D])
    win_d = din("w_in", [D, 4480])
    bin_d = din("b_in", [1, 4480])
    wout_d = din("w_out", [D, D])
    glu_w_d = din("glu_w", [512, 512])
    glu_b_d = din("glu_b", [1, 512])
    fg_d = din("final_gain", [1, D])
    sinks_d = din("sinks", [1, 24])
    bre_d = din("b_re", [32, 64, 16])
    bim_d = din("b_im", [32, 64, 16])
    cre_d = din("c_re", [32, 16, 64])
    cim_d = din("c_im", [32, 16, 64])
    flags_d = din("flags", [128, 4])
    xprev_d = din("xprev", [NPRE_ST * ST * 128, D])
    cmask_d = din("cmask", [128, NPRE_ST * ST])
    out_d = nc.dram_tensor("out", [T_CORE, D], F32, kind="ExternalOutput").ap()
    dbg = {}

    def dout(name, shape):
        dbg[name] = nc.dram_tensor(name, list(shape), F32, kind="ExternalOutput").ap()
        return dbg[name]

    wfm = nc.dram_tensor("wfm", [D, 2432], BF16).ap()
    wtm = nc.dram_tensor("wtm", [D, 2240], BF16).ap()
    wo = nc.dram_tensor("wo", [D, D], BF16).ap()
    B_wfm, B_wtm, B_wo = Buf("wfm"), Buf("wtm"), Buf("wo")

    es = ExitStack()
    with es:
        def sb(name, shape, dt=F32):
            return es.enter_context(nc.sbuf_tensor("sb_" + name, list(shape), dt))

        banks = [es.enter_context(nc.psum_tensor("bank%d" % i, [128, 512], F32)) for i in range(8)]
        B_bank = [Buf("bank%d" % i) for i in range(8)]
        bank_rr = [0]

        def nb():
            i = bank_rr[0]
            bank_rr[0] = (i + 1) % 7
            return banks[i], B_bank[i]

        ident_f = sb("ident_f", [128, 128]); B_idf = Buf("idf")
        ident_b = sb("ident_b", [128, 128], BF16); B_idb = Buf("idb")
        ones_b = sb("ones_b", [1, 512], BF16); B_ones = Buf("ones")
        epsc = sb("epsc", [128, 1]); B_epsc = Buf("epsc")
        vt = sb("vt", [128, 84]); B_vt = Buf("vt")
        vtb = sb("vtb", [128, 67]); B_vtb = Buf("vtb")
        vstage = sb("vstage", [84, 128]); B_vst = Buf("vstage")
        vstageb = sb("vstageb", [67, 128]); B_vstb = Buf("vstageb")
        scfm = sb("scfm", [128, 16]); B_scfm = Buf("scfm")
        modfm = sb("modfm", [128, 32]); B_modfm = Buf("modfm")
        gsfm = sb("gsfm", [128, 16]); B_gsfm = Buf("gsfm")
        gateB = sb("gateB", [128, D]); B_gateB = Buf("gateB")
        fgB = sb("fgB", [128, D]); B_fgB = Buf("fgB")
        esink = sb("esink", [128, 24]); B_esink = Buf("esink")
        flags = sb("flags", [128, 4]); B_flags = Buf("flags")
        maskc = sb("maskc", [128, 128], BF16); B_maskc = Buf("maskc")
        maskp = sb("maskp", [128, 128], BF16); B_maskp = Buf("maskp")
        maskp0 = sb("maskp0", [128, 128], BF16); B_maskp0 = Buf("maskp0")
        brow_tm = sb("brow_tm", [1, 2240], BF16); B_btm = Buf("brow_tm")
        brow_glu = sb("brow_glu", [1, 512], BF16); B_bglu = Buf("brow_glu")
        wglu = sb("wglu", [128, 4, 512], BF16); B_wglu = Buf("wglu")
        xt = sb("xt", [128, D]); B_xt = Buf("xt")
        xn = sb("xn", [128, D], BF16); B_xn = Buf("xn")
        junk = xn; B_junk = B_xn
        hT = sb("hT", [128, 16, 512], BF16); B_hT = [Buf("hT%d" % i) for i in range(ST)]
        wsl = [sb("wsl%d" % i, [128, 16, 256], BF16) for i in range(2)]
        B_wsl = [Buf("wsl%d" % i) for i in range(2)]
        wsl_rr = [0]
        arena = sb("arena", [128, 8192])
        ab = arena[:, :].bitcast(BF16)
        QT = ab[:, 0:6144].rearrange("p (c t) -> p c t", c=12)
        za = ab[:, 6144:12288].rearrange("p (t n) -> p t n", t=4)
        UT = ab[:, 12288:14336].rearrange("p (c t) -> p c t", c=4)
        zs = ab[:, 14336:16384].rearrange("p (t n) -> p t n", t=4)
        cells = []
        B_QT = [Buf("QT%d" % c) for c in range(12)]
        B_za = [Buf("za%d" % t) for t in range(4)]
        B_UT = [Buf("UT%d" % c) for c in range(4)]
        B_zs = [Buf("zs%d" % t) for t in range(4)]
        for c in range(12):
            cells.append((c * 1024, (c + 1) * 1024, B_QT[c]))
        for t in range(4):
            cells.append((12288 + t * 3072, 12288 + (t + 1) * 3072, B_za[t]))
        for c in range(4):
            cells.append((24576 + c * 1024, 24576 + (c + 1) * 1024, B_UT[c]))
        for t in range(4):
            cells.append((28672 + t * 1024, 28672 + (t + 1) * 1024, B_zs[t]))

        def xr_cells(t):
            lo, hi = t * 8192, (t + 1) * 8192
            return [b for (l, h, b) in cells if l < hi and h > lo]
        all_cells = [b for (_, _, b) in cells]

        def xr(t):
            return arena[:, t * 2048:(t + 1) * 2048]

        KT = sb("KT", [128, 3, 640], BF16); B_KT = [Buf("KT%d" % g) for g in range(3)]
        Vt = sb("Vt", [128, 5, 3, 65], BF16); B_Vt = [Buf("Vt%d" % i) for i in range(5)]
        Eb = [sb("Eb%d" % i, [128, 512], BF16) for i in range(8)]; B_Eb = [Buf("Eb%d" % i) for i in range(8)]
        Mc = sb("Mc", [128, 512], BF16); Mp = sb("Mp", [128, 512], BF16); Mp0 = sb("Mp0", [128, 512], BF16)
        B_Mc, B_Mp, B_Mp0 = Buf("Mc"), Buf("Mp"), Buf("Mp0")
        attn_f = sb("attn_f", [128, 1536]); B_attn = [Buf("attn%d" % g) for g in range(3)]
        den = sb("den", [128, 3, 8]); B_den = [Buf("den%d" % g) for g in range(3)]
        rden = sb("rden", [128, 3, 8]); B_rden = [Buf("rden%d" % g) for g in range(3)]
        stat = sb("stat", [128, 8]); B_stat = [Buf("stat%d" % i) for i in range(8)]
        mixed = sb("mixed", [128, D], BF16); B_mixed = Buf("mixed"); B_mixA = Buf("mixA"); B_mixS = Buf("mixS")
        CT = sb("CT", [128, 16, 128]); SN = sb("SN", [128, 16, 128]); B_tab = Buf("tab")
        BL = sb("BL", [128, 32, 128], BF16); B_BL = Buf("BL")
        Cm = sb("Cm", [128, 48, 32], BF16); B_Cm = Buf("Cm")
        p24 = sb("p24", [128, 2, 512], BF16); B_p24 = [Buf("p24a"), Buf("p24b")]
        Dd = sb("Dd", [128, 4, 128], BF16); B_Dd = Buf("Dd")
        rho = sb("rho", [128, 16]); c128 = sb("c128", [128, 16]); s128 = sb("s128", [128, 16]); B_par = Buf("par")
        gin_r = sb("gin_r", [128, 16]); gin_i = sb("gin_i", [128, 16]); B_gin = [Buf("gin%d" % q) for q in range(4)]
        st1 = sb("st1", [128, 512]); st2 = sb("st2", [128, 512]); st3 = sb("st3", [128, 512]); st4 = sb("st4", [128, 512])
        B_st = [Buf("st%d" % i) for i in range(4)]
        st5 = sb("st5", [128, 512]); st6 = sb("st6", [128, 512]); st7 = sb("st7", [128, 512]); st8 = sb("st8", [128, 512])
        B_stb = [Buf("stb%d" % i) for i in range(4)]
        spool2 = sb("spool2", [128, 1536])
        p24b = sb("p24b", [128, 2, 512], BF16)
        wr_ = st1; wi_ = st3; B_wr, B_wi = B_st[0], B_st[2]
        spool = sb("spool", [128, 2048])
        gr_ = spool[:, 0:512].rearrange("p (a b) -> p a b", a=4); gi_ = spool[:, 512:1024].rearrange("p (a b) -> p a b", a=4)
        B_gr, B_gi = Buf("gr"), Buf("gi")
        hr_ = spool[:, 1024:1280].bitcast(BF16).rearrange("p (a b) -> p a b", a=4)
        hi_ = spool[:, 1280:1536].bitcast(BF16).rearrange("p (a b) -> p a b", a=4); B_hr, B_hi = Buf("hr"), Buf("hi")
        ctmp = sb("ctmp", [128, 8, 4]); B_ctmp = Buf("ctmp")
        g1 = spool[:, 1536:2048]; B_g1 = Buf("g1")
        g1b = sb("g1b", [128, 512], BF16); B_g1b = Buf("g1b")
        g1T = sb("g1T", [128, 4, 128], BF16); B_g1T = Buf("g1T")
        sg = sb("sg", [128, 512]); B_sg = Buf("sg")
        s2 = sg; B_s2 = B_sg
        otmp = st2; B_otmp = B_st[1]
        sm = sb("sm", [128, 64]); B_sm = Buf("sm")
        par2 = sb("par2", [128, 64]); B_par2 = Buf("par2")
        bbar = sb("bbar", [128, 2, 16, 16]); B_bbar = Buf("bbar")
        decT = spool[:, :].bitcast(BF16).rearrange("p (q i n) -> p q i n", q=16, i=2); B_decT = Buf("decT")
        ssA = sb("ssA", [128, 4]); B_ssA = [Buf("ssA%d" % i) for i in range(3)]
        brow_u = sb("brow_u", [1, 512], BF16); B_bu = Buf("brow_u")
        u0B = sb("u0B", [128, 512]); B_u0B = Buf("u0B")
        rsA = sb("rsA", [128, 4]); B_rsA = [Buf("rsA%d" % i) for i in range(4)]
        cmask = sb("cmask", [128, NPRE_ST * ST]); B_cmask = Buf("cmask")
        hloc = sb("hloc", [128, 8, 16]); B_hloc = Buf("hloc")
        uTM = mixed[:, :].rearrange("p (t n) -> p t n", t=4)
        SKr = attn_f[:, 0:512].rearrange("p (q k) -> p q k", q=16)
        SKi = attn_f[:, 512:1024].rearrange("p (q k) -> p q k", q=16)

        def V(fn, r=(), w=()): return P.op("vector", fn, r, w)
        def A(fn, r=(), w=()): return P.op("scalar", fn, r, w)
        def G(fn, r=(), w=()): return P.op("gpsimd", fn, r, w)
        def T(fn, r=(), w=()): return P.op("tensor", fn, r, w)
        B_swq = Buf("swdge_chain")

        def DMA(fn, r=(), w=(), q="sync", out=False):
            if q == "gpsimd":
                return P.op(q, fn, tuple(r), tuple(w) + (B_swq,), dma=True, out=out)
            return P.op(q, fn, r, w, dma=True, out=out)

        def cast(dst, src, wbuf):
            DMA(lambda e: e.dma_start(out=dst, in_=src), (), (wbuf,), q="gpsimd")
        cast(wfm[:, 0:1536], win_d[:, 0:1536], B_wfm)
        for g in range(3):
            cast(wfm[:, 1536 + 128 * g:1536 + 128 * g + 64], win_d[:, 1536 + 64 * g:1600 + 64 * g], B_wfm)
            cast(wfm[:, 1536 + 128 * g + 64:1536 + 128 * g + 128], win_d[:, 1536 + 64 * g:1600 + 64 * g], B_wfm)
        cast(wfm[:, 1920:2432], win_d[:, 3456:3968], B_wfm)
        cast(wtm[:, 0:192], win_d[:, 1728:1920], B_wtm)
        cast(wtm[:, 192:1728], win_d[:, 1920:3456], B_wtm)
        cast(wtm[:, 1728:2240], win_d[:, 3968:4480], B_wtm)
        cast(wo[:, :], wout_d[:, :], B_wo)
        DMA(lambda e: e.dma_start(out=brow_tm[0:1, 0:192], in_=bin_d[0:1, 1728:1920]), (), (B_btm,), q="gpsimd")
        DMA(lambda e: e.dma_start(out=brow_tm[0:1, 192:1728], in_=bin_d[0:1, 1920:3456]), (), (B_btm,), q="gpsimd")
        DMA(lambda e: e.dma_start(out=brow_tm[0:1, 1728:2240], in_=bin_d[0:1, 3968:4480]), (), (B_btm,), q="gpsimd")
        DMA(lambda e: e.dma_start(out=brow_glu[0:1, :], in_=glu_b_d[0:1, :]), (), (B_bglu,), q="gpsimd")
        DMA(lambda e: e.dma_start(out=brow_u[0:1, :], in_=bin_d[0:1, 3456:3968]), (), (B_bu,), q="gpsimd")
        DMA(lambda e: e.dma_start(out=cmask[:, :], in_=cmask_d), (), (B_cmask,))
        DMA(lambda e: e.dma_start(out=wglu[:, :, :], in_=glu_w_d.rearrange("(k p) n -> p k n", p=128)), (), (B_wglu,), q="gpsimd")

        DMA(lambda e: e.dma_start(out=vstage[:, :], in_=vecsA_d), (), (B_vst,))
        DMA(lambda e: e.dma_start(out=vstageb[:, :], in_=vecsB_d), (), (B_vstb,))
        DMA(lambda e: e.dma_start(out=flags[:, :], in_=flags_d), (), (B_flags,))
        DMA(lambda e: e.dma_start(out=fgB[:, :], in_=fg_d[0:1, :].partition_broadcast(128)), (), (B_fgB,))
        DMA(lambda e: e.dma_start(out=gateB[:, :], in_=bada_d[0:1, 2 * D:3 * D].partition_broadcast(128)), (), (B_gateB,))
        DMA(lambda e: e.dma_start(out=esink[:, :], in_=sinks_d[0:1, :].partition_broadcast(128)), (), (B_esink,))

        G(lambda e: e.memset(ident_f[:, :], 1.0), (), (B_idf,))
        G(lambda e: e.affine_select(out=ident_f[:, :], in_=ident_f[:, :], pattern=[[-1, 128]], compare_op=ALU.is_equal,
                                    fill=0.0, base=0, channel_multiplier=1), (B_idf,), (B_idf,))
        V(lambda e: e.tensor_copy(out=ident_b[:, :], in_=ident_f[:, :]), (B_idf,), (B_idb,))
        V(lambda e: e.memset(ones_b[:, :], 1.0), (), (B_ones,))
        V(lambda e: e.memset(epsc[:, :], EPS), (), (B_epsc,))
        G(lambda e: e.memset(maskc[:, :], 1.0), (), (B_maskc,))
        G(lambda e: e.affine_select(out=maskc[:, :], in_=maskc[:, :], pattern=[[1, 128]], compare_op=ALU.is_ge,
                                    fill=0.0, base=0, channel_multiplier=-1), (B_maskc,), (B_maskc,))
        G(lambda e: e.memset(maskp[:, :], 1.0), (), (B_maskp,))
        G(lambda e: e.affine_select(out=maskp[:, :], in_=maskp[:, :], pattern=[[-1, 128]], compare_op=ALU.is_gt,
                                    fill=0.0, base=0, channel_multiplier=1), (B_maskp,), (B_maskp,))
        V(lambda e: e.tensor_scalar(out=maskp0[:, :], in0=maskp[:, :], scalar1=flags[:, 0:1], scalar2=None, op0=ALU.mult),
          (B_maskp, B_flags), (B_maskp0,))
        A(lambda e: e.activation(out=esink[:, :], in_=esink[:, :], func=AF.Exp), (B_esink,), (B_esink,))
        for (Mx, Bx, m01, Bm) in ((Mc, B_Mc, maskc, B_maskc), (Mp, B_Mp, maskp, B_maskp), (Mp0, B_Mp0, maskp0, B_maskp0)):
            V((lambda Mx, m01: lambda e: e.tensor_scalar(out=Mx[:, :].rearrange("p (j q) -> p j q", j=4),
                                                         in0=m01[:, :].unsqueeze(1).to_broadcast([128, 4, 128]),
                                                         scalar1=-1.0, scalar2=30000.0, op0=ALU.add, op1=ALU.mult))(Mx, m01), (Bm,), (Bx,))
        V(lambda e: e.memset(Vt[:, :, :, 64:65], 1.0), (), tuple(B_Vt))

        bk, Bb = nb()
        T(lambda e: e.transpose(bk[:, 0:84], vstage[:, :], ident_f[0:84, 0:84]), (B_vst, B_idf), (Bb,))
        V(lambda e: e.tensor_copy(out=vt[:, :], in_=bk[:, 0:84]), (Bb,), (B_vt,))
        bk2, Bb2 = nb()
        T(lambda e: e.transpose(bk2[:, 0:67], vstageb[:, :], ident_f[0:67, 0:67]), (B_vstb, B_idf), (Bb2,))
        V(lambda e: e.tensor_copy(out=vtb[:, :], in_=bk2[:, 0:67]), (Bb2,), (B_vtb,))
        A(lambda e: e.activation(out=scfm[:, :], in_=vt[:, 64:80], func=AF.Silu), (B_vt,), (B_scfm,))

        screp = xt[:, :].rearrange("p (k m) -> p k m", k=16)
        V(lambda e: e.tensor_copy(out=screp, in_=scfm[:, :].unsqueeze(2).to_broadcast([128, 16, 128])), (B_scfm,), (B_xt,))
        wada_v = wada_d.rearrange("(k p) n -> p k n", p=128)
        stg = [arena[:, 0:4096].rearrange("p (k n) -> p k n", k=16), arena[:, 4096:8192].rearrange("p (k n) -> p k n", k=16)]
        B_stg = [Buf("stg0"), Buf("stg1")]
        modbk, B_modbk = banks[7], B_bank[7]
        for pc in range(24):
            s = pc % 2
            DMA((lambda pc, s: lambda e: e.dma_start(out=stg[s], in_=wada_v[:, :, pc * 256:(pc + 1) * 256]))(pc, s),
                (), (B_stg[s],))
            if pc < 16:
                for cc in range(2):
                    j = pc * 2 + cc
                    for k in range(16):
                        T((lambda s, cc, k, j: lambda e: e.matmul(modbk[:, j:j + 1], lhsT=stg[s][:, k, cc * 128:(cc + 1) * 128],
                                                                  rhs=scfm[:, k:k + 1], start=(k == 0), stop=(k == 15)))(s, cc, k, j),
                          (B_stg[s], B_scfm), (B_modbk,))
            else:
                gb, B_gb = nb()
                for k in range(16):
                    T((lambda s, k, gb: lambda e: e.matmul(gb[:, 0:256], lhsT=screp[:, k, :], rhs=stg[s][:, k, :],
                                                           start=(k == 0), stop=(k == 15)))(s, k, gb),
                      (B_stg[s], B_xt), (B_gb,))
                c0 = (pc - 16) * 256
                V((lambda gb, c0: lambda e: e.tensor_tensor(out=gateB[:, c0:c0 + 256], in0=gb[:, 0:256], in1=gateB[:, c0:c0 + 256],
                                                            op=ALU.add))(gb, c0), (B_gb, B_gateB), (B_gateB,))
        V(lambda e: e.tensor_tensor(out=modfm[:, :], in0=modbk[:, 0:32], in1=vt[:, 0:32], op=ALU.add), (B_modbk, B_vt), (B_modfm,))
        V(lambda e: e.scalar_tensor_tensor(out=gsfm[:, :], in0=modfm[:, 16:32], scalar=1.0, in1=vt[:, 32:48],
                                           op0=ALU.add, op1=ALU.mult), (B_modfm, B_vt), (B_gsfm,))
        for b in all_cells:
            b.w = dict(B_stg[0].w); b.w.update(B_stg[1].w)
            b.r = dict(B_stg[0].r); b.r.update(B_stg[1].r)

        step = sm[:, 0:16]; lrs = sm[:, 16:32]; th = sm[:, 32:48]; tmpa = sm[:, 48:64]
        A(lambda e: e.activation(out=step, in_=vtb[:, 32:48], func=AF.Exp), (B_vtb,), (B_sm,))
        V(lambda e: e.tensor_tensor(out=lrs, in0=vtb[:, 0:16], in1=step, op=ALU.mult), (B_vtb, B_sm), (B_sm,))
        V(lambda e: e.tensor_tensor(out=th, in0=vtb[:, 16:32], in1=step, op=ALU.mult), (B_vtb, B_sm), (B_sm,))
        A(lambda e: e.activation(out=rho[:, :], in_=lrs, func=AF.Exp), (B_sm,), (B_par,))

        def sincos(out_sin, out_cos, ang, tmps, rbufs, wbufs):
            tf, tf2 = tmps
            ti = tf2.bitcast(I32)
            allb = tuple(rbufs) + tuple(wbufs)
            V(lambda e: e.tensor_scalar(out=tf, in0=ang, scalar1=1.0 / TWO_PI, scalar2=None, op0=ALU.mult), allb, wbufs)
            for k, dst in enumerate((out_sin, out_cos)):
                if k == 1:
                    V(lambda e: e.tensor_scalar(out=tf, in0=tf, scalar1=0.25, scalar2=None, op0=ALU.add), wbufs, wbufs)
                V(lambda e: e.tensor_copy(out=ti, in_=tf), wbufs, wbufs)
                V(lambda e: e.tensor_copy(out=tf2, in_=ti), wbufs, wbufs)
                V(lambda e: e.tensor_tensor(out=tf2, in0=tf, in1=tf2, op=ALU.subtract), wbufs, wbufs)
                A(lambda e: e.activation(out=dst, in_=tf2, func=AF.Sin, scale=TWO_PI), wbufs, wbufs)

        mixf = mixed[:, :].bitcast(F32)
        sm2 = mixf[:, 256:384]; B_sm2 = B_mixed
        sth = sm2[:, 0:16]; cth = sm2[:, 16:32]; nr = sm2[:, 32:48]; ni = sm2[:, 48:64]
        dn = sm2[:, 64:80]; cfr = sm2[:, 80:96]; cfi = sm2[:, 96:112]; t5 = sm2[:, 112:128]
        sincos(sth, cth, th, (t5, dn), (B_sm,), (B_sm2,))
        V(lambda e: e.tensor_tensor(out=nr, in0=cth, in1=rho[:, :], op=ALU.mult), (B_sm2, B_par), (B_sm2,))
        V(lambda e: e.tensor_scalar(out=nr, in0=nr, scalar1=-1.0, scalar2=None, op0=ALU.add), (B_sm2,), (B_sm2,))
        V(lambda e: e.tensor_tensor(out=ni, in0=sth, in1=rho[:, :], op=ALU.mult), (B_sm2, B_par), (B_sm2,))
        lre = vtb[:, 0:16]; lim = vtb[:, 16:32]
        V(lambda e: e.tensor_tensor(out=dn, in0=lre, in1=lre, op=ALU.mult), (B_vtb,), (B_sm2,))
        V(lambda e: e.tensor_tensor(out=t5, in0=lim, in1=lim, op=ALU.mult), (B_vtb,), (B_sm2,))
        V(lambda e: e.tensor_tensor(out=dn, in0=dn, in1=t5, op=ALU.add), (B_sm2,), (B_sm2,))
        V(lambda e: e.reciprocal(out=dn, in_=dn), (B_sm2,), (B_sm2,))
        V(lambda e: e.tensor_tensor(out=cfr, in0=nr, in1=lre, op=ALU.mult), (B_sm2, B_vtb), (B_sm2,))
        V(lambda e: e.tensor_tensor(out=t5, in0=ni, in1=lim, op=ALU.mult), (B_sm2, B_vtb), (B_sm2,))
        V(lambda e: e.tensor_tensor(out=cfr, in0=cfr, in1=t5, op=ALU.add), (B_sm2,), (B_sm2,))
        V(lambda e: e.tensor_tensor(out=cfr, in0=cfr, in1=dn, op=ALU.mult), (B_sm2,), (B_sm2,))
        V(lambda e: e.tensor_tensor(out=cfi, in0=ni, in1=lre, op=ALU.mult), (B_sm2, B_vtb), (B_sm2,))
        V(lambda e: e.tensor_tensor(out=t5, in0=nr, in1=lim, op=ALU.mult), (B_sm2, B_vtb), (B_sm2,))
        V(lambda e: e.tensor_tensor(out=cfi, in0=cfi, in1=t5, op=ALU.subtract), (B_sm2,), (B_sm2,))
        V(lambda e: e.tensor_tensor(out=cfi, in0=cfi, in1=dn, op=ALU.mult), (B_sm2,), (B_sm2,))
        iot = mixf[:, 0:128]; B_iot = B_mixed
        G(lambda e: e.iota(iot[:, :], pattern=[[1, 128]], base=0, channel_multiplier=0,
                           allow_small_or_imprecise_dtypes=True), (), (B_iot,))
        ang = st1[:, 0:128]; tmpw = st2[:, 0:128]; tmpw2 = st2[:, 128:256]
        for q in range(16):
            V((lambda q: lambda e: e.tensor_scalar(out=ang, in0=iot[:, :], scalar1=th[:, q:q + 1], scalar2=None, op0=ALU.mult))(q),
              (B_iot, B_sm), (B_st[0],))
            sincos(SN[:, q, :], CT[:, q, :], ang, (tmpw, tmpw2), (B_st[0],), (B_st[1], B_tab))
        V(lambda e: e.tensor_scalar(out=tmpa, in0=th, scalar1=128.0, scalar2=None, op0=ALU.mult), (B_sm,), (B_sm,))
        sincos(s128[:, :], c128[:, :], tmpa, (t5, nr), (B_sm,), (B_sm2, B_par))
        V(lambda e: e.tensor_copy(out=par2[:, 0:32], in_=sm2[:, 0:32]), (B_sm2,), (B_par2,))
        A(lambda e: e.activation(out=t5, in_=lrs, func=AF.Exp, scale=128.0), (B_sm,), (B_sm2,))
        V(lambda e: e.tensor_tensor(out=par2[:, 32:48], in0=c128[:, :], in1=t5, op=ALU.mult), (B_par, B_sm2), (B_par2,))
        V(lambda e: e.tensor_tensor(out=par2[:, 48:64], in0=s128[:, :], in1=t5, op=ALU.mult), (B_par, B_sm2), (B_par2,))
        iotr = mixf[:, 384:512]
        G(lambda e: e.iota(iotr, pattern=[[-1, 128]], base=127, channel_multiplier=0, allow_small_or_imprecise_dtypes=True), (), (B_iot,))
        for q in range(16):
            V((lambda q: lambda e: e.tensor_scalar(out=ang, in0=iotr, scalar1=th[:, q:q + 1], scalar2=None, op0=ALU.mult))(q),
              (B_iot, B_sm), (B_st[0],))
            sincos(st3[:, 0:128], st4[:, 0:128], ang, (tmpw, tmpw2), (B_st[0],), (B_st[1], B_st[2], B_st[3]))
            A((lambda q: lambda e: e.activation(out=st1[:, 128:256], in_=iotr, func=AF.Exp, scale=lrs[:, q:q + 1]))(q), (B_iot, B_sm), (B_st[0],))
            V(lambda e: e.tensor_tensor(out=st4[:, 0:128], in0=st4[:, 0:128], in1=st1[:, 128:256], op=ALU.mult), (B_st[0], B_st[3]), (B_st[3],))
            V(lambda e: e.tensor_tensor(out=st3[:, 0:128], in0=st3[:, 0:128], in1=st1[:, 128:256], op=ALU.mult), (B_st[0], B_st[2]), (B_st[2],))
            for part, srct, Bs in ((0, st4, B_st[3]), (1, st3, B_st[2])):
                bkq, Bbq = nb()
                T((lambda bkq, srct: lambda e: e.transpose(bkq[:, 0:128], srct[:, 0:128], ident_f[:, :]))(bkq, srct), (Bs, B_idf), (Bbq,))
                A((lambda bkq, q, part: lambda e: e.copy(out=decT[:, q, part, :], in_=bkq[:, 0:128]))(bkq, q, part), (Bbq,), (B_decT,))

        xnf = xn[:, :].bitcast(F32)
        braw = xnf[:, 0:512].rearrange("p (i q c) -> p i q c", i=2, q=16); B_braw = B_xn
        for i, src in enumerate((bre_d, bim_d)):
            srcv = src.rearrange("(q l) p c -> (l p) q c", l=2)
            DMA((lambda i, srcv: lambda e: e.dma_start(out=braw[:, i, :, :], in_=srcv))(i, srcv), (), (B_braw,))
        cfr_b = cfr.unsqueeze(2).to_broadcast([128, 16, 16]); cfi_b = cfi.unsqueeze(2).to_broadcast([128, 16, 16])
        tb1 = st3[:, 0:256].rearrange("p (q c) -> p q c", q=16); tb2 = st4[:, 0:256].rearrange("p (q c) -> p q c", q=16)
        V(lambda e: e.tensor_tensor(out=tb1, in0=braw[:, 0, :, :], in1=cfr_b, op=ALU.mult), (B_braw, B_sm2), (B_st[2],))
        V(lambda e: e.tensor_tensor(out=tb2, in0=braw[:, 1, :, :], in1=cfi_b, op=ALU.mult), (B_braw, B_sm2), (B_st[3],))
        V(lambda e: e.tensor_tensor(out=bbar[:, 0, :, :], in0=tb1, in1=tb2, op=ALU.subtract), (B_st[2], B_st[3]), (B_bbar,))
        V(lambda e: e.tensor_tensor(out=tb1, in0=braw[:, 1, :, :], in1=cfr_b, op=ALU.mult), (B_braw, B_sm2), (B_st[2],))
        V(lambda e: e.tensor_tensor(out=tb2, in0=braw[:, 0, :, :], in1=cfi_b, op=ALU.mult), (B_braw, B_sm2), (B_st[3],))
        V(lambda e: e.tensor_tensor(out=bbar[:, 1, :, :], in0=tb1, in1=tb2, op=ALU.add), (B_st[2], B_st[3]), (B_bbar,))
        xq = mixf[:, 128:256]; B_xq = B_mixed
        for i in range(2):
            for q in range(16):
                r0 = ((2 * q) % 8) * 16
                V(lambda e: e.memset(xq[:, :], 0.0), (), (B_xq,))
                V((lambda i, q, r0: lambda e: e.tensor_copy(out=xq[0:64, r0:r0 + 16], in_=bbar[0:64, i, q, :]))(i, q, r0), (B_bbar,), (B_xq,))
                V((lambda i, q, r0: lambda e: e.tensor_copy(out=xq[64:128, r0 + 16:r0 + 32], in_=bbar[64:128, i, q, :]))(i, q, r0), (B_bbar,), (B_xq,))
                bkq, Bbq = nb()
                T((lambda bkq: lambda e: e.transpose(bkq[:, 0:128], xq[:, :], ident_f[:, :]))(bkq), (B_xq, B_idf), (Bbq,))
                A((lambda i, q, bkq: lambda e: e.copy(out=BL[:, 16 * i + q, :], in_=bkq[:, 0:128]))(i, q, bkq), (Bbq,), (B_BL,))
        zc = hT[0:32, :, :].bitcast(F32).rearrange("p k n -> p (k n)").rearrange("p (i q n) -> p i q n", i=2, q=16); B_zc = Buf("zc")
        V(lambda e: e.memset(zc[:, :, :, :], 0.0), (), (B_zc,))
        for i, src in enumerate((cre_d, cim_d)):
            for l in range(2):
                srcv = src.rearrange("(q l) c p -> l c q p", l=2)[l]
                DMA((lambda i, l, srcv: lambda e: e.dma_start(out=zc[16 * l:16 * l + 16, i, :, 64 * l:64 * l + 64], in_=srcv))(i, l, srcv),
                    (B_zc,), (B_zc,))
        for i in range(2):
            for q in range(16):
                bkq, Bbq = nb()
                T((lambda i, q, bkq: lambda e: e.transpose(bkq[:, 0:32], zc[:, i, q, :], ident_f[0:32, 0:32]))(i, q, bkq), (B_zc, B_idf), (Bbq,))
                if i == 0:
                    A((lambda q, bkq: lambda e: e.copy(out=Cm[:, q, :], in_=bkq[:, 0:32]))(q, bkq), (Bbq,), (B_Cm,))
                    A((lambda q, bkq: lambda e: e.mul(out=Cm[:, 32 + q, :], in_=bkq[:, 0:32], mul=-1.0))(q, bkq), (Bbq,), (B_Cm,))
                else:
                    A((lambda q, bkq: lambda e: e.mul(out=Cm[:, 16 + q, :], in_=bkq[:, 0:32], mul=-1.0))(q, bkq), (Bbq,), (B_Cm,))
        for c in range(4):
            V((lambda c: lambda e: e.tensor_scalar(out=Dd[:, c, :], in0=ident_f[:, :], scalar1=vt[:, 80 + c:81 + c], scalar2=None,
                                                   op0=ALU.mult))(c), (B_idf, B_vt), (B_Dd,))

        for b in B_hT:
            b.w = dict(B_zc.w); b.r = dict(B_zc.r)

        def rstd_from(ss_ap, out_ap, n, rb, wb):
            A(lambda e: e.activation(out=out_ap, in_=ss_ap, func=AF.Sqrt, bias=epsc[:, 0:1], scale=1.0 / n), tuple(rb) + (B_epsc,), wb)
            V(lambda e: e.reciprocal(out=out_ap, in_=out_ap), wb, wb)

        def load_norm_transpose(row0, slot):
            DMA(lambda e: e.dma_start(out=xt[:, :], in_=x_d[row0:row0 + 128, :]), (), (B_xt,))
            A(lambda e: e.activation(out=junk[:, :], in_=xt[:, :], func=AF.Square, accum_out=stat[:, 0:1]),
              (B_xt,), (B_junk, B_stat[0]))
            rstd_from(stat[:, 0:1], stat[:, 1:2], float(D), (B_stat[0],), (B_stat[1],))
            A(lambda e: e.activation(out=xn[:, :], in_=xt[:, :], func=AF.Identity, scale=stat[:, 1:2]), (B_xt, B_stat[1]), (B_xn,))
            for half in range(2):
                bk_, Bb_ = nb()
                bkv = bk_[:, :].bitcast(BF16)
                for j in range(8):
                    k = half * 8 + j
                    T((lambda bkv, j, k: lambda e: e.transpose(bkv[:, j * 128:(j + 1) * 128], xn[:, k * 128:(k + 1) * 128], ident_b[:, :]))(bkv, j, k),
                      (B_xn, B_idb), (Bb_,))
                for j in range(8):
                    k = half * 8 + j
                    if j % 2 == 0:
                        A((lambda bkv, j, k: lambda e: e.activation(out=hT[:, k, slot * 128:(slot + 1) * 128], in_=bkv[:, j * 128:(j + 1) * 128],
                                                                    func=AF.Identity, scale=gsfm[:, k:k + 1], bias=modfm[:, k:k + 1]))(bkv, j, k),
                          (Bb_, B_gsfm, B_modfm), (B_hT[slot],))
                    else:
                        V((lambda bkv, j, k: lambda e: e.tensor_scalar(out=hT[:, k, slot * 128:(slot + 1) * 128], in0=bkv[:, j * 128:(j + 1) * 128],
                                                                       scalar1=gsfm[:, k:k + 1], scalar2=modfm[:, k:k + 1],
                                                                       op0=ALU.mult, op1=ALU.add))(bkv, j, k),
                          (Bb_, B_gsfm, B_modfm), (B_hT[slot],))

        def load_w(src_ap, ncols, srcbuf):
            s = wsl_rr[0]
            wsl_rr[0] = 1 - s
            DMA(lambda e: e.dma_start(out=wsl[s][:, :, 0:ncols], in_=src_ap), (srcbuf,), (B_wsl[s],))
            return wsl[s], B_wsl[s]

        evac_rr = [0]

        def evac_copy(out_ap, in_ap, rb, wb):
            if evac_rr[0] % 2 == 0:
                A(lambda e: e.copy(out=out_ap, in_=in_ap), rb, wb)
            else:
                V(lambda e: e.tensor_copy(out=out_ap, in_=in_ap), rb, wb)
            evac_rr[0] += 1

        wfm_v = wfm.rearrange("(k p) n -> p k n", p=128)
        wtm_v = wtm.rearrange("(k p) n -> p k n", p=128)
        wo_v = wo.rearrange("(k p) n -> p k n", p=128)

        def fm_chunk(w, Bw, cc, col0, ntok, tok0, dst_ap, dst_buf, hbufs):
            bk_, Bb_ = nb()
            ch = col0 // 128
            for k in range(16):
                T((lambda k: lambda e: e.matmul(bk_[:, 0:ntok], lhsT=w[:, k, cc * 128:(cc + 1) * 128], rhs=hT[:, k, tok0:tok0 + ntok],
                                                start=(k == 0), stop=(k == 15)))(k), (Bw,) + tuple(hbufs), (Bb_,))
            bcol = vtb[:, 48 + ch:49 + ch]
            if evac_rr[0] % 2 == 0:
                A(lambda e: e.activation(out=dst_ap, in_=bk_[:, 0:ntok], func=AF.Identity, bias=bcol, scale=1.0), (Bb_, B_vtb), (dst_buf,))
            else:
                V(lambda e: e.tensor_scalar(out=dst_ap, in0=bk_[:, 0:ntok], scalar1=bcol, scalar2=None, op0=ALU.add), (Bb_, B_vtb), (dst_buf,))
            evac_rr[0] += 1

        def tm_unit(w, Bw, ncols, col0, slot, hbuf):
            bk_, Bb_ = nb()
            for k in range(16):
                T((lambda k: lambda e: e.matmul(bk_[:, 0:ncols], lhsT=hT[:, k, slot * 128:(slot + 1) * 128], rhs=w[:, k, 0:ncols],
                                                start=(k == 0), stop=False))(k), (Bw, hbuf), (Bb_,))
            T(lambda e: e.matmul(bk_[:, 0:ncols], lhsT=ones_b[0:1, 0:128], rhs=brow_tm[0:1, col0:col0 + ncols], start=False, stop=True),
              (B_btm, B_ones), (Bb_,))
            return bk_, Bb_

        if carry:
            npre = NPRE_ST if nst == NST else 2
            win_v = win_d.rearrange("(k p) n -> p k n", p=128)
            wpu = ab[:, 0:8192].rearrange("p (k n) -> p k n", k=16)
            Bwpu = Buf("wpu")
            for cb in all_cells:
                Bwpu.w.update(cb.w); Bwpu.r.update(cb.r)
            hTf = hT[:, :, :].bitcast(F32).rearrange("p k n -> p (k n)").rearrange("p (k n) -> p k n", k=8)
            shrep = xt[:, :].rearrange("p (k m) -> p k m", k=16)
            V(lambda e: e.tensor_copy(out=shrep, in_=modfm[:, 0:16].unsqueeze(2).to_broadcast([128, 16, 128])), (B_modfm,), (B_xt,))
            DMA(lambda e: e.dma_start(out=u0B[:, :], in_=bin_d[0:1, 3456:3968].partition_broadcast(128)), (), (B_u0B,))
            ub, Bub = nb()
            for pc in range(2):
                DMA(lambda e: e.dma_start(out=hTf, in_=win_v[:, 8 * pc:8 * pc + 8, 3456:3968]), (), tuple(B_hT))
                for kk in range(8):
                    k = 8 * pc + kk
                    if k % 2 == 0:
                        V(lambda e: e.tensor_scalar(out=wpu[:, k, :], in0=hTf[:, kk, :], scalar1=gsfm[:, k:k + 1], scalar2=None, op0=ALU.mult),
                          tuple(B_hT) + (B_gsfm,), (Bwpu,))
                    else:
                        A(lambda e: e.activation(out=wpu[:, k, :], in_=hTf[:, kk, :], func=AF.Identity, scale=gsfm[:, k:k + 1]),
                          tuple(B_hT) + (B_gsfm,), (Bwpu,))
                    T(lambda e: e.matmul(ub[:, :], lhsT=shrep[:, k, :], rhs=hTf[:, kk, :], start=(k == 0), stop=(k == 15)),
                      (B_xt,) + tuple(B_hT), (Bub,))
            V(lambda e: e.tensor_tensor(out=u0B[:, :], in0=ub[:, :], in1=u0B[:, :], op=ALU.add), (Bub, B_u0B), (B_u0B,))
            Hr, Hi = hloc[:, 0, :], hloc[:, 1, :]
            c1, c2, c3, c4 = hloc[:, 2, :], hloc[:, 3, :], hloc[:, 4, :], hloc[:, 5, :]
            a128r, a128i = par2[:, 32:48], par2[:, 48:64]
            V(lambda e: e.memset(hloc[:, :, :], 0.0), (), (B_hloc,))
            B_SK = tuple(B_attn)
            SK4r = attn_f[:, 0:64].rearrange("p (q k) -> p q k", q=16)
            SK4i = attn_f[:, 64:128].rearrange("p (q k) -> p q k", q=16)
            xtA = [xt[:, :], arena[:, 4096:6144]]
            B_xtA = [B_xt, Buf("xa1")]
            xnA = [xn[:, :], ab[:, 12288:14336], ab[:, 14336:16384]]
            B_xnA = [B_xn, Buf("xna1"), Buf("xna2")]
            for bb in B_xtA[1:] + B_xnA[1:]:
                for cb in all_cells:
                    bb.w.update(cb.w); bb.r.update(cb.r)
            def stA(n):
                row0 = n * 128
                sl, t = n % 3, n % ST
                xts, Bxts, xns, Bxns = xtA[n % 2], B_xtA[n % 2], xnA[sl], B_xnA[sl]
                DMA(lambda e: e.dma_start(out=xts, in_=xprev_d[row0:row0 + 128, :]), (), (Bxts,))
                A(lambda e: e.activation(out=xns, in_=xts, func=AF.Square, accum_out=ssA[:, sl:sl + 1]), (Bxts,), (Bxns, B_ssA[sl]))
                rstd_from(ssA[:, sl:sl + 1], rsA[:, t:t + 1], float(D), (B_ssA[sl],), (B_rsA[t],))
                A(lambda e: e.activation(out=xns, in_=xts, func=AF.Identity, scale=1.0), (Bxts,), (Bxns,))

            def stB(n):
                sl, t = n % 3, n % ST
                xns, Bxns = xnA[sl], B_xnA[sl]
                for half in range(2):
                    bk_, Bb_ = nb()
                    bkv = bk_[:, :].bitcast(BF16)
                    for j in range(8):
                        k = half * 8 + j
                        T(lambda e: e.transpose(bkv[:, j * 128:(j + 1) * 128], xns[:, k * 128:(k + 1) * 128], ident_b[:, :]), (Bxns, B_idb), (Bb_,))
                    V(lambda e: e.tensor_copy(out=hT[:, half * 8:(half + 1) * 8, t * 128:(t + 1) * 128],
                                              in_=bkv.rearrange("p (j m) -> p j m", j=8)), (Bb_,), (B_hT[t],))

            def stC(n):
                t = n % ST
                bk_, Bb_ = nb()
                for k in range(16):
                    T(lambda e: e.matmul(bk_[:, :], lhsT=hT[:, k, t * 128:(t + 1) * 128], rhs=wpu[:, k, :], start=(k == 0), stop=(k == 15)),
                      (Bwpu, B_hT[t]), (Bb_,))
                V(lambda e: e.scalar_tensor_tensor(out=uTM[:, t, :], in0=bk_[:, :], scalar=rsA[:, t:t + 1], in1=u0B[:, :],
                                                   op0=ALU.mult, op1=ALU.add), (Bb_, B_rsA[t], B_u0B), (B_mixed,))

            ntile = npre * ST
            for it in range(ntile + 2):
                if it < ntile:
                    stA(it)
                if 0 <= it - 1 < ntile:
                    stB(it - 1)
                if not (0 <= it - 2 < ntile):
                    continue
                stC(it - 2)
                if (it - 2) % ST != ST - 1:
                    continue
                s = (it - 2) // ST
                for hb in range(2):
                    vb = [nb(), nb()]
                    for qq in range(8):
                        q = hb * 8 + qq
                        for gl in range(2):
                            g = 2 * q + gl
                            for part in range(2):
                                T((lambda vb, qq, q, gl, g, part: lambda e: e.matmul(
                                    vb[part][0][64 * gl:64 * gl + 64, qq * 64:(qq + 1) * 64],
                                    lhsT=decT[:, q, part, 64 * gl:64 * gl + 64], rhs=uTM[:, :, g * 16:(g + 1) * 16],
                                    start=True, stop=True))(vb, qq, q, gl, g, part), (B_decT, B_mixed), (vb[part][1],))
                    vr4 = vb[0][0][:, :].rearrange("p (q k c) -> p q k c", q=8, k=4)
                    vi4 = vb[1][0][:, :].rearrange("p (q k c) -> p q k c", q=8, k=4)
                    bbr = bbar[:, 0, hb * 8:hb * 8 + 8, :].unsqueeze(2).to_broadcast([128, 8, 4, 16])
                    bbi = bbar[:, 1, hb * 8:hb * 8 + 8, :].unsqueeze(2).to_broadcast([128, 8, 4, 16])
                    v4 = lambda tl: tl[:, :].rearrange("p (q k c) -> p q k c", q=8, k=4)
                    V((lambda vr4, bbr: lambda e: e.tensor_tensor(out=v4(st1), in0=vr4, in1=bbr, op=ALU.mult))(vr4, bbr), (vb[0][1], B_bbar), (B_st[0],))
                    V((lambda vi4, bbi: lambda e: e.tensor_tensor(out=v4(st2), in0=vi4, in1=bbi, op=ALU.mult))(vi4, bbi), (vb[1][1], B_bbar), (B_st[1],))
                    G(lambda e: e.tensor_tensor(out=st1[:, :], in0=st1[:, :], in1=st2[:, :], op=ALU.subtract), (B_st[0], B_st[1]), (B_st[0],))
                    V((lambda hb: lambda e: e.tensor_reduce(out=SK4r[:, hb * 8:hb * 8 + 8, :], in_=v4(st1), axis=AX.X, op=ALU.add))(hb), (B_st[0],), B_SK)
                    V((lambda vi4, bbr: lambda e: e.tensor_tensor(out=v4(st3), in0=vi4, in1=bbr, op=ALU.mult))(vi4, bbr), (vb[1][1], B_bbar), (B_st[2],))
                    V((lambda vr4, bbi: lambda e: e.tensor_tensor(out=v4(st4), in0=vr4, in1=bbi, op=ALU.mult))(vr4, bbi), (vb[0][1], B_bbar), (B_st[3],))
                    G(lambda e: e.tensor_tensor(out=st3[:, :], in0=st3[:, :], in1=st4[:, :], op=ALU.add), (B_st[2], B_st[3]), (B_st[2],))
                    V((lambda hb: lambda e: e.tensor_reduce(out=SK4i[:, hb * 8:hb * 8 + 8, :], in_=v4(st3), axis=AX.X, op=ALU.add))(hb), (B_st[2],), B_SK)
                for kk in range(ST):
                    ch = s * ST + kk
                    V(lambda e: e.tensor_tensor(out=c1, in0=Hr, in1=a128r, op=ALU.mult), (B_hloc, B_par2), (B_hloc,))
                    V(lambda e: e.tensor_tensor(out=c2, in0=Hi, in1=a128i, op=ALU.mult), (B_hloc, B_par2), (B_hloc,))
                    V(lambda e: e.tensor_tensor(out=c3, in0=Hi, in1=a128r, op=ALU.mult), (B_hloc, B_par2), (B_hloc,))
                    V(lambda e: e.tensor_tensor(out=c4, in0=Hr, in1=a128i, op=ALU.mult), (B_hloc, B_par2), (B_hloc,))
                    V(lambda e: e.tensor_tensor(out=c1, in0=c1, in1=c2, op=ALU.subtract), (B_hloc,), (B_hloc,))
                    V(lambda e: e.tensor_tensor(out=c3, in0=c3, in1=c4, op=ALU.add), (B_hloc,), (B_hloc,))
                    V((lambda kk, ch: lambda e: e.scalar_tensor_tensor(out=Hr, in0=SK4r[:, :, kk], scalar=cmask[:, ch:ch + 1], in1=c1,
                                                                       op0=ALU.mult, op1=ALU.add))(kk, ch), (B_hloc, B_cmask) + B_SK, (B_hloc,))
                    V((lambda kk, ch: lambda e: e.scalar_tensor_tensor(out=Hi, in0=SK4i[:, :, kk], scalar=cmask[:, ch:ch + 1], in1=c3,
                                                                       op0=ALU.mult, op1=ALU.add))(kk, ch), (B_hloc, B_cmask) + B_SK, (B_hloc,))
            for cb in all_cells:
                for bb in B_xtA[1:] + B_xnA[1:] + [Bwpu]:
                    cb.w.update(bb.w); cb.r.update(bb.r)
            for bb in (B_gr, B_gi, B_hr, B_hi, B_g1):
                bb.w = dict(B_decT.w); bb.r = dict(B_decT.r)
            w4 = sm[:, 32:64]
            sth_, cth_ = par2[:, 0:16], par2[:, 16:32]
            V(lambda e: e.tensor_tensor(out=w4[:, 0:16], in0=Hr, in1=cth_, op=ALU.mult), (B_hloc, B_par2), (B_sm,))
            V(lambda e: e.tensor_tensor(out=w4[:, 16:32], in0=Hi, in1=sth_, op=ALU.mult), (B_hloc, B_par2), (B_sm,))
            V(lambda e: e.tensor_tensor(out=gin_r[:, :], in0=w4[:, 0:16], in1=w4[:, 16:32], op=ALU.subtract), (B_sm,), tuple(B_gin))
            V(lambda e: e.tensor_tensor(out=w4[:, 0:16], in0=Hi, in1=cth_, op=ALU.mult), (B_hloc, B_par2), (B_sm,))
            V(lambda e: e.tensor_tensor(out=w4[:, 16:32], in0=Hr, in1=sth_, op=ALU.mult), (B_hloc, B_par2), (B_sm,))
            V(lambda e: e.tensor_tensor(out=gin_i[:, :], in0=w4[:, 0:16], in1=w4[:, 16:32], op=ALU.add), (B_sm,), tuple(B_gin))
        else:
            V(lambda e: e.memset(gin_r[:, :], 0.0), (), tuple(B_gin))
            V(lambda e: e.memset(gin_i[:, :], 0.0), (), tuple(B_gin))

        for bb in (B_mixA, B_mixS):
            bb.w = dict(B_mixed.w); bb.r = dict(B_mixed.r)

        load_norm_transpose(0, 0)
        for g in range(3):
            w, Bw = load_w(wfm_v[:, :, 1536 + 128 * g:1664 + 128 * g], 128, B_wfm)
            fm_chunk(w, Bw, 0, 1536 + 128 * g, 128, 0, KT[:, g, 0:128], B_KT[g], (B_hT[0],))
        w, Bw = load_w(wtm_v[:, :, 0:192], 192, B_wtm)
        bk_, Bb_ = tm_unit(w, Bw, 192, 0, 0, B_hT[0])
        V((lambda bk_: lambda e: e.tensor_copy(out=Vt[:, 0, :, 0:64], in_=bk_[:, 0:192].rearrange("p (g d) -> p g d", g=3)))(bk_),
          (Bb_,), (B_Vt[0],))

        stsets = [[st1, st2, st3, st4], [st5, st6, st7, st8]]
        Bstsets = [B_st, B_stb]
        gsets = [[gr_, gi_, hr_, hi_, p24, ctmp[:, 0:4, :]],
                 [spool2[:, 0:512].rearrange("p (a b) -> p a b", a=4), spool2[:, 512:1024].rearrange("p (a b) -> p a b", a=4),
                  spool2[:, 1024:1280].bitcast(BF16).rearrange("p (a b) -> p a b", a=4),
                  spool2[:, 1280:1536].bitcast(BF16).rearrange("p (a b) -> p a b", a=4), p24b, ctmp[:, 4:8, :]]]
        Bgsets = [[B_gr, B_gi, B_hr, B_hi, B_p24, B_ctmp], [Buf("gr2"), Buf("gi2"), Buf("hr2"), Buf("hi2"), [Buf("p24c"), Buf("p24d")], Buf("ctmp2")]]

        for s in range(nst):
            for t in range(ST):
                load_norm_transpose(128 + (s * ST + t) * 128, t)
            fm_units = [(c0, min(256, 2432 - c0)) for c0 in range(0, 2432, 256)]
            for (c0, ncols) in fm_units:
                w, Bw = load_w(wfm_v[:, :, c0:c0 + ncols], ncols, B_wfm)
                for cc in range(ncols // 128):
                    ch = c0 // 128 + cc
                    if ch < 12:
                        dst, db = QT[:, ch, :], B_QT[ch]
                    elif ch < 15:
                        dst, db = KT[:, ch - 12, 128:640], B_KT[ch - 12]
                    else:
                        dst, db = UT[:, ch - 15, :], B_UT[ch - 15]
                    fm_chunk(w, Bw, cc, c0 + cc * 128, 512, 0, dst, db, B_hT)
            tm_units = [(0, 192)] + [(192 + 256 * i, 256) for i in range(8)]
            for ui, (c0, ncols) in enumerate(tm_units):
                w, Bw = load_w(wtm_v[:, :, c0:c0 + ncols], ncols, B_wtm)
                for t in range(ST):
                    bk_, Bb_ = tm_unit(w, Bw, ncols, c0, t, B_hT[t])
                    if ui == 0:
                        V((lambda bk_, t: lambda e: e.tensor_copy(out=Vt[:, t + 1, :, 0:64],
                                                                  in_=bk_[:, 0:192].rearrange("p (g d) -> p g d", g=3)))(bk_, t),
                          (Bb_,), (B_Vt[t + 1],))
                    elif c0 < 1728:
                        zc0 = c0 - 192
                        A((lambda bk_, t, zc0: lambda e: e.activation(out=za[:, t, zc0:zc0 + 256], in_=bk_[:, 0:256], func=AF.Silu))(bk_, t, zc0),
                          (Bb_,), (B_za[t],))
                    else:
                        zc0 = c0 - 1728
                        A((lambda bk_, t, zc0: lambda e: e.activation(out=zs[:, t, zc0:zc0 + 256], in_=bk_[:, 0:256], func=AF.Silu))(bk_, t, zc0),
                          (Bb_,), (B_zs[t],))

            def attn_gen(t, gt):
                for g in range(3):
                    pts = {}
                    for ci, (kc0, Mx, BMx) in enumerate(((t * 128, Mp0 if gt == 0 else Mp, B_Mp0 if gt == 0 else B_Mp),
                                                         ((t + 1) * 128, Mc, B_Mc))):
                        for base in (0, 64):
                            idx = (g % 2) * 4 + ci * 2 + base // 64
                            bk_, Bb_ = nb()
                            T(lambda e: e.matmul(bk_[:, 0:512], lhsT=KT[base:base + 64, g, kc0:kc0 + 128],
                                                 rhs=QT[base:base + 64, 4 * g:4 * g + 4, t * 128:(t + 1) * 128], start=True, stop=False),
                              (B_KT[g],) + tuple(B_QT[4 * g:4 * g + 4]), (Bb_,))
                            T(lambda e: e.matmul(bk_[:, 0:512], lhsT=ident_b[:, :], rhs=Mx[:, :], start=False, stop=True), (B_idb, BMx), (Bb_,))
                            A(lambda e: e.activation(out=Eb[idx][:, :], in_=bk_[:, 0:512], func=AF.Exp, scale=0.125), (Bb_,), (B_Eb[idx],))
                            pts[(ci, base)] = idx
                        yield
                    bo = [nb(), nb()]
                    for hl in range(8):
                        j, base = hl // 2, 64 * (hl % 2)
                        ob, Bob = bo[hl // 4]
                        oc = (hl % 4) * 65
                        for ci in range(2):
                            pidx = pts[(ci, base)]
                            vslot = t + ci
                            T(lambda e: e.matmul(ob[:, oc:oc + 65], lhsT=Eb[pidx][:, j * 128:(j + 1) * 128], rhs=Vt[:, vslot, g, :],
                                                 start=(ci == 0), stop=(ci == 1)), (B_Eb[pidx], B_Vt[vslot]), (Bob,))
                    for half in range(2):
                        ob, Bob = bo[half]
                        ov = ob[:, 0:260].rearrange("p (h d) -> p h d", h=4)
                        hs = slice(half * 4, half * 4 + 4)
                        V(lambda e: e.tensor_tensor(out=den[:, g, hs], in0=ov[:, :, 64], in1=esink[:, 8 * g + 4 * half:8 * g + 4 * half + 4], op=ALU.add),
                          (Bob, B_esink), (B_den[g],))
                        V(lambda e: e.reciprocal(out=rden[:, g, hs], in_=den[:, g, hs]), (B_den[g],), (B_rden[g],))
                        a0 = g * 512 + half * 256
                        V(lambda e: e.tensor_tensor(out=attn_f[:, a0:a0 + 256].rearrange("p (h d) -> p h d", h=4), in0=ov[:, :, 0:64],
                                                    in1=rden[:, g, hs].unsqueeze(2).to_broadcast([128, 4, 64]), op=ALU.mult),
                          (Bob, B_rden[g]), (B_attn[g],))
                    yield
                A(lambda e: e.activation(out=junk[:, 0:1536], in_=attn_f[:, :], func=AF.Square, accum_out=stat[:, 2:3]),
                  tuple(B_attn), (B_junk, B_stat[2]))
                rstd_from(stat[:, 2:3], stat[:, 3:4], 1536.0, (B_stat[2],), (B_stat[3],))
                V(lambda e: e.scalar_tensor_tensor(out=mixed[:, 0:1536], in0=attn_f[:, :], scalar=stat[:, 3:4], in1=za[:, t, :],
                                                   op0=ALU.mult, op1=ALU.mult), tuple(B_attn) + (B_stat[3], B_za[t]), (B_mixA,))
                yield

            def ssm_gen(t, gt):
                ybk, Bybk = banks[7], B_bank[7]
                for Q in range(4):
                    par = Q % 2
                    st1, st2, st3, st4 = stsets[par]
                    B_st = Bstsets[par]
                    wr_, wi_, B_wr, B_wi = st1, st3, B_st[0], B_st[2]
                    gr_, gi_, hr_, hi_, p24, ctmp = gsets[par]
                    B_gr, B_gi, B_hr, B_hi, B_p24, B_ctmp = Bgsets[par]
                    pr, Bpr = nb()
                    pi_, Bpi = nb()
                    for i in range(4):
                        T(lambda e: e.matmul(pr[:, i * 128:(i + 1) * 128], lhsT=BL[:, 4 * Q + i, :], rhs=UT[:, Q, t * 128:(t + 1) * 128],
                                             start=True, stop=True), (B_BL, B_UT[Q]), (Bpr,))
                    for i in range(4):
                        T(lambda e: e.matmul(pi_[:, i * 128:(i + 1) * 128], lhsT=BL[:, 16 + 4 * Q + i, :], rhs=UT[:, Q, t * 128:(t + 1) * 128],
                                             start=True, stop=True), (B_BL, B_UT[Q]), (Bpi,))
                    ctq = CT[:, 4 * Q:4 * Q + 4, :].rearrange("p a b -> p (a b)")
                    snq = SN[:, 4 * Q:4 * Q + 4, :].rearrange("p a b -> p (a b)")
                    V(lambda e: e.tensor_tensor(out=st1[:, :], in0=pr[:, :], in1=ctq, op=ALU.mult), (Bpr, B_tab), (B_st[0],))
                    V(lambda e: e.tensor_tensor(out=st2[:, :], in0=pi_[:, :], in1=snq, op=ALU.mult), (Bpi, B_tab), (B_st[1],))
                    G(lambda e: e.tensor_tensor(out=wr_[:, :], in0=st1[:, :], in1=st2[:, :], op=ALU.add), (B_st[0], B_st[1]), (B_wr,))
                    V(lambda e: e.tensor_tensor(out=st3[:, :], in0=pi_[:, :], in1=ctq, op=ALU.mult), (Bpi, B_tab), (B_st[2],))
                    V(lambda e: e.tensor_tensor(out=st4[:, :], in0=pr[:, :], in1=snq, op=ALU.mult), (Bpr, B_tab), (B_st[3],))
                    G(lambda e: e.tensor_tensor(out=wi_[:, :], in0=st3[:, :], in1=st4[:, :], op=ALU.subtract), (B_st[2], B_st[3]), (B_wi,))
                    for i in range(4):
                        q = 4 * Q + i
                        V(lambda e: e.tensor_tensor_scan(out=gr_[:, i, :], data0=rho[:, q:q + 1].to_broadcast([128, 128]),
                                                         data1=wr_[:, i * 128:(i + 1) * 128], initial=gin_r[:, q:q + 1], op0=ALU.mult, op1=ALU.add),
                          (B_par, B_wr, B_gin[Q]), (B_gr,))
                        V(lambda e: e.tensor_tensor_scan(out=gi_[:, i, :], data0=rho[:, q:q + 1].to_broadcast([128, 128]),
                                                         data1=wi_[:, i * 128:(i + 1) * 128], initial=gin_i[:, q:q + 1], op0=ALU.mult, op1=ALU.add),
                          (B_par, B_wi, B_gin[Q]), (B_gi,))
                    qs = slice(4 * Q, 4 * Q + 4)
                    glr = gr_[:, :, 127]; gli = gi_[:, :, 127]
                    G(lambda e: e.tensor_tensor(out=ctmp[:, 0, :], in0=glr, in1=c128[:, qs], op=ALU.mult), (B_gr, B_par), (B_ctmp,))
                    G(lambda e: e.tensor_tensor(out=ctmp[:, 1, :], in0=gli, in1=s128[:, qs], op=ALU.mult), (B_gi, B_par), (B_ctmp,))
                    G(lambda e: e.tensor_tensor(out=ctmp[:, 2, :], in0=gli, in1=c128[:, qs], op=ALU.mult), (B_gi, B_par), (B_ctmp,))
                    G(lambda e: e.tensor_tensor(out=ctmp[:, 3, :], in0=glr, in1=s128[:, qs], op=ALU.mult), (B_gr, B_par), (B_ctmp,))
                    G(lambda e: e.tensor_tensor(out=gin_r[:, qs], in0=ctmp[:, 0, :], in1=ctmp[:, 1, :], op=ALU.subtract), (B_ctmp,), (B_gin[Q],))
                    G(lambda e: e.tensor_tensor(out=gin_i[:, qs], in0=ctmp[:, 2, :], in1=ctmp[:, 3, :], op=ALU.add), (B_ctmp,), (B_gin[Q],))
                    yield
                    grf = gr_.rearrange("p a b -> p (a b)"); gif = gi_.rearrange("p a b -> p (a b)")
                    G(lambda e: e.tensor_tensor(out=hr_.rearrange("p a b -> p (a b)"), in0=grf, in1=ctq, op=ALU.mult), (B_gr, B_tab), (B_hr,))
                    G(lambda e: e.tensor_tensor(out=p24[:, 0, :], in0=gif, in1=snq, op=ALU.mult), (B_gi, B_tab), (B_p24[0],))
                    G(lambda e: e.tensor_tensor(out=hi_.rearrange("p a b -> p (a b)"), in0=gif, in1=ctq, op=ALU.mult), (B_gi, B_tab), (B_hi,))
                    G(lambda e: e.tensor_tensor(out=p24[:, 1, :], in0=grf, in1=snq, op=ALU.mult), (B_gr, B_tab), (B_p24[1],))
                    for i in range(4):
                        q = 4 * Q + i
                        yc = q * 32
                        T(lambda e: e.matmul(ybk[:, yc:yc + 32], lhsT=hr_[:, i, :], rhs=Cm[:, q, :], start=True, stop=False), (B_hr, B_Cm), (Bybk,))
                        T(lambda e: e.matmul(ybk[:, yc:yc + 32], lhsT=p24[:, 0, i * 128:(i + 1) * 128], rhs=Cm[:, 32 + q, :], start=False, stop=False),
                          (B_p24[0], B_Cm), (Bybk,))
                        T(lambda e: e.matmul(ybk[:, yc:yc + 32], lhsT=hi_[:, i, :], rhs=Cm[:, 16 + q, :], start=False, stop=False), (B_hi, B_Cm), (Bybk,))
                        T(lambda e: e.matmul(ybk[:, yc:yc + 32], lhsT=p24[:, 1, i * 128:(i + 1) * 128], rhs=Cm[:, 16 + q, :], start=False, stop=False),
                          (B_p24[1], B_Cm), (Bybk,))
                        T(lambda e: e.matmul(ybk[:, yc:yc + 32], lhsT=UT[:, Q, t * 128:(t + 1) * 128], rhs=Dd[:, Q, i * 32:(i + 1) * 32],
                                             start=False, stop=True), (B_UT[Q], B_Dd), (Bybk,))
                    yield
                A(lambda e: e.activation(out=g1, in_=ybk[:, :], func=AF.Gelu), (Bybk,), (B_g1,))
                V(lambda e: e.tensor_copy(out=g1b[:, :], in_=g1), (B_g1,), (B_g1b,))
                tb_, Btb_ = nb()
                tbv = tb_[:, :].bitcast(BF16)
                for k in range(4):
                    T(lambda e: e.transpose(tbv[:, k * 128:(k + 1) * 128], g1b[:, k * 128:(k + 1) * 128], ident_b[:, :]), (B_g1b, B_idb), (Btb_,))
                V(lambda e: e.tensor_copy(out=g1T[:, :, :].rearrange("p a b -> p (a b)"), in_=tbv[:, 0:512]), (Btb_,), (B_g1T,))
                yield
                gbk, Bgbk = nb()
                for k in range(4):
                    T(lambda e: e.matmul(gbk[:, :], lhsT=g1T[:, k, :], rhs=wglu[:, k, :], start=(k == 0), stop=False), (B_g1T, B_wglu), (Bgbk,))
                T(lambda e: e.matmul(gbk[:, :], lhsT=ones_b[0:1, 0:128], rhs=brow_glu[0:1, :], start=False, stop=True), (B_ones, B_bglu), (Bgbk,))
                A(lambda e: e.activation(out=sg[:, :], in_=gbk[:, :], func=AF.Sigmoid), (Bgbk,), (B_sg,))
                V(lambda e: e.tensor_tensor(out=s2[:, :], in0=g1, in1=sg[:, :], op=ALU.mult), (B_g1, B_sg), (B_s2,))
                A(lambda e: e.activation(out=junk[:, 1536:2048], in_=s2[:, :], func=AF.Square, accum_out=stat[:, 4:5]), (B_s2,), (B_junk, B_stat[4]))
                rstd_from(stat[:, 4:5], stat[:, 5:6], 512.0, (B_stat[4],), (B_stat[5],))
                V(lambda e: e.scalar_tensor_tensor(out=mixed[:, 1536:2048], in0=s2[:, :], scalar=stat[:, 5:6], in1=zs[:, t, :],
                                                   op0=ALU.mult, op1=ALU.mult), (B_s2, B_stat[5], B_zs[t]), (B_mixS,))
                yield

            for t in range(ST):
                gt = s * ST + t
                gens = [attn_gen(t, gt), ssm_gen(t, gt)]
                while gens:
                    for gg in list(gens):
                        try:
                            next(gg)
                        except StopIteration:
                            gens.remove(gg)
                for half in range(2):
                    bk_, Bb_ = nb()
                    bkv = bk_[:, :].bitcast(BF16)
                    for j in range(8):
                        c = half * 8 + j
                        T(lambda e: e.transpose(bkv[:, j * 128:(j + 1) * 128], mixed[:, c * 128:(c + 1) * 128], ident_b[:, :]),
                          (B_mixA, B_mixS, B_idb), (Bb_,))
                    for j in range(8):
                        c = half * 8 + j
                        if j % 2 == 0:
                            A(lambda e: e.activation(out=hT[:, c, t * 128:(t + 1) * 128], in_=bkv[:, j * 128:(j + 1) * 128],
                                                     func=AF.Identity, scale=vt[:, 48 + c:49 + c]), (Bb_, B_vt), (B_hT[t],))
                        else:
                            V(lambda e: e.tensor_scalar(out=hT[:, c, t * 128:(t + 1) * 128], in0=bkv[:, j * 128:(j + 1) * 128],
                                                        scalar1=vt[:, 48 + c:49 + c], scalar2=None, op0=ALU.mult), (Bb_, B_vt), (B_hT[t],))

            if s + 1 < nst:
                for g in range(3):
                    G((lambda g: lambda e: e.tensor_copy(out=KT[:, g, 0:128], in_=KT[:, g, 512:640]))(g), (B_KT[g],), (B_KT[g],))
                G(lambda e: e.tensor_copy(out=Vt[:, 0, :, 0:64], in_=Vt[:, 4, :, 0:64]), (B_Vt[4],), (B_Vt[0],))

            for t in range(ST):
                row0 = 128 + (s * ST + t) * 128
                DMA((lambda t, row0: lambda e: e.dma_start(out=xr(t), in_=x_d[row0:row0 + 128, :]))(t, row0), (), tuple(xr_cells(t)))
            for u in range(8):
                w, Bw = load_w(wo_v[:, :, u * 256:(u + 1) * 256], 256, B_wo)
                for t in range(ST):
                    bk_, Bb_ = nb()
                    for c in range(16):
                        T(lambda e: e.matmul(bk_[:, 0:256], lhsT=hT[:, c, t * 128:(t + 1) * 128], rhs=w[:, c, :],
                                             start=(c == 0), stop=(c == 15)), (Bw, B_hT[t]), (Bb_,))
                    ot = (st2, st6)[t % 2][:, 0:256]
                    Bot = (B_st[1], B_stb[1])[t % 2]
                    V(lambda e: e.tensor_tensor(out=ot, in0=bk_[:, 0:256], in1=gateB[:, u * 256:(u + 1) * 256], op=ALU.mult),
                      (Bb_, B_gateB), (Bot,))
                    G(lambda e: e.tensor_tensor(out=xr(t)[:, u * 256:(u + 1) * 256], in0=xr(t)[:, u * 256:(u + 1) * 256], in1=ot, op=ALU.add),
                      (Bot,) + tuple(xr_cells(t)), tuple(xr_cells(t)))
            for t in range(ST):
                gt = s * ST + t
                cl = tuple(xr_cells(t))
                A((lambda t: lambda e: e.activation(out=junk[:, :], in_=xr(t), func=AF.Square, accum_out=stat[:, 6:7]))(t), cl, (B_junk, B_stat[6]))
                rstd_from(stat[:, 6:7], stat[:, 7:8], float(D), (B_stat[6],), (B_stat[7],))
                V((lambda t: lambda e: e.scalar_tensor_tensor(out=xr(t), in0=xr(t), scalar=stat[:, 7:8], in1=fgB[:, :],
                                                              op0=ALU.mult, op1=ALU.mult))(t), cl + (B_stat[7], B_fgB), cl)
                DMA((lambda t, gt: lambda e: e.dma_start(out=out_d[gt * 128:(gt + 1) * 128, :], in_=xr(t)))(t, gt), cl, (), out=True)

        nsem = P.finalize(40)
        sems = {e: [es.enter_context(nc.semaphore("s_%s_%d" % (e, i))) for i in range(nsem[e])] for e in Plan.ENGS}
        sems["dma"] = [es.enter_context(nc.semaphore("s_dma_%d" % i)) for i in range(40)]
        with nc.Block() as block:
            @block.sync
            def _(e): P.run_stream("sync", e, sems)

            @block.scalar
            def _(e): P.run_stream("scalar", e, sems)

            @block.vector
            def _(e): P.run_stream("vector", e, sems)

            @block.gpsimd
            def _(e): P.run_stream("gpsimd", e, sems)

            @block.tensor
            def _(e): P.run_stream("tensor", e, sems)
    return nc


def make_in_maps(inp):
    f = lambda a: np.ascontiguousarray(np.asarray(a, dtype=np.float32))
    x = f(inp["x"]); c = f(inp["c"])
    b_ada = f(inp["b_ada"])[0]
    shared = {
        "w_ada": f(inp["w_ada"])[0], "b_ada": b_ada.reshape(1, -1), "w_in": f(inp["w_in"])[0],
        "b_in": f(inp["b_in"])[0].reshape(1, -1), "w_out": f(inp["w_out"])[0], "glu_w": f(inp["glu_w"])[0],
        "glu_b": f(inp["glu_b"])[0].reshape(1, -1), "final_gain": f(inp["final_gain"]).reshape(1, -1),
        "sinks": f(inp["attn_sinks"])[0].reshape(1, -1),
        "b_re": f(inp["ssm_b_re"])[0], "b_im": f(inp["ssm_b_im"])[0],
        "c_re": f(inp["ssm_c_re"])[0], "c_im": f(inp["ssm_c_im"])[0],
    }
    vecsB = np.concatenate([f(inp["ssm_lambda_re"])[0].reshape(16, 128), f(inp["ssm_lambda_im"])[0].reshape(16, 128),
                            np.repeat(f(inp["ssm_log_step"])[0], 64).reshape(16, 128),
                            shared["b_in"][0, 0:1536].reshape(12, 128),
                            np.stack([np.tile(shared["b_in"][0, 1536 + 64 * g:1600 + 64 * g], 2) for g in range(3)], 0),
                            shared["b_in"][0, 3456:3968].reshape(4, 128)], 0)
    maps = []
    for i in range(NCORES):
        b, j = i // 4, i % 4
        t0 = j * T_CORE
        xs = np.zeros((128 + T_CORE, D), np.float32)
        xs[128:] = x[b, t0:t0 + T_CORE]
        if j > 0:
            xs[:128] = x[b, t0 - 128:t0]
        vecsA = np.concatenate([b_ada[0:4096].reshape(32, 128), f(inp["norm_gain"])[0].reshape(16, 128),
                                f(inp["attn_out_gain"])[0].reshape(12, 128), f(inp["ssm_out_gain"])[0].reshape(4, 128),
                                c[b].reshape(16, 128), f(inp["ssm_d"])[0].reshape(4, 128)], 0)
        flags = np.zeros((128, 4), np.float32)
        flags[:, 0] = 1.0 if j > 0 else 0.0
        npre = NPRE_ST * ST * 128
        xprev = np.zeros((npre, D), np.float32)
        cmask = np.zeros((128, NPRE_ST * ST), np.float32)
        if j > 0:
            xprev[npre - t0:] = x[b, 0:t0]
            cmask[:, (npre - t0) // 128:] = 1.0
        m = dict(shared)
        m.update({"x": xs, "vecsA": np.ascontiguousarray(vecsA), "vecsB": np.ascontiguousarray(vecsB), "flags": flags, "xprev": xprev, "cmask": cmask})
        maps.append(m)
    return maps


def kernel(**inputs):
    nc = build_program()
    maps = make_in_maps(inputs)
    res = run_bass_kernel_spmd(nc, maps, core_ids=list(range(NCORES)))
    out = np.zeros((2, 4 * T_CORE, D), np.float32)
    for i in range(NCORES):
        b, j = i // 4, i % 4
        out[b, j * T_CORE:(j + 1) * T_CORE] = res.results[i]["out"]
    return out
```

```python
import math
from contextlib import ExitStack
import numpy as np
import concourse.bass as bass
import concourse.mybir as mybir
from concourse.bass_utils import run_bass_kernel_spmd

F32 = mybir.dt.float32
BF16 = mybir.dt.bfloat16
I32 = mybir.dt.int32
AF = mybir.ActivationFunctionType
ALU = mybir.AluOpType
AX = mybir.AxisListType

NCORES = 8
D = 2048
T_CORE = 4096
NT = 32
ST = 4
NST = NT // ST
NPRE_ST = 24
EPS = 1e-5
SEM_CAP = 20000
TWO_PI = 2.0 * math.pi


class Buf:
    __slots__ = ("name", "w", "r")

    def __init__(self, name):
        self.name = name
        self.w = {}
        self.r = {}


class _Op:
    __slots__ = ("eng", "fn", "deps", "dma", "sig", "idx", "signal", "presem")

    def __init__(self, eng, fn, dma):
        self.eng = eng
        self.fn = fn
        self.dma = dma
        self.deps = []
        self.sig = None
        self.signal = False
        self.presem = None


class _Rec:
    def __init__(self):
        self.call = None

    def __getattr__(self, name):
        def f(*a, **kw):
            self.call = (name, a, kw)
            return self
        return f


class Plan:
    ENGS = ("sync", "scalar", "vector", "gpsimd", "tensor")

    def __init__(self):
        self.streams = {e: [] for e in self.ENGS}
        self.all_ops = []
        self.out_dmas = []

    def op(self, eng, fn, reads=(), writes=(), dma=False, out=False):
        rec = _Rec()
        fn(rec)
        o = _Op(eng, rec.call, dma)
        o.idx = len(self.all_ops)
        deps = set()
        for b in reads:
            deps.update(b.w.values())
        for b in writes:
            deps.update(b.w.values())
            deps.update(b.r.values())
        key = ("dma", o.idx) if dma else eng
        fdeps = []
        for d in deps:
            dop = self.all_ops[d]
            if (not dop.dma) and dop.eng == eng and (not dma) and eng == "tensor":
                continue
            fdeps.append(d)
        o.deps = fdeps
        for d in fdeps:
            self.all_ops[d].signal = True
        for b in reads:
            b.r[key] = o.idx
        for b in writes:
            b.w = {key: o.idx}
            b.r = {}
        self.streams[eng].append(o)
        self.all_ops.append(o)
        if out:
            self.out_dmas.append(o.idx)
        return o

    def finalize(self, n_dma_sems):
        if self.out_dmas:
            fin = _Op("sync", None, False)
            fin.idx = len(self.all_ops)
            fin.deps = list(self.out_dmas)
            for d in fin.deps:
                self.all_ops[d].signal = True
            self.streams["sync"].append(fin)
            self.all_ops.append(fin)
        cnt = {e: [0, 0] for e in self.ENGS}
        dma_vals = [0] * n_dma_sems
        dma_used = [False] * n_dma_sems
        rr = 0
        for o in self.all_ops:
            if o.dma:
                s = rr
                rr = (rr + 1) % n_dma_sems
                if dma_used[s]:
                    o.presem = ("dma", s, dma_vals[s])
                dma_vals[s] += 16
                dma_used[s] = True
                o.sig = ("dma", s, dma_vals[s])
            elif o.signal:
                c = cnt[o.eng]
                if c[1] >= SEM_CAP:
                    c[0] += 1
                    c[1] = 0
                c[1] += 1
                o.sig = (o.eng, c[0], c[1])
        return {e: cnt[e][0] + 1 for e in self.ENGS}

    def run_stream(self, eng_name, e, sems):
        seen = {}
        for o in self.streams[eng_name]:
            waits = {}
            if o.presem is not None:
                waits[o.presem[:2]] = o.presem[2]
            for d in o.deps:
                sg = self.all_ops[d].sig
                k = sg[:2]
                if waits.get(k, 0) < sg[2]:
                    waits[k] = sg[2]
            for k, v in waits.items():
                if seen.get(k, 0) >= v:
                    continue
                seen[k] = v
                e.wait_ge(sems[k[0]][k[1]], v)
            if o.fn is None:
                continue
            name, a, kw = o.fn
            ins = getattr(e, name)(*a, **kw)
            if o.sig is not None:
                if o.dma:
                    ins.then_inc(sems["dma"][o.sig[1]], 16)
                else:
                    ins.then_inc(sems[o.sig[0]][o.sig[1]], 1)


def build_program(nst=NST, debug=False, carry=True):
    nc = bass.Bass("TRN2", target_bir_lowering=False)
    P = Plan()
    nrows = 128 + nst * ST * 128

    def din(name, shape, dt=F32):
        return nc.dram_tensor(name, list(shape), dt, kind="ExternalInput").ap()

    x_d = din("x", [128 + T_CORE, D])
    vecsA_d = din("vecsA", [84, 128])
    vecsB_d = din("vecsB", [67, 128])
    wada_d = din("w_ada", [D, 3 * D])
    bada_d = din("b_ada", [1, 3 * D])
    win_d = din("w_in", [D, 4480])
    bin_d = din("b_in", [1, 4480])
    wout_d = din("w_out", [D, D])
    glu_w_d = din("glu_w", [512, 512])
    glu_b_d = din("glu_b", [1, 512])
    fg_d = din("final_gain", [1, D])
    sinks_d = din("sinks", [1, 24])
    bre_d = din("b_re", [32, 64, 16])
    bim_d = din("b_im", [32, 64, 16])
    cre_d = din("c_re", [32, 16, 64])
    cim_d = din("c_im", [32, 16, 64])
    flags_d = din("flags", [128, 4])
    xprev_d = din("xprev", [NPRE_ST * ST * 128, D])
    cmask_d = din("cmask", [128, NPRE_ST * ST])
    out_d = nc.dram_tensor("out", [T_CORE, D], F32, kind="ExternalOutput").ap()
    dbg = {}

    def dout(name, shape):
        dbg[name] = nc.dram_tensor(name, list(shape), F32, kind="ExternalOutput").ap()
        return dbg[name]

    wfm = nc.dram_tensor("wfm", [D, 2432], BF16).ap()
    wtm = nc.dram_tensor("wtm", [D, 2240], BF16).ap()
    wo = nc.dram_tensor("wo", [D, D], BF16).ap()
    B_wfm, B_wtm, B_wo = Buf("wfm"), Buf("wtm"), Buf("wo")

    es = ExitStack()
    with es:
        def sb(name, shape, dt=F32):
            return es.enter_context(nc.sbuf_tensor("sb_" + name, list(shape), dt))

        banks = [es.enter_context(nc.psum_tensor("bank%d" % i, [128, 512], F32)) for i in range(8)]
        B_bank = [Buf("bank%d" % i) for i in range(8)]
        bank_rr = [0]

        bank_pool = [list(range(7))]

        def nb():
            pool = bank_pool[0]
            i = pool[bank_rr[0] % len(pool)]
            bank_rr[0] += 1
            return banks[i], B_bank[i]

        ident_f = sb("ident_f", [128, 128]); B_idf = Buf("idf")
        ident_b = sb("ident_b", [128, 128], BF16); B_idb = Buf("idb")
        ones_b = sb("ones_b", [1, 512], BF16); B_ones = Buf("ones")
        epsc = sb("epsc", [128, 1]); B_epsc = Buf("epsc")
        vt = sb("vt", [128, 84]); B_vt = Buf("vt")
        vtb = sb("vtb", [128, 67]); B_vtb = Buf("vtb")
        vstage = sb("vstage", [84, 128]); B_vst = Buf("vstage")
        vstageb = sb("vstageb", [67, 128]); B_vstb = Buf("vstageb")
        scfm = sb("scfm", [128, 16]); B_scfm = Buf("scfm")
        modfm = sb("modfm", [128, 32]); B_modfm = Buf("modfm")
        gsfm = sb("gsfm", [128, 16]); B_gsfm = Buf("gsfm")
        gateB = sb("gateB", [128, D]); B_gateB = Buf("gateB")
        fgB = sb("fgB", [128, D]); B_fgB = Buf("fgB")
        esink = sb("esink", [128, 24]); B_esink = Buf("esink")
        flags = sb("flags", [128, 4]); B_flags = Buf("flags")
        maskc = sb("maskc", [128, 128], BF16); B_maskc = Buf("maskc")
        maskp = sb("maskp", [128, 128], BF16); B_maskp = Buf("maskp")
        maskp0 = sb("maskp0", [128, 128], BF16); B_maskp0 = Buf("maskp0")
        brow_tm = sb("brow_tm", [1, 2240], BF16); B_btm = Buf("brow_tm")
        brow_glu = sb("brow_glu", [1, 512], BF16); B_bglu = Buf("brow_glu")
        wglu = sb("wglu", [128, 4, 512], BF16); B_wglu = Buf("wglu")
        xt = sb("xt", [128, D]); B_xt = Buf("xt")
        xn = sb("xn", [128, D], BF16); B_xn = Buf("xn")
        junk = xn; B_junk = B_xn
        hT = sb("hT", [128, 16, 512], BF16); B_hT = [Buf("hT%d" % i) for i in range(ST)]
        wsl = [sb("wsl%d" % i, [128, 16, 256], BF16) for i in range(2)]
        B_wsl = [Buf("wsl%d" % i) for i in range(2)]
        wsl_rr = [0]
        arena = sb("arena", [128, 8192])
        ab = arena[:, :].bitcast(BF16)
        QT = ab[:, 0:6144].rearrange("p (c t) -> p c t", c=12)
        za = ab[:, 6144:12288].rearrange("p (t n) -> p t n", t=4)
        UT = ab[:, 12288:14336].rearrange("p (c t) -> p c t", c=4)
        zs = ab[:, 14336:16384].rearrange("p (t n) -> p t n", t=4)
        cells = []
        B_QT = [Buf("QT%d" % c) for c in range(12)]
        B_za = [Buf("za%d" % t) for t in range(4)]
        B_UT = [Buf("UT%d" % c) for c in range(4)]
        B_zs = [Buf("zs%d" % t) for t in range(4)]
        for c in range(12):
            cells.append((c * 1024, (c + 1) * 1024, B_QT[c]))
        for t in range(4):
            cells.append((12288 + t * 3072, 12288 + (t + 1) * 3072, B_za[t]))
        for c in range(4):
            cells.append((24576 + c * 1024, 24576 + (c + 1) * 1024, B_UT[c]))
        for t in range(4):
            cells.append((28672 + t * 1024, 28672 + (t + 1) * 1024, B_zs[t]))

        def xr_cells(t):
            lo, hi = t * 8192, (t + 1) * 8192
            return [b for (l, h, b) in cells if l < hi and h > lo]
        all_cells = [b for (_, _, b) in cells]

        def xr(t):
            return arena[:, t * 2048:(t + 1) * 2048]

        KT = sb("KT", [128, 3, 640], BF16); B_KT = [Buf("KT%d" % g) for g in range(3)]
        Vt = sb("Vt", [128, 5, 3, 65], BF16); B_Vt = [Buf("Vt%d" % i) for i in range(5)]
        Eb = [sb("Eb%d" % i, [128, 512], BF16) for i in range(8)]; B_Eb = [Buf("Eb%d" % i) for i in range(8)]
        Mc = sb("Mc", [128, 512], BF16); Mp = sb("Mp", [128, 512], BF16); Mp0 = sb("Mp0", [128, 512], BF16)
        B_Mc, B_Mp, B_Mp0 = Buf("Mc"), Buf("Mp"), Buf("Mp0")
        attn_f = sb("attn_f", [128, 1536]); B_attn = [Buf("attn%d" % g) for g in range(3)]
        den = sb("den", [128, 3, 8]); B_den = [Buf("den%d" % g) for g in range(3)]
        rden = sb("rden", [128, 3, 8]); B_rden = [Buf("rden%d" % g) for g in range(3)]
        stat = sb("stat", [128, 8]); B_stat = [Buf("stat%d" % i) for i in range(8)]
        mixed = sb("mixed", [128, D], BF16); B_mixed = Buf("mixed"); B_mixA = Buf("mixA"); B_mixS = Buf("mixS")
        CT = sb("CT", [128, 16, 128]); SN = sb("SN", [128, 16, 128]); B_tab = Buf("tab")
        BL = sb("BL", [128, 32, 128], BF16); B_BL = Buf("BL")
        Cm = sb("Cm", [128, 48, 32], BF16); B_Cm = Buf("Cm")
        p24 = sb("p24", [128, 2, 512], BF16); B_p24 = [Buf("p24a"), Buf("p24b")]
        Dd = sb("Dd", [128, 4, 128], BF16); B_Dd = Buf("Dd")
        rho = sb("rho", [128, 16]); c128 = sb("c128", [128, 16]); s128 = sb("s128", [128, 16]); B_par = Buf("par")
        gin_r = sb("gin_r", [128, 16]); gin_i = sb("gin_i", [128, 16]); B_gin = [Buf("gin%d" % q) for q in range(4)]
        st1 = sb("st1", [128, 512]); st2 = sb("st2", [128, 512]); st3 = sb("st3", [128, 512]); st4 = sb("st4", [128, 512])
        B_st = [Buf("st%d" % i) for i in range(4)]
        st5 = sb("st5", [128, 512]); st6 = sb("st6", [128, 512]); st7 = sb("st7", [128, 512]); st8 = sb("st8", [128, 512])
        B_stb = [Buf("stb%d" % i) for i in range(4)]
        spool2 = sb("spool2", [128, 1536])
        p24b = sb("p24b", [128, 2, 512], BF16)
        wr_ = st1; wi_ = st3; B_wr, B_wi = B_st[0], B_st[2]
        spool = sb("spool", [128, 2048])
        gr_ = spool[:, 0:512].rearrange("p (a b) -> p a b", a=4); gi_ = spool[:, 512:1024].rearrange("p (a b) -> p a b", a=4)
        B_gr, B_gi = Buf("gr"), Buf("gi")
        hr_ = spool[:, 1024:1280].bitcast(BF16).rearrange("p (a b) -> p a b", a=4)
        hi_ = spool[:, 1280:1536].bitcast(BF16).rearrange("p (a b) -> p a b", a=4); B_hr, B_hi = Buf("hr"), Buf("hi")
        ctmp = sb("ctmp", [128, 8, 4]); B_ctmp = Buf("ctmp")
        g1 = spool[:, 1536:2048]; B_g1 = Buf("g1")
        g1b = sb("g1b", [128, 512], BF16); B_g1b = Buf("g1b")
        g1T = sb("g1T", [128, 4, 128], BF16); B_g1T = Buf("g1T")
        sg = sb("sg", [128, 512]); B_sg = Buf("sg")
        s2 = sg; B_s2 = B_sg
        otmp = st2; B_otmp = B_st[1]
        sm = sb("sm", [128, 64]); B_sm = Buf("sm")
        par2 = sb("par2", [128, 64]); B_par2 = Buf("par2")
        bbar = sb("bbar", [128, 2, 16, 16]); B_bbar = Buf("bbar")
        decT = spool[:, :].bitcast(BF16).rearrange("p (q i n) -> p q i n", q=16, i=2); B_decT = Buf("decT")
        ssA = sb("ssA", [128, 4]); B_ssA = [Buf("ssA%d" % i) for i in range(3)]
        brow_u = sb("brow_u", [1, 512], BF16); B_bu = Buf("brow_u")
        u0B = sb("u0B", [128, 512]); B_u0B = Buf("u0B")
        rsA = sb("rsA", [128, 4]); B_rsA = [Buf("rsA%d" % i) for i in range(4)]
        cmask = sb("cmask", [128, NPRE_ST * ST]); B_cmask = Buf("cmask")
        hloc = sb("hloc", [128, 8, 16]); B_hloc = Buf("hloc")
        uTM = mixed[:, :].rearrange("p (t n) -> p t n", t=4)
        SKr = attn_f[:, 0:512].rearrange("p (q k) -> p q k", q=16)
        SKi = attn_f[:, 512:1024].rearrange("p (q k) -> p q k", q=16)

        def V(fn, r=(), w=()): return P.op("vector", fn, r, w)
        def A(fn, r=(), w=()): return P.op("scalar", fn, r, w)
        def G(fn, r=(), w=()): return P.op("gpsimd", fn, r, w)
        def T(fn, r=(), w=()): return P.op("tensor", fn, r, w)
        B_swq = Buf("swdge_chain")

        def DMA(fn, r=(), w=(), q="sync", out=False):
            if q == "gpsimd":
                return P.op(q, fn, tuple(r), tuple(w) + (B_swq,), dma=True, out=out)
            return P.op(q, fn, r, w, dma=True, out=out)

        def cast(dst, src, wbuf):
            DMA(lambda e: e.dma_start(out=dst, in_=src), (), (wbuf,), q="gpsimd")
        cast(wfm[:, 0:1536], win_d[:, 0:1536], B_wfm)
        for g in range(3):
            cast(wfm[:, 1536 + 128 * g:1536 + 128 * g + 64], win_d[:, 1536 + 64 * g:1600 + 64 * g], B_wfm)
            cast(wfm[:, 1536 + 128 * g + 64:1536 + 128 * g + 128], win_d[:, 1536 + 64 * g:1600 + 64 * g], B_wfm)
        cast(wfm[:, 1920:2432], win_d[:, 3456:3968], B_wfm)
        cast(wtm[:, 0:192], win_d[:, 1728:1920], B_wtm)
        cast(wtm[:, 192:1728], win_d[:, 1920:3456], B_wtm)
        cast(wtm[:, 1728:2240], win_d[:, 3968:4480], B_wtm)
        cast(wo[:, :], wout_d[:, :], B_wo)
        DMA(lambda e: e.dma_start(out=brow_tm[0:1, 0:192], in_=bin_d[0:1, 1728:1920]), (), (B_btm,), q="gpsimd")
        DMA(lambda e: e.dma_start(out=brow_tm[0:1, 192:1728], in_=bin_d[0:1, 1920:3456]), (), (B_btm,), q="gpsimd")
        DMA(lambda e: e.dma_start(out=brow_tm[0:1, 1728:2240], in_=bin_d[0:1, 3968:4480]), (), (B_btm,), q="gpsimd")
        DMA(lambda e: e.dma_start(out=brow_glu[0:1, :], in_=glu_b_d[0:1, :]), (), (B_bglu,), q="gpsimd")
        DMA(lambda e: e.dma_start(out=brow_u[0:1, :], in_=bin_d[0:1, 3456:3968]), (), (B_bu,), q="gpsimd")
        DMA(lambda e: e.dma_start(out=cmask[:, :], in_=cmask_d), (), (B_cmask,))
        DMA(lambda e: e.dma_start(out=wglu[:, :, :], in_=glu_w_d.rearrange("(k p) n -> p k n", p=128)), (), (B_wglu,), q="gpsimd")

        DMA(lambda e: e.dma_start(out=vstage[:, :], in_=vecsA_d), (), (B_vst,))
        DMA(lambda e: e.dma_start(out=vstageb[:, :], in_=vecsB_d), (), (B_vstb,))
        DMA(lambda e: e.dma_start(out=flags[:, :], in_=flags_d), (), (B_flags,))
        DMA(lambda e: e.dma_start(out=fgB[:, :], in_=fg_d[0:1, :].partition_broadcast(128)), (), (B_fgB,))
        DMA(lambda e: e.dma_start(out=gateB[:, :], in_=bada_d[0:1, 2 * D:3 * D].partition_broadcast(128)), (), (B_gateB,))
        DMA(lambda e: e.dma_start(out=esink[:, :], in_=sinks_d[0:1, :].partition_broadcast(128)), (), (B_esink,))

        G(lambda e: e.memset(ident_f[:, :], 1.0), (), (B_idf,))
        G(lambda e: e.affine_select(out=ident_f[:, :], in_=ident_f[:, :], pattern=[[-1, 128]], compare_op=ALU.is_equal,
                                    fill=0.0, base=0, channel_multiplier=1), (B_idf,), (B_idf,))
        V(lambda e: e.tensor_copy(out=ident_b[:, :], in_=ident_f[:, :]), (B_idf,), (B_idb,))
        V(lambda e: e.memset(ones_b[:, :], 1.0), (), (B_ones,))
        V(lambda e: e.memset(epsc[:, :], EPS), (), (B_epsc,))
        G(lambda e: e.memset(maskc[:, :], 1.0), (), (B_maskc,))
        G(lambda e: e.affine_select(out=maskc[:, :], in_=maskc[:, :], pattern=[[1, 128]], compare_op=ALU.is_ge,
                                    fill=0.0, base=0, channel_multiplier=-1), (B_maskc,), (B_maskc,))
        G(lambda e: e.memset(maskp[:, :], 1.0), (), (B_maskp,))
        G(lambda e: e.affine_select(out=maskp[:, :], in_=maskp[:, :], pattern=[[-1, 128]], compare_op=ALU.is_gt,
                                    fill=0.0, base=0, channel_multiplier=1), (B_maskp,), (B_maskp,))
        V(lambda e: e.tensor_scalar(out=maskp0[:, :], in0=maskp[:, :], scalar1=flags[:, 0:1], scalar2=None, op0=ALU.mult),
          (B_maskp, B_flags), (B_maskp0,))
        A(lambda e: e.activation(out=esink[:, :], in_=esink[:, :], func=AF.Exp), (B_esink,), (B_esink,))
        for (Mx, Bx, m01, Bm) in ((Mc, B_Mc, maskc, B_maskc), (Mp, B_Mp, maskp, B_maskp), (Mp0, B_Mp0, maskp0, B_maskp0)):
            V((lambda Mx, m01: lambda e: e.tensor_scalar(out=Mx[:, :].rearrange("p (j q) -> p j q", j=4),
                                                         in0=m01[:, :].unsqueeze(1).to_broadcast([128, 4, 128]),
                                                         scalar1=-1.0, scalar2=30000.0, op0=ALU.add, op1=ALU.mult))(Mx, m01), (Bm,), (Bx,))
        V(lambda e: e.memset(Vt[:, :, :, 64:65], 1.0), (), tuple(B_Vt))

        bk, Bb = nb()
        T(lambda e: e.transpose(bk[:, 0:84], vstage[:, :], ident_f[0:84, 0:84]), (B_vst, B_idf), (Bb,))
        V(lambda e: e.tensor_copy(out=vt[:, :], in_=bk[:, 0:84]), (Bb,), (B_vt,))
        bk2, Bb2 = nb()
        T(lambda e: e.transpose(bk2[:, 0:67], vstageb[:, :], ident_f[0:67, 0:67]), (B_vstb, B_idf), (Bb2,))
        V(lambda e: e.tensor_copy(out=vtb[:, :], in_=bk2[:, 0:67]), (Bb2,), (B_vtb,))
        A(lambda e: e.activation(out=scfm[:, :], in_=vt[:, 64:80], func=AF.Silu), (B_vt,), (B_scfm,))

        screp = xt[:, :].rearrange("p (k m) -> p k m", k=16)
        V(lambda e: e.tensor_copy(out=screp, in_=scfm[:, :].unsqueeze(2).to_broadcast([128, 16, 128])), (B_scfm,), (B_xt,))
        wada_v = wada_d.rearrange("(k p) n -> p k n", p=128)
        stg = [arena[:, 0:4096].rearrange("p (k n) -> p k n", k=16), arena[:, 4096:8192].rearrange("p (k n) -> p k n", k=16)]
        B_stg = [Buf("stg0"), Buf("stg1")]
        modbk, B_modbk = banks[7], B_bank[7]
        for pc in range(24):
            s = pc % 2
            DMA((lambda pc, s: lambda e: e.dma_start(out=stg[s], in_=wada_v[:, :, pc * 256:(pc + 1) * 256]))(pc, s),
                (), (B_stg[s],))
            if pc < 16:
                for cc in range(2):
                    j = pc * 2 + cc
                    for k in range(16):
                        T((lambda s, cc, k, j: lambda e: e.matmul(modbk[:, j:j + 1], lhsT=stg[s][:, k, cc * 128:(cc + 1) * 128],
                                                                  rhs=scfm[:, k:k + 1], start=(k == 0), stop=(k == 15)))(s, cc, k, j),
                          (B_stg[s], B_scfm), (B_modbk,))
            else:
                gb, B_gb = nb()
                for k in range(16):
                    T((lambda s, k, gb: lambda e: e.matmul(gb[:, 0:256], lhsT=screp[:, k, :], rhs=stg[s][:, k, :],
                                                           start=(k == 0), stop=(k == 15)))(s, k, gb),
                      (B_stg[s], B_xt), (B_gb,))
                c0 = (pc - 16) * 256
                V((lambda gb, c0: lambda e: e.tensor_tensor(out=gateB[:, c0:c0 + 256], in0=gb[:, 0:256], in1=gateB[:, c0:c0 + 256],
                                                            op=ALU.add))(gb, c0), (B_gb, B_gateB), (B_gateB,))
        V(lambda e: e.tensor_tensor(out=modfm[:, :], in0=modbk[:, 0:32], in1=vt[:, 0:32], op=ALU.add), (B_modbk, B_vt), (B_modfm,))
        V(lambda e: e.scalar_tensor_tensor(out=gsfm[:, :], in0=modfm[:, 16:32], scalar=1.0, in1=vt[:, 32:48],
                                           op0=ALU.add, op1=ALU.mult), (B_modfm, B_vt), (B_gsfm,))
        for b in all_cells:
            b.w = dict(B_stg[0].w); b.w.update(B_stg[1].w)
            b.r = dict(B_stg[0].r); b.r.update(B_stg[1].r)

        step = sm[:, 0:16]; lrs = sm[:, 16:32]; th = sm[:, 32:48]; tmpa = sm[:, 48:64]
        A(lambda e: e.activation(out=step, in_=vtb[:, 32:48], func=AF.Exp), (B_vtb,), (B_sm,))
        V(lambda e: e.tensor_tensor(out=lrs, in0=vtb[:, 0:16], in1=step, op=ALU.mult), (B_vtb, B_sm), (B_sm,))
        V(lambda e: e.tensor_tensor(out=th, in0=vtb[:, 16:32], in1=step, op=ALU.mult), (B_vtb, B_sm), (B_sm,))
        A(lambda e: e.activation(out=rho[:, :], in_=lrs, func=AF.Exp), (B_sm,), (B_par,))

        def sincos(out_sin, out_cos, ang, tmps, rbufs, wbufs):
            tf, tf2 = tmps
            ti = tf2.bitcast(I32)
            allb = tuple(rbufs) + tuple(wbufs)
            V(lambda e: e.tensor_scalar(out=tf, in0=ang, scalar1=1.0 / TWO_PI, scalar2=None, op0=ALU.mult), allb, wbufs)
            for k, dst in enumerate((out_sin, out_cos)):
                if k == 1:
                    V(lambda e: e.tensor_scalar(out=tf, in0=tf, scalar1=0.25, scalar2=None, op0=ALU.add), wbufs, wbufs)
                V(lambda e: e.tensor_copy(out=ti, in_=tf), wbufs, wbufs)
                V(lambda e: e.tensor_copy(out=tf2, in_=ti), wbufs, wbufs)
                V(lambda e: e.tensor_tensor(out=tf2, in0=tf, in1=tf2, op=ALU.subtract), wbufs, wbufs)
                A(lambda e: e.activation(out=dst, in_=tf2, func=AF.Sin, scale=TWO_PI), wbufs, wbufs)

        mixf = mixed[:, :].bitcast(F32)
        sm2 = mixf[:, 256:384]; B_sm2 = B_mixed
        sth = sm2[:, 0:16]; cth = sm2[:, 16:32]; nr = sm2[:, 32:48]; ni = sm2[:, 48:64]
        dn = sm2[:, 64:80]; cfr = sm2[:, 80:96]; cfi = sm2[:, 96:112]; t5 = sm2[:, 112:128]
        sincos(sth, cth, th, (t5, dn), (B_sm,), (B_sm2,))
        V(lambda e: e.tensor_tensor(out=nr, in0=cth, in1=rho[:, :], op=ALU.mult), (B_sm2, B_par), (B_sm2,))
        V(lambda e: e.tensor_scalar(out=nr, in0=nr, scalar1=-1.0, scalar2=None, op0=ALU.add), (B_sm2,), (B_sm2,))
        V(lambda e: e.tensor_tensor(out=ni, in0=sth, in1=rho[:, :], op=ALU.mult), (B_sm2, B_par), (B_sm2,))
        lre = vtb[:, 0:16]; lim = vtb[:, 16:32]
        V(lambda e: e.tensor_tensor(out=dn, in0=lre, in1=lre, op=ALU.mult), (B_vtb,), (B_sm2,))
        V(lambda e: e.tensor_tensor(out=t5, in0=lim, in1=lim, op=ALU.mult), (B_vtb,), (B_sm2,))
        V(lambda e: e.tensor_tensor(out=dn, in0=dn, in1=t5, op=ALU.add), (B_sm2,), (B_sm2,))
        V(lambda e: e.reciprocal(out=dn, in_=dn), (B_sm2,), (B_sm2,))
        V(lambda e: e.tensor_tensor(out=cfr, in0=nr, in1=lre, op=ALU.mult), (B_sm2, B_vtb), (B_sm2,))
        V(lambda e: e.tensor_tensor(out=t5, in0=ni, in1=lim, op=ALU.mult), (B_sm2, B_vtb), (B_sm2,))
        V(lambda e: e.tensor_tensor(out=cfr, in0=cfr, in1=t5, op=ALU.add), (B_sm2,), (B_sm2,))
        V(lambda e: e.tensor_tensor(out=cfr, in0=cfr, in1=dn, op=ALU.mult), (B_sm2,), (B_sm2,))
        V(lambda e: e.tensor_tensor(out=cfi, in0=ni, in1=lre, op=ALU.mult), (B_sm2, B_vtb), (B_sm2,))
        V(lambda e: e.tensor_tensor(out=t5, in0=nr, in1=lim, op=ALU.mult), (B_sm2, B_vtb), (B_sm2,))
        V(lambda e: e.tensor_tensor(out=cfi, in0=cfi, in1=t5, op=ALU.subtract), (B_sm2,), (B_sm2,))
        V(lambda e: e.tensor_tensor(out=cfi, in0=cfi, in1=dn, op=ALU.mult), (B_sm2,), (B_sm2,))
        iot = mixf[:, 0:128]; B_iot = B_mixed
        G(lambda e: e.iota(iot[:, :], pattern=[[1, 128]], base=0, channel_multiplier=0,
                           allow_small_or_imprecise_dtypes=True), (), (B_iot,))
        ang = st1[:, 0:128]; tmpw = st2[:, 0:128]; tmpw2 = st2[:, 128:256]
        for q in range(16):
            V((lambda q: lambda e: e.tensor_scalar(out=ang, in0=iot[:, :], scalar1=th[:, q:q + 1], scalar2=None, op0=ALU.mult))(q),
              (B_iot, B_sm), (B_st[0],))
            sincos(SN[:, q, :], CT[:, q, :], ang, (tmpw, tmpw2), (B_st[0],), (B_st[1], B_tab))
        V(lambda e: e.tensor_scalar(out=tmpa, in0=th, scalar1=128.0, scalar2=None, op0=ALU.mult), (B_sm,), (B_sm,))
        sincos(s128[:, :], c128[:, :], tmpa, (t5, nr), (B_sm,), (B_sm2, B_par))
        V(lambda e: e.tensor_copy(out=par2[:, 0:32], in_=sm2[:, 0:32]), (B_sm2,), (B_par2,))
        A(lambda e: e.activation(out=t5, in_=lrs, func=AF.Exp, scale=128.0), (B_sm,), (B_sm2,))
        V(lambda e: e.tensor_tensor(out=par2[:, 32:48], in0=c128[:, :], in1=t5, op=ALU.mult), (B_par, B_sm2), (B_par2,))
        V(lambda e: e.tensor_tensor(out=par2[:, 48:64], in0=s128[:, :], in1=t5, op=ALU.mult), (B_par, B_sm2), (B_par2,))
        iotr = mixf[:, 384:512]
        G(lambda e: e.iota(iotr, pattern=[[-1, 128]], base=127, channel_multiplier=0, allow_small_or_imprecise_dtypes=True), (), (B_iot,))
        for q in range(16):
            V((lambda q: lambda e: e.tensor_scalar(out=ang, in0=iotr, scalar1=th[:, q:q + 1], scalar2=None, op0=ALU.mult))(q),
              (B_iot, B_sm), (B_st[0],))
            sincos(st3[:, 0:128], st4[:, 0:128], ang, (tmpw, tmpw2), (B_st[0],), (B_st[1], B_st[2], B_st[3]))
            A((lambda q: lambda e: e.activation(out=st1[:, 128:256], in_=iotr, func=AF.Exp, scale=lrs[:, q:q + 1]))(q), (B_iot, B_sm), (B_st[0],))
            V(lambda e: e.tensor_tensor(out=st4[:, 0:128], in0=st4[:, 0:128], in1=st1[:, 128:256], op=ALU.mult), (B_st[0], B_st[3]), (B_st[3],))
            V(lambda e: e.tensor_tensor(out=st3[:, 0:128], in0=st3[:, 0:128], in1=st1[:, 128:256], op=ALU.mult), (B_st[0], B_st[2]), (B_st[2],))
            for part, srct, Bs in ((0, st4, B_st[3]), (1, st3, B_st[2])):
                bkq, Bbq = nb()
                T((lambda bkq, srct: lambda e: e.transpose(bkq[:, 0:128], srct[:, 0:128], ident_f[:, :]))(bkq, srct), (Bs, B_idf), (Bbq,))
                A((lambda bkq, q, part: lambda e: e.copy(out=decT[:, q, part, :], in_=bkq[:, 0:128]))(bkq, q, part), (Bbq,), (B_decT,))

        xnf = xn[:, :].bitcast(F32)
        braw = xnf[:, 0:512].rearrange("p (i q c) -> p i q c", i=2, q=16); B_braw = B_xn
        for i, src in enumerate((bre_d, bim_d)):
            srcv = src.rearrange("(q l) p c -> (l p) q c", l=2)
            DMA((lambda i, srcv: lambda e: e.dma_start(out=braw[:, i, :, :], in_=srcv))(i, srcv), (), (B_braw,))
        cfr_b = cfr.unsqueeze(2).to_broadcast([128, 16, 16]); cfi_b = cfi.unsqueeze(2).to_broadcast([128, 16, 16])
        tb1 = st3[:, 0:256].rearrange("p (q c) -> p q c", q=16); tb2 = st4[:, 0:256].rearrange("p (q c) -> p q c", q=16)
        V(lambda e: e.tensor_tensor(out=tb1, in0=braw[:, 0, :, :], in1=cfr_b, op=ALU.mult), (B_braw, B_sm2), (B_st[2],))
        V(lambda e: e.tensor_tensor(out=tb2, in0=braw[:, 1, :, :], in1=cfi_b, op=ALU.mult), (B_braw, B_sm2), (B_st[3],))
        V(lambda e: e.tensor_tensor(out=bbar[:, 0, :, :], in0=tb1, in1=tb2, op=ALU.subtract), (B_st[2], B_st[3]), (B_bbar,))
        V(lambda e: e.tensor_tensor(out=tb1, in0=braw[:, 1, :, :], in1=cfr_b, op=ALU.mult), (B_braw, B_sm2), (B_st[2],))
        V(lambda e: e.tensor_tensor(out=tb2, in0=braw[:, 0, :, :], in1=cfi_b, op=ALU.mult), (B_braw, B_sm2), (B_st[3],))
        V(lambda e: e.tensor_tensor(out=bbar[:, 1, :, :], in0=tb1, in1=tb2, op=ALU.add), (B_st[2], B_st[3]), (B_bbar,))
        xq = mixf[:, 128:256]; B_xq = B_mixed
        for i in range(2):
            for q in range(16):
                r0 = ((2 * q) % 8) * 16
                V(lambda e: e.memset(xq[:, :], 0.0), (), (B_xq,))
                V((lambda i, q, r0: lambda e: e.tensor_copy(out=xq[0:64, r0:r0 + 16], in_=bbar[0:64, i, q, :]))(i, q, r0), (B_bbar,), (B_xq,))
                V((lambda i, q, r0: lambda e: e.tensor_copy(out=xq[64:128, r0 + 16:r0 + 32], in_=bbar[64:128, i, q, :]))(i, q, r0), (B_bbar,), (B_xq,))
                bkq, Bbq = nb()
                T((lambda bkq: lambda e: e.transpose(bkq[:, 0:128], xq[:, :], ident_f[:, :]))(bkq), (B_xq, B_idf), (Bbq,))
                A((lambda i, q, bkq: lambda e: e.copy(out=BL[:, 16 * i + q, :], in_=bkq[:, 0:128]))(i, q, bkq), (Bbq,), (B_BL,))
        zc = hT[0:32, :, :].bitcast(F32).rearrange("p k n -> p (k n)").rearrange("p (i q n) -> p i q n", i=2, q=16); B_zc = Buf("zc")
        V(lambda e: e.memset(zc[:, :, :, :], 0.0), (), (B_zc,))
        for i, src in enumerate((cre_d, cim_d)):
            for l in range(2):
                srcv = src.rearrange("(q l) c p -> l c q p", l=2)[l]
                DMA((lambda i, l, srcv: lambda e: e.dma_start(out=zc[16 * l:16 * l + 16, i, :, 64 * l:64 * l + 64], in_=srcv))(i, l, srcv),
                    (B_zc,), (B_zc,))
        for i in range(2):
            for q in range(16):
                bkq, Bbq = nb()
                T((lambda i, q, bkq: lambda e: e.transpose(bkq[:, 0:32], zc[:, i, q, :], ident_f[0:32, 0:32]))(i, q, bkq), (B_zc, B_idf), (Bbq,))
                if i == 0:
                    A((lambda q, bkq: lambda e: e.copy(out=Cm[:, q, :], in_=bkq[:, 0:32]))(q, bkq), (Bbq,), (B_Cm,))
                    A((lambda q, bkq: lambda e: e.mul(out=Cm[:, 32 + q, :], in_=bkq[:, 0:32], mul=-1.0))(q, bkq), (Bbq,), (B_Cm,))
                else:
                    A((lambda q, bkq: lambda e: e.mul(out=Cm[:, 16 + q, :], in_=bkq[:, 0:32], mul=-1.0))(q, bkq), (Bbq,), (B_Cm,))
        for c in range(4):
            V((lambda c: lambda e: e.tensor_scalar(out=Dd[:, c, :], in0=ident_f[:, :], scalar1=vt[:, 80 + c:81 + c], scalar2=None,
                                                   op0=ALU.mult))(c), (B_idf, B_vt), (B_Dd,))

        for b in B_hT:
            b.w = dict(B_zc.w); b.r = dict(B_zc.r)

        def rstd_from(ss_ap, out_ap, n, rb, wb):
            A(lambda e: e.activation(out=out_ap, in_=ss_ap, func=AF.Sqrt, bias=epsc[:, 0:1], scale=1.0 / n), tuple(rb) + (B_epsc,), wb)
            V(lambda e: e.reciprocal(out=out_ap, in_=out_ap), wb, wb)

        def load_norm_transpose(row0, slot):
            DMA(lambda e: e.dma_start(out=xt[:, :], in_=x_d[row0:row0 + 128, :]), (), (B_xt,))
            A(lambda e: e.activation(out=junk[:, :], in_=xt[:, :], func=AF.Square, accum_out=stat[:, 0:1]),
              (B_xt,), (B_junk, B_stat[0]))
            rstd_from(stat[:, 0:1], stat[:, 1:2], float(D), (B_stat[0],), (B_stat[1],))
            A(lambda e: e.activation(out=xn[:, :], in_=xt[:, :], func=AF.Identity, scale=stat[:, 1:2]), (B_xt, B_stat[1]), (B_xn,))
            for half in range(2):
                bk_, Bb_ = nb()
                bkv = bk_[:, :].bitcast(BF16)
                for j in range(8):
                    k = half * 8 + j
                    T((lambda bkv, j, k: lambda e: e.transpose(bkv[:, j * 128:(j + 1) * 128], xn[:, k * 128:(k + 1) * 128], ident_b[:, :]))(bkv, j, k),
                      (B_xn, B_idb), (Bb_,))
                for j in range(8):
                    k = half * 8 + j
                    if j % 2 == 0:
                        A((lambda bkv, j, k: lambda e: e.activation(out=hT[:, k, slot * 128:(slot + 1) * 128], in_=bkv[:, j * 128:(j + 1) * 128],
                                                                    func=AF.Identity, scale=gsfm[:, k:k + 1], bias=modfm[:, k:k + 1]))(bkv, j, k),
                          (Bb_, B_gsfm, B_modfm), (B_hT[slot],))
                    else:
                        V((lambda bkv, j, k: lambda e: e.tensor_scalar(out=hT[:, k, slot * 128:(slot + 1) * 128], in0=bkv[:, j * 128:(j + 1) * 128],
                                                                       scalar1=gsfm[:, k:k + 1], scalar2=modfm[:, k:k + 1],
                                                                       op0=ALU.mult, op1=ALU.add))(bkv, j, k),
                          (Bb_, B_gsfm, B_modfm), (B_hT[slot],))

        def load_w(src_ap, ncols, srcbuf):
            s = wsl_rr[0]
            wsl_rr[0] = 1 - s
            DMA(lambda e: e.dma_start(out=wsl[s][:, :, 0:ncols], in_=src_ap), (srcbuf,), (B_wsl[s],))
            return wsl[s], B_wsl[s]

        evac_rr = [0]

        def evac_copy(out_ap, in_ap, rb, wb):
            if evac_rr[0] % 2 == 0:
                A(lambda e: e.copy(out=out_ap, in_=in_ap), rb, wb)
            else:
                V(lambda e: e.tensor_copy(out=out_ap, in_=in_ap), rb, wb)
            evac_rr[0] += 1

        wfm_v = wfm.rearrange("(k p) n -> p k n", p=128)
        wtm_v = wtm.rearrange("(k p) n -> p k n", p=128)
        wo_v = wo.rearrange("(k p) n -> p k n", p=128)

        def fm_chunk(w, Bw, cc, col0, ntok, tok0, dst_ap, dst_buf, hbufs):
            bk_, Bb_ = nb()
            ch = col0 // 128
            for k in range(16):
                T((lambda k: lambda e: e.matmul(bk_[:, 0:ntok], lhsT=w[:, k, cc * 128:(cc + 1) * 128], rhs=hT[:, k, tok0:tok0 + ntok],
                                                start=(k == 0), stop=(k == 15)))(k), (Bw,) + tuple(hbufs), (Bb_,))
            bcol = vtb[:, 48 + ch:49 + ch]
            if evac_rr[0] % 2 == 0:
                A(lambda e: e.activation(out=dst_ap, in_=bk_[:, 0:ntok], func=AF.Identity, bias=bcol, scale=1.0), (Bb_, B_vtb), (dst_buf,))
            else:
                V(lambda e: e.tensor_scalar(out=dst_ap, in0=bk_[:, 0:ntok], scalar1=bcol, scalar2=None, op0=ALU.add), (Bb_, B_vtb), (dst_buf,))
            evac_rr[0] += 1

        def tm_unit(w, Bw, ncols, col0, slot, hbuf):
            bk_, Bb_ = nb()
            for k in range(16):
                T((lambda k: lambda e: e.matmul(bk_[:, 0:ncols], lhsT=hT[:, k, slot * 128:(slot + 1) * 128], rhs=w[:, k, 0:ncols],
                                                start=(k == 0), stop=False))(k), (Bw, hbuf), (Bb_,))
            T(lambda e: e.matmul(bk_[:, 0:ncols], lhsT=ones_b[0:1, 0:128], rhs=brow_tm[0:1, col0:col0 + ncols], start=False, stop=True),
              (B_btm, B_ones), (Bb_,))
            return bk_, Bb_

        if carry:
            npre = NPRE_ST if nst == NST else 2
            win_v = win_d.rearrange("(k p) n -> p k n", p=128)
            wpu = ab[:, 0:8192].rearrange("p (k n) -> p k n", k=16)
            Bwpu = Buf("wpu")
            for cb in all_cells:
                Bwpu.w.update(cb.w); Bwpu.r.update(cb.r)
            hTf = hT[:, :, :].bitcast(F32).rearrange("p k n -> p (k n)").rearrange("p (k n) -> p k n", k=8)
            shrep = xt[:, :].rearrange("p (k m) -> p k m", k=16)
            V(lambda e: e.tensor_copy(out=shrep, in_=modfm[:, 0:16].unsqueeze(2).to_broadcast([128, 16, 128])), (B_modfm,), (B_xt,))
            DMA(lambda e: e.dma_start(out=u0B[:, :], in_=bin_d[0:1, 3456:3968].partition_broadcast(128)), (), (B_u0B,))
            ub, Bub = nb()
            for pc in range(2):
                DMA(lambda e: e.dma_start(out=hTf, in_=win_v[:, 8 * pc:8 * pc + 8, 3456:3968]), (), tuple(B_hT))
                for kk in range(8):
                    k = 8 * pc + kk
                    if k % 2 == 0:
                        V(lambda e: e.tensor_scalar(out=wpu[:, k, :], in0=hTf[:, kk, :], scalar1=gsfm[:, k:k + 1], scalar2=None, op0=ALU.mult),
                          tuple(B_hT) + (B_gsfm,), (Bwpu,))
                    else:
                        A(lambda e: e.activation(out=wpu[:, k, :], in_=hTf[:, kk, :], func=AF.Identity, scale=gsfm[:, k:k + 1]),
                          tuple(B_hT) + (B_gsfm,), (Bwpu,))
                    T(lambda e: e.matmul(ub[:, :], lhsT=shrep[:, k, :], rhs=hTf[:, kk, :], start=(k == 0), stop=(k == 15)),
                      (B_xt,) + tuple(B_hT), (Bub,))
            V(lambda e: e.tensor_tensor(out=u0B[:, :], in0=ub[:, :], in1=u0B[:, :], op=ALU.add), (Bub, B_u0B), (B_u0B,))
            Hr, Hi = hloc[:, 0, :], hloc[:, 1, :]
            c1, c2, c3, c4 = hloc[:, 2, :], hloc[:, 3, :], hloc[:, 4, :], hloc[:, 5, :]
            a128r, a128i = par2[:, 32:48], par2[:, 48:64]
            V(lambda e: e.memset(hloc[:, :, :], 0.0), (), (B_hloc,))
            B_SK = tuple(B_attn)
            SK4r = attn_f[:, 0:64].rearrange("p (q k) -> p q k", q=16)
            SK4i = attn_f[:, 64:128].rearrange("p (q k) -> p q k", q=16)
            xtA = [xt[:, :], arena[:, 4096:6144]]
            B_xtA = [B_xt, Buf("xa1")]
            xnA = [xn[:, :], ab[:, 12288:14336], ab[:, 14336:16384]]
            B_xnA = [B_xn, Buf("xna1"), Buf("xna2")]
            for bb in B_xtA[1:] + B_xnA[1:]:
                for cb in all_cells:
                    bb.w.update(cb.w); bb.r.update(cb.r)
            def stA(n):
                row0 = n * 128
                sl, t = n % 3, n % ST
                xts, Bxts, xns, Bxns = xtA[n % 2], B_xtA[n % 2], xnA[sl], B_xnA[sl]
                DMA(lambda e: e.dma_start(out=xts, in_=xprev_d[row0:row0 + 128, :]), (), (Bxts,))
                A(lambda e: e.activation(out=xns, in_=xts, func=AF.Square, accum_out=ssA[:, sl:sl + 1]), (Bxts,), (Bxns, B_ssA[sl]))
                rstd_from(ssA[:, sl:sl + 1], rsA[:, t:t + 1], float(D), (B_ssA[sl],), (B_rsA[t],))
                A(lambda e: e.activation(out=xns, in_=xts, func=AF.Identity, scale=1.0), (Bxts,), (Bxns,))

            def stB(n):
                sl, t = n % 3, n % ST
                xns, Bxns = xnA[sl], B_xnA[sl]
                for half in range(2):
                    bk_, Bb_ = nb()
                    bkv = bk_[:, :].bitcast(BF16)
                    for j in range(8):
                        k = half * 8 + j
                        T(lambda e: e.transpose(bkv[:, j * 128:(j + 1) * 128], xns[:, k * 128:(k + 1) * 128], ident_b[:, :]), (Bxns, B_idb), (Bb_,))
                    V(lambda e: e.tensor_copy(out=hT[:, half * 8:(half + 1) * 8, t * 128:(t + 1) * 128],
                                              in_=bkv.rearrange("p (j m) -> p j m", j=8)), (Bb_,), (B_hT[t],))

            def stC(n):
                t = n % ST
                bk_, Bb_ = nb()
                for k in range(16):
                    T(lambda e: e.matmul(bk_[:, :], lhsT=hT[:, k, t * 128:(t + 1) * 128], rhs=wpu[:, k, :], start=(k == 0), stop=(k == 15)),
                      (Bwpu, B_hT[t]), (Bb_,))
                V(lambda e: e.scalar_tensor_tensor(out=uTM[:, t, :], in0=bk_[:, :], scalar=rsA[:, t:t + 1], in1=u0B[:, :],
                                                   op0=ALU.mult, op1=ALU.add), (Bb_, B_rsA[t], B_u0B), (B_mixed,))

            ntile = npre * ST
            for it in range(ntile + 2):
                if it < ntile:
                    stA(it)
                if 0 <= it - 1 < ntile:
                    stB(it - 1)
                if not (0 <= it - 2 < ntile):
                    continue
                stC(it - 2)
                if (it - 2) % ST != ST - 1:
                    continue
                s = (it - 2) // ST
                for hb in range(2):
                    vb = [nb(), nb()]
                    for qq in range(8):
                        q = hb * 8 + qq
                        for gl in range(2):
                            g = 2 * q + gl
                            for part in range(2):
                                T((lambda vb, qq, q, gl, g, part: lambda e: e.matmul(
                                    vb[part][0][64 * gl:64 * gl + 64, qq * 64:(qq + 1) * 64],
                                    lhsT=decT[:, q, part, 64 * gl:64 * gl + 64], rhs=uTM[:, :, g * 16:(g + 1) * 16],
                                    start=True, stop=True))(vb, qq, q, gl, g, part), (B_decT, B_mixed), (vb[part][1],))
                    vr4 = vb[0][0][:, :].rearrange("p (q k c) -> p q k c", q=8, k=4)
                    vi4 = vb[1][0][:, :].rearrange("p (q k c) -> p q k c", q=8, k=4)
                    bbr = bbar[:, 0, hb * 8:hb * 8 + 8, :].unsqueeze(2).to_broadcast([128, 8, 4, 16])
                    bbi = bbar[:, 1, hb * 8:hb * 8 + 8, :].unsqueeze(2).to_broadcast([128, 8, 4, 16])
                    v4 = lambda tl: tl[:, :].rearrange("p (q k c) -> p q k c", q=8, k=4)
                    V((lambda vr4, bbr: lambda e: e.tensor_tensor(out=v4(st1), in0=vr4, in1=bbr, op=ALU.mult))(vr4, bbr), (vb[0][1], B_bbar), (B_st[0],))
                    V((lambda vi4, bbi: lambda e: e.tensor_tensor(out=v4(st2), in0=vi4, in1=bbi, op=ALU.mult))(vi4, bbi), (vb[1][1], B_bbar), (B_st[1],))
                    G(lambda e: e.tensor_tensor(out=st1[:, :], in0=st1[:, :], in1=st2[:, :], op=ALU.subtract), (B_st[0], B_st[1]), (B_st[0],))
                    V((lambda hb: lambda e: e.tensor_reduce(out=SK4r[:, hb * 8:hb * 8 + 8, :], in_=v4(st1), axis=AX.X, op=ALU.add))(hb), (B_st[0],), B_SK)
                    V((lambda vi4, bbr: lambda e: e.tensor_tensor(out=v4(st3), in0=vi4, in1=bbr, op=ALU.mult))(vi4, bbr), (vb[1][1], B_bbar), (B_st[2],))
                    V((lambda vr4, bbi: lambda e: e.tensor_tensor(out=v4(st4), in0=vr4, in1=bbi, op=ALU.mult))(vr4, bbi), (vb[0][1], B_bbar), (B_st[3],))
                    G(lambda e: e.tensor_tensor(out=st3[:, :], in0=st3[:, :], in1=st4[:, :], op=ALU.add), (B_st[2], B_st[3]), (B_st[2],))
                    V((lambda hb: lambda e: e.tensor_reduce(out=SK4i[:, hb * 8:hb * 8 + 8, :], in_=v4(st3), axis=AX.X, op=ALU.add))(hb), (B_st[2],), B_SK)
                for kk in range(ST):
                    ch = s * ST + kk
                    V(lambda e: e.tensor_tensor(out=c1, in0=Hr, in1=a128r, op=ALU.mult), (B_hloc, B_par2), (B_hloc,))
                    V(lambda e: e.tensor_tensor(out=c2, in0=Hi, in1=a128i, op=ALU.mult), (B_hloc, B_par2), (B_hloc,))
                    V(lambda e: e.tensor_tensor(out=c3, in0=Hi, in1=a128r, op=ALU.mult), (B_hloc, B_par2), (B_hloc,))
                    V(lambda e: e.tensor_tensor(out=c4, in0=Hr, in1=a128i, op=ALU.mult), (B_hloc, B_par2), (B_hloc,))
                    V(lambda e: e.tensor_tensor(out=c1, in0=c1, in1=c2, op=ALU.subtract), (B_hloc,), (B_hloc,))
                    V(lambda e: e.tensor_tensor(out=c3, in0=c3, in1=c4, op=ALU.add), (B_hloc,), (B_hloc,))
                    V((lambda kk, ch: lambda e: e.scalar_tensor_tensor(out=Hr, in0=SK4r[:, :, kk], scalar=cmask[:, ch:ch + 1], in1=c1,
                                                                       op0=ALU.mult, op1=ALU.add))(kk, ch), (B_hloc, B_cmask) + B_SK, (B_hloc,))
                    V((lambda kk, ch: lambda e: e.scalar_tensor_tensor(out=Hi, in0=SK4i[:, :, kk], scalar=cmask[:, ch:ch + 1], in1=c3,
                                                                       op0=ALU.mult, op1=ALU.add))(kk, ch), (B_hloc, B_cmask) + B_SK, (B_hloc,))
            for cb in all_cells:
                for bb in B_xtA[1:] + B_xnA[1:] + [Bwpu]:
                    cb.w.update(bb.w); cb.r.update(bb.r)
            for bb in (B_gr, B_gi, B_hr, B_hi, B_g1):
                bb.w = dict(B_decT.w); bb.r = dict(B_decT.r)
            w4 = sm[:, 32:64]
            sth_, cth_ = par2[:, 0:16], par2[:, 16:32]
            V(lambda e: e.tensor_tensor(out=w4[:, 0:16], in0=Hr, in1=cth_, op=ALU.mult), (B_hloc, B_par2), (B_sm,))
            V(lambda e: e.tensor_tensor(out=w4[:, 16:32], in0=Hi, in1=sth_, op=ALU.mult), (B_hloc, B_par2), (B_sm,))
            V(lambda e: e.tensor_tensor(out=gin_r[:, :], in0=w4[:, 0:16], in1=w4[:, 16:32], op=ALU.subtract), (B_sm,), tuple(B_gin))
            V(lambda e: e.tensor_tensor(out=w4[:, 0:16], in0=Hi, in1=cth_, op=ALU.mult), (B_hloc, B_par2), (B_sm,))
            V(lambda e: e.tensor_tensor(out=w4[:, 16:32], in0=Hr, in1=sth_, op=ALU.mult), (B_hloc, B_par2), (B_sm,))
            V(lambda e: e.tensor_tensor(out=gin_i[:, :], in0=w4[:, 0:16], in1=w4[:, 16:32], op=ALU.add), (B_sm,), tuple(B_gin))
        else:
            V(lambda e: e.memset(gin_r[:, :], 0.0), (), tuple(B_gin))
            V(lambda e: e.memset(gin_i[:, :], 0.0), (), tuple(B_gin))

        for bb in (B_mixA, B_mixS):
            bb.w = dict(B_mixed.w); bb.r = dict(B_mixed.r)

        load_norm_transpose(0, 0)
        for g in range(3):
            w, Bw = load_w(wfm_v[:, :, 1536 + 128 * g:1664 + 128 * g], 128, B_wfm)
            fm_chunk(w, Bw, 0, 1536 + 128 * g, 128, 0, KT[:, g, 0:128], B_KT[g], (B_hT[0],))
        w, Bw = load_w(wtm_v[:, :, 0:192], 192, B_wtm)
        bk_, Bb_ = tm_unit(w, Bw, 192, 0, 0, B_hT[0])
        V((lambda bk_: lambda e: e.tensor_copy(out=Vt[:, 0, :, 0:64], in_=bk_[:, 0:192].rearrange("p (g d) -> p g d", g=3)))(bk_),
          (Bb_,), (B_Vt[0],))

        stsets = [[st1, st2, st3, st4], [st5, st6, st7, st8]]
        Bstsets = [B_st, B_stb]
        gsets = [[gr_, gi_, hr_, hi_, p24, ctmp[:, 0:4, :]],
                 [spool2[:, 0:512].rearrange("p (a b) -> p a b", a=4), spool2[:, 512:1024].rearrange("p (a b) -> p a b", a=4),
                  spool2[:, 1024:1280].bitcast(BF16).rearrange("p (a b) -> p a b", a=4),
                  spool2[:, 1280:1536].bitcast(BF16).rearrange("p (a b) -> p a b", a=4), p24b, ctmp[:, 4:8, :]]]
        Bgsets = [[B_gr, B_gi, B_hr, B_hi, B_p24, B_ctmp], [Buf("gr2"), Buf("gi2"), Buf("hr2"), Buf("hi2"), [Buf("p24c"), Buf("p24d")], Buf("ctmp2")]]

        for s in range(nst):
            for t in range(ST):
                load_norm_transpose(128 + (s * ST + t) * 128, t)
            fm_units = [(c0, min(256, 2432 - c0)) for c0 in range(0, 2432, 256)]
            for (c0, ncols) in fm_units:
                w, Bw = load_w(wfm_v[:, :, c0:c0 + ncols], ncols, B_wfm)
                for cc in range(ncols // 128):
                    ch = c0 // 128 + cc
                    if ch < 12:
                        dst, db = QT[:, ch, :], B_QT[ch]
                    elif ch < 15:
                        dst, db = KT[:, ch - 12, 128:640], B_KT[ch - 12]
                    else:
                        dst, db = UT[:, ch - 15, :], B_UT[ch - 15]
                    fm_chunk(w, Bw, cc, c0 + cc * 128, 512, 0, dst, db, B_hT)
            tm_units = [(0, 192)] + [(192 + 256 * i, 256) for i in range(8)]
            for ui, (c0, ncols) in enumerate(tm_units):
                w, Bw = load_w(wtm_v[:, :, c0:c0 + ncols], ncols, B_wtm)
                for t in range(ST):
                    bk_, Bb_ = tm_unit(w, Bw, ncols, c0, t, B_hT[t])
                    if ui == 0:
                        V((lambda bk_, t: lambda e: e.tensor_copy(out=Vt[:, t + 1, :, 0:64],
                                                                  in_=bk_[:, 0:192].rearrange("p (g d) -> p g d", g=3)))(bk_, t),
                          (Bb_,), (B_Vt[t + 1],))
                    elif c0 < 1728:
                        zc0 = c0 - 192
                        A((lambda bk_, t, zc0: lambda e: e.activation(out=za[:, t, zc0:zc0 + 256], in_=bk_[:, 0:256], func=AF.Silu))(bk_, t, zc0),
                          (Bb_,), (B_za[t],))
                    else:
                        zc0 = c0 - 1728
                        A((lambda bk_, t, zc0: lambda e: e.activation(out=zs[:, t, zc0:zc0 + 256], in_=bk_[:, 0:256], func=AF.Silu))(bk_, t, zc0),
                          (Bb_,), (B_zs[t],))

            sst, ast = {}, {}

            def AA(j):
                t, g = divmod(j, 3)
                gt = s * ST + t
                pts = {}
                for ci, (kc0, Mx, BMx) in enumerate(((t * 128, Mp0 if gt == 0 else Mp, B_Mp0 if gt == 0 else B_Mp),
                                                     ((t + 1) * 128, Mc, B_Mc))):
                    for base in (0, 64):
                        idx = (j % 2) * 4 + ci * 2 + base // 64
                        bk_, Bb_ = nb()
                        T(lambda e: e.matmul(bk_[:, 0:512], lhsT=KT[base:base + 64, g, kc0:kc0 + 128],
                                             rhs=QT[base:base + 64, 4 * g:4 * g + 4, t * 128:(t + 1) * 128], start=True, stop=False),
                          (B_KT[g],) + tuple(B_QT[4 * g:4 * g + 4]), (Bb_,))
                        T(lambda e: e.matmul(bk_[:, 0:512], lhsT=ident_b[:, :], rhs=Mx[:, :], start=False, stop=True), (B_idb, BMx), (Bb_,))
                        A(lambda e: e.activation(out=Eb[idx][:, :], in_=bk_[:, 0:512], func=AF.Exp, scale=0.125), (Bb_,), (B_Eb[idx],))
                        pts[(ci, base)] = idx
                ast[j] = pts

            def AB(j):
                t, g = divmod(j, 3)
                pts = ast.pop(j)
                bo = [nb(), nb()]
                for hl in range(8):
                    jj, base = hl // 2, 64 * (hl % 2)
                    ob, Bob = bo[hl // 4]
                    oc = (hl % 4) * 65
                    for ci in range(2):
                        pidx = pts[(ci, base)]
                        vslot = t + ci
                        T(lambda e: e.matmul(ob[:, oc:oc + 65], lhsT=Eb[pidx][:, jj * 128:(jj + 1) * 128], rhs=Vt[:, vslot, g, :],
                                             start=(ci == 0), stop=(ci == 1)), (B_Eb[pidx], B_Vt[vslot]), (Bob,))
                for half in range(2):
                    ob, Bob = bo[half]
                    ov = ob[:, 0:260].rearrange("p (h d) -> p h d", h=4)
                    hs = slice(half * 4, half * 4 + 4)
                    V(lambda e: e.tensor_tensor(out=den[:, g, hs], in0=ov[:, :, 64], in1=esink[:, 8 * g + 4 * half:8 * g + 4 * half + 4], op=ALU.add),
                      (Bob, B_esink), (B_den[g],))
                    V(lambda e: e.reciprocal(out=rden[:, g, hs], in_=den[:, g, hs]), (B_den[g],), (B_rden[g],))
                    a0 = g * 512 + half * 256
                    V(lambda e: e.tensor_tensor(out=attn_f[:, a0:a0 + 256].rearrange("p (h d) -> p h d", h=4), in0=ov[:, :, 0:64],
                                                in1=rden[:, g, hs].unsqueeze(2).to_broadcast([128, 4, 64]), op=ALU.mult),
                      (Bob, B_rden[g]), (B_attn[g],))
                if g == 2:
                    A(lambda e: e.activation(out=junk[:, 0:1536], in_=attn_f[:, :], func=AF.Square, accum_out=stat[:, 2:3]),
                      tuple(B_attn), (B_junk, B_stat[2]))
                    rstd_from(stat[:, 2:3], stat[:, 3:4], 1536.0, (B_stat[2],), (B_stat[3],))
                    V(lambda e: e.scalar_tensor_tensor(out=mixed[:, 0:1536], in0=attn_f[:, :], scalar=stat[:, 3:4], in1=za[:, t, :],
                                                       op0=ALU.mult, op1=ALU.mult), tuple(B_attn) + (B_stat[3], B_za[t]), (B_mixA,))

            def sbufs(qi):
                par = qi % 2
                return stsets[par], Bstsets[par], gsets[par], Bgsets[par]

            def SA(qi):
                t, Q = divmod(qi, 4)
                pr, Bpr = banks[5], B_bank[5]
                pi_, Bpi = banks[6], B_bank[6]
                for i in range(4):
                    T(lambda e: e.matmul(pr[:, i * 128:(i + 1) * 128], lhsT=BL[:, 4 * Q + i, :], rhs=UT[:, Q, t * 128:(t + 1) * 128],
                                         start=True, stop=True), (B_BL, B_UT[Q]), (Bpr,))
                for i in range(4):
                    T(lambda e: e.matmul(pi_[:, i * 128:(i + 1) * 128], lhsT=BL[:, 16 + 4 * Q + i, :], rhs=UT[:, Q, t * 128:(t + 1) * 128],
                                         start=True, stop=True), (B_BL, B_UT[Q]), (Bpi,))
                sst[qi] = (pr, Bpr, pi_, Bpi)

            def SB(qi):
                t, Q = divmod(qi, 4)
                (s1, s2, s3, s4), Bs, (gr, gi, hr, hi, pq, ct_), (Bgr, Bgi, Bhr, Bhi, Bpq, Bct) = sbufs(qi)
                pr, Bpr, pi_, Bpi = sst.pop(qi)
                ctq = CT[:, 4 * Q:4 * Q + 4, :].rearrange("p a b -> p (a b)")
                snq = SN[:, 4 * Q:4 * Q + 4, :].rearrange("p a b -> p (a b)")
                V(lambda e: e.tensor_tensor(out=s1[:, :], in0=pr[:, :], in1=ctq, op=ALU.mult), (Bpr, B_tab), (Bs[0],))
                V(lambda e: e.tensor_tensor(out=s2[:, :], in0=pi_[:, :], in1=snq, op=ALU.mult), (Bpi, B_tab), (Bs[1],))
                V(lambda e: e.tensor_tensor(out=s3[:, :], in0=pi_[:, :], in1=ctq, op=ALU.mult), (Bpi, B_tab), (Bs[2],))
                V(lambda e: e.tensor_tensor(out=s4[:, :], in0=pr[:, :], in1=snq, op=ALU.mult), (Bpr, B_tab), (Bs[3],))
                V(lambda e: e.tensor_tensor(out=s1[:, :], in0=s1[:, :], in1=s2[:, :], op=ALU.add), (Bs[0], Bs[1]), (Bs[0],))
                V(lambda e: e.tensor_tensor(out=s3[:, :], in0=s3[:, :], in1=s4[:, :], op=ALU.subtract), (Bs[2], Bs[3]), (Bs[2],))
                for i in range(4):
                    q = 4 * Q + i
                    V(lambda e: e.tensor_tensor_scan(out=gr[:, i, :], data0=rho[:, q:q + 1].to_broadcast([128, 128]),
                                                     data1=s1[:, i * 128:(i + 1) * 128], initial=gin_r[:, q:q + 1], op0=ALU.mult, op1=ALU.add),
                      (B_par, Bs[0], B_gin[Q]), (Bgr,))
                    V(lambda e: e.tensor_tensor_scan(out=gi[:, i, :], data0=rho[:, q:q + 1].to_broadcast([128, 128]),
                                                     data1=s3[:, i * 128:(i + 1) * 128], initial=gin_i[:, q:q + 1], op0=ALU.mult, op1=ALU.add),
                      (B_par, Bs[2], B_gin[Q]), (Bgi,))

            def SC(qi):
                t, Q = divmod(qi, 4)
                _, _, (gr, gi, hr, hi, pq, ct_), (Bgr, Bgi, Bhr, Bhi, Bpq, Bct) = sbufs(qi)
                ctq = CT[:, 4 * Q:4 * Q + 4, :].rearrange("p a b -> p (a b)")
                snq = SN[:, 4 * Q:4 * Q + 4, :].rearrange("p a b -> p (a b)")
                qs = slice(4 * Q, 4 * Q + 4)
                glr = gr[:, :, 127]; gli = gi[:, :, 127]
                G(lambda e: e.tensor_tensor(out=ct_[:, 0, :], in0=glr, in1=c128[:, qs], op=ALU.mult), (Bgr, B_par), (Bct,))
                G(lambda e: e.tensor_tensor(out=ct_[:, 1, :], in0=gli, in1=s128[:, qs], op=ALU.mult), (Bgi, B_par), (Bct,))
                G(lambda e: e.tensor_tensor(out=ct_[:, 2, :], in0=gli, in1=c128[:, qs], op=ALU.mult), (Bgi, B_par), (Bct,))
                G(lambda e: e.tensor_tensor(out=ct_[:, 3, :], in0=glr, in1=s128[:, qs], op=ALU.mult), (Bgr, B_par), (Bct,))
                G(lambda e: e.tensor_tensor(out=gin_r[:, qs], in0=ct_[:, 0, :], in1=ct_[:, 1, :], op=ALU.subtract), (Bct,), (B_gin[Q],))
                G(lambda e: e.tensor_tensor(out=gin_i[:, qs], in0=ct_[:, 2, :], in1=ct_[:, 3, :], op=ALU.add), (Bct,), (B_gin[Q],))
                grf = gr.rearrange("p a b -> p (a b)"); gif = gi.rearrange("p a b -> p (a b)")
                G(lambda e: e.tensor_tensor(out=hr.rearrange("p a b -> p (a b)"), in0=grf, in1=ctq, op=ALU.mult), (Bgr, B_tab), (Bhr,))
                G(lambda e: e.tensor_tensor(out=pq[:, 0, :], in0=gif, in1=snq, op=ALU.mult), (Bgi, B_tab), (Bpq[0],))
                G(lambda e: e.tensor_tensor(out=hi.rearrange("p a b -> p (a b)"), in0=gif, in1=ctq, op=ALU.mult), (Bgi, B_tab), (Bhi,))
                G(lambda e: e.tensor_tensor(out=pq[:, 1, :], in0=grf, in1=snq, op=ALU.mult), (Bgr, B_tab), (Bpq[1],))

            def SD(qi):
                t, Q = divmod(qi, 4)
                _, _, (gr, gi, hr, hi, pq, ct_), (Bgr, Bgi, Bhr, Bhi, Bpq, Bct) = sbufs(qi)
                ybk, Bybk = banks[7], B_bank[7]
                for i in range(4):
                    q = 4 * Q + i
                    yc = q * 32
                    T(lambda e: e.matmul(ybk[:, yc:yc + 32], lhsT=hr[:, i, :], rhs=Cm[:, q, :], start=True, stop=False), (Bhr, B_Cm), (Bybk,))
                    T(lambda e: e.matmul(ybk[:, yc:yc + 32], lhsT=pq[:, 0, i * 128:(i + 1) * 128], rhs=Cm[:, 32 + q, :], start=False, stop=False),
                      (Bpq[0], B_Cm), (Bybk,))
                    T(lambda e: e.matmul(ybk[:, yc:yc + 32], lhsT=hi[:, i, :], rhs=Cm[:, 16 + q, :], start=False, stop=False), (Bhi, B_Cm), (Bybk,))
                    T(lambda e: e.matmul(ybk[:, yc:yc + 32], lhsT=pq[:, 1, i * 128:(i + 1) * 128], rhs=Cm[:, 16 + q, :], start=False, stop=False),
                      (Bpq[1], B_Cm), (Bybk,))
                    T(lambda e: e.matmul(ybk[:, yc:yc + 32], lhsT=UT[:, Q, t * 128:(t + 1) * 128], rhs=Dd[:, Q, i * 32:(i + 1) * 32],
                                         start=False, stop=True), (B_UT[Q], B_Dd), (Bybk,))
                if Q != 3:
                    return
                A(lambda e: e.activation(out=g1, in_=ybk[:, :], func=AF.Gelu), (Bybk,), (B_g1,))
                V(lambda e: e.tensor_copy(out=g1b[:, :], in_=g1), (B_g1,), (B_g1b,))
                tb_, Btb_ = nb()
                tbv = tb_[:, :].bitcast(BF16)
                for k in range(4):
                    T(lambda e: e.transpose(tbv[:, k * 128:(k + 1) * 128], g1b[:, k * 128:(k + 1) * 128], ident_b[:, :]), (B_g1b, B_idb), (Btb_,))
                V(lambda e: e.tensor_copy(out=g1T[:, :, :].rearrange("p a b -> p (a b)"), in_=tbv[:, 0:512]), (Btb_,), (B_g1T,))
                gbk, Bgbk = nb()
                for k in range(4):
                    T(lambda e: e.matmul(gbk[:, :], lhsT=g1T[:, k, :], rhs=wglu[:, k, :], start=(k == 0), stop=False), (B_g1T, B_wglu), (Bgbk,))
                T(lambda e: e.matmul(gbk[:, :], lhsT=ones_b[0:1, 0:128], rhs=brow_glu[0:1, :], start=False, stop=True), (B_ones, B_bglu), (Bgbk,))
                A(lambda e: e.activation(out=sg[:, :], in_=gbk[:, :], func=AF.Sigmoid), (Bgbk,), (B_sg,))
                V(lambda e: e.tensor_tensor(out=s2[:, :], in0=g1, in1=sg[:, :], op=ALU.mult), (B_g1, B_sg), (B_s2,))
                A(lambda e: e.activation(out=junk[:, 1536:2048], in_=s2[:, :], func=AF.Square, accum_out=stat[:, 4:5]), (B_s2,), (B_junk, B_stat[4]))
                rstd_from(stat[:, 4:5], stat[:, 5:6], 512.0, (B_stat[4],), (B_stat[5],))
                V(lambda e: e.scalar_tensor_tensor(out=mixed[:, 1536:2048], in0=s2[:, :], scalar=stat[:, 5:6], in1=zs[:, t, :],
                                                   op0=ALU.mult, op1=ALU.mult), (B_s2, B_stat[5], B_zs[t]), (B_mixS,))
                for half in range(2):
                    bk_, Bb_ = nb()
                    bkv = bk_[:, :].bitcast(BF16)
                    for j in range(8):
                        c = half * 8 + j
                        T(lambda e: e.transpose(bkv[:, j * 128:(j + 1) * 128], mixed[:, c * 128:(c + 1) * 128], ident_b[:, :]),
                          (B_mixA, B_mixS, B_idb), (Bb_,))
                    for j in range(8):
                        c = half * 8 + j
                        if j % 2 == 0:
                            A(lambda e: e.activation(out=hT[:, c, t * 128:(t + 1) * 128], in_=bkv[:, j * 128:(j + 1) * 128],
                                                     func=AF.Identity, scale=vt[:, 48 + c:49 + c]), (Bb_, B_vt), (B_hT[t],))
                        else:
                            V(lambda e: e.tensor_scalar(out=hT[:, c, t * 128:(t + 1) * 128], in0=bkv[:, j * 128:(j + 1) * 128],
                                                        scalar1=vt[:, 48 + c:49 + c], scalar2=None, op0=ALU.mult), (Bb_, B_vt), (B_hT[t],))

            NQ = ST * 4
            pend_ab = None
            bank_pool[0] = [0, 1, 2, 3, 4]
            for k in range(NQ + 3):
                if pend_ab is not None:
                    AB(pend_ab)
                    pend_ab = None
                if k < NQ and k % 4 != 3:
                    j = (k // 4) * 3 + k % 4
                    AA(j)
                    pend_ab = j
                if 0 <= k - 1 < NQ:
                    SB(k - 1)
                if k < NQ:
                    SA(k)
                if 0 <= k - 2 < NQ:
                    SC(k - 2)
                if 0 <= k - 3 < NQ:
                    SD(k - 3)
            bank_pool[0] = list(range(7))

            if s + 1 < nst:
                for g in range(3):
                    G((lambda g: lambda e: e.tensor_copy(out=KT[:, g, 0:128], in_=KT[:, g, 512:640]))(g), (B_KT[g],), (B_KT[g],))
                G(lambda e: e.tensor_copy(out=Vt[:, 0, :, 0:64], in_=Vt[:, 4, :, 0:64]), (B_Vt[4],), (B_Vt[0],))

            for t in range(ST):
                row0 = 128 + (s * ST + t) * 128
                DMA((lambda t, row0: lambda e: e.dma_start(out=xr(t), in_=x_d[row0:row0 + 128, :]))(t, row0), (), tuple(xr_cells(t)))
            for u in range(8):
                w, Bw = load_w(wo_v[:, :, u * 256:(u + 1) * 256], 256, B_wo)
                for t in range(ST):
                    bk_, Bb_ = nb()
                    for c in range(16):
                        T(lambda e: e.matmul(bk_[:, 0:256], lhsT=hT[:, c, t * 128:(t + 1) * 128], rhs=w[:, c, :],
                                             start=(c == 0), stop=(c == 15)), (Bw, B_hT[t]), (Bb_,))
                    ot = (st2, st6)[t % 2][:, 0:256]
                    Bot = (B_st[1], B_stb[1])[t % 2]
                    V(lambda e: e.tensor_tensor(out=ot, in0=bk_[:, 0:256], in1=gateB[:, u * 256:(u + 1) * 256], op=ALU.mult),
                      (Bb_, B_gateB), (Bot,))
                    G(lambda e: e.tensor_tensor(out=xr(t)[:, u * 256:(u + 1) * 256], in0=xr(t)[:, u * 256:(u + 1) * 256], in1=ot, op=ALU.add),
                      (Bot,) + tuple(xr_cells(t)), tuple(xr_cells(t)))
            for t in range(ST):
                gt = s * ST + t
                cl = tuple(xr_cells(t))
                A((lambda t: lambda e: e.activation(out=junk[:, :], in_=xr(t), func=AF.Square, accum_out=stat[:, 6:7]))(t), cl, (B_junk, B_stat[6]))
                rstd_from(stat[:, 6:7], stat[:, 7:8], float(D), (B_stat[6],), (B_stat[7],))
                V((lambda t: lambda e: e.scalar_tensor_tensor(out=xr(t), in0=xr(t), scalar=stat[:, 7:8], in1=fgB[:, :],
                                                              op0=ALU.mult, op1=ALU.mult))(t), cl + (B_stat[7], B_fgB), cl)
                DMA((lambda t, gt: lambda e: e.dma_start(out=out_d[gt * 128:(gt + 1) * 128, :], in_=xr(t)))(t, gt), cl, (), out=True)

        nsem = P.finalize(40)
        sems = {e: [es.enter_context(nc.semaphore("s_%s_%d" % (e, i))) for i in range(nsem[e])] for e in Plan.ENGS}
        sems["dma"] = [es.enter_context(nc.semaphore("s_dma_%d" % i)) for i in range(40)]
        with nc.Block() as block:
            @block.sync
            def _(e): P.run_stream("sync", e, sems)

            @block.scalar
            def _(e): P.run_stream("scalar", e, sems)

            @block.vector
            def _(e): P.run_stream("vector", e, sems)

            @block.gpsimd
            def _(e): P.run_stream("gpsimd", e, sems)

            @block.tensor
            def _(e): P.run_stream("tensor", e, sems)
    return nc


def make_in_maps(inp):
    f = lambda a: np.ascontiguousarray(np.asarray(a, dtype=np.float32))
    x = f(inp["x"]); c = f(inp["c"])
    b_ada = f(inp["b_ada"])[0]
    shared = {
        "w_ada": f(inp["w_ada"])[0], "b_ada": b_ada.reshape(1, -1), "w_in": f(inp["w_in"])[0],
        "b_in": f(inp["b_in"])[0].reshape(1, -1), "w_out": f(inp["w_out"])[0], "glu_w": f(inp["glu_w"])[0],
        "glu_b": f(inp["glu_b"])[0].reshape(1, -1), "final_gain": f(inp["final_gain"]).reshape(1, -1),
        "sinks": f(inp["attn_sinks"])[0].reshape(1, -1),
        "b_re": f(inp["ssm_b_re"])[0], "b_im": f(inp["ssm_b_im"])[0],
        "c_re": f(inp["ssm_c_re"])[0], "c_im": f(inp["ssm_c_im"])[0],
    }
    vecsB = np.concatenate([f(inp["ssm_lambda_re"])[0].reshape(16, 128), f(inp["ssm_lambda_im"])[0].reshape(16, 128),
                            np.repeat(f(inp["ssm_log_step"])[0], 64).reshape(16, 128),
                            shared["b_in"][0, 0:1536].reshape(12, 128),
                            np.stack([np.tile(shared["b_in"][0, 1536 + 64 * g:1600 + 64 * g], 2) for g in range(3)], 0),
                            shared["b_in"][0, 3456:3968].reshape(4, 128)], 0)
    maps = []
    for i in range(NCORES):
        b, j = i // 4, i % 4
        t0 = j * T_CORE
        xs = np.zeros((128 + T_CORE, D), np.float32)
        xs[128:] = x[b, t0:t0 + T_CORE]
        if j > 0:
            xs[:128] = x[b, t0 - 128:t0]
        vecsA = np.concatenate([b_ada[0:4096].reshape(32, 128), f(inp["norm_gain"])[0].reshape(16, 128),
                                f(inp["attn_out_gain"])[0].reshape(12, 128), f(inp["ssm_out_gain"])[0].reshape(4, 128),
                                c[b].reshape(16, 128), f(inp["ssm_d"])[0].reshape(4, 128)], 0)
        flags = np.zeros((128, 4), np.float32)
        flags[:, 0] = 1.0 if j > 0 else 0.0
        npre = NPRE_ST * ST * 128
        xprev = np.zeros((npre, D), np.float32)
        cmask = np.zeros((128, NPRE_ST * ST), np.float32)
        if j > 0:
            xprev[npre - t0:] = x[b, 0:t0]
            cmask[:, (npre - t0) // 128:] = 1.0
        m = dict(shared)
        m.update({"x": xs, "vecsA": np.ascontiguousarray(vecsA), "vecsB": np.ascontiguousarray(vecsB), "flags": flags, "xprev": xprev, "cmask": cmask})
        maps.append(m)
    return maps


def kernel(**inputs):
    nc = build_program()
    maps = make_in_maps(inputs)
    res = run_bass_kernel_spmd(nc, maps, core_ids=list(range(NCORES)))
    out = np.zeros((2, 4 * T_CORE, D), np.float32)
    for i in range(NCORES):
        b, j = i // 4, i % 4
        out[b, j * T_CORE:(j + 1) * T_CORE] = res.results[i]["out"]
    return out
```

```python
import math
from contextlib import ExitStack
import numpy as np
import concourse.bass as bass
import concourse.mybir as mybir
from concourse.bass_utils import run_bass_kernel_spmd

F32 = mybir.dt.float32
BF16 = mybir.dt.bfloat16
I32 = mybir.dt.int32
AF = mybir.ActivationFunctionType
ALU = mybir.AluOpType
AX = mybir.AxisListType

NCORES = 8
D = 2048
T_CORE = 4096
NT = 32
ST = 4
NST = NT // ST
NPRE_ST = 24
EPS = 1e-5
SEM_CAP = 20000
TWO_PI = 2.0 * math.pi


class Buf:
    __slots__ = ("name", "w", "r")

    def __init__(self, name):
        self.name = name
        self.w = {}
        self.r = {}


class _Op:
    __slots__ = ("eng", "fn", "deps", "dma", "sig", "idx", "signal", "presem")

    def __init__(self, eng, fn, dma):
        self.eng = eng
        self.fn = fn
        self.dma = dma
        self.deps = []
        self.sig = None
        self.signal = False
        self.presem = None


class _Rec:
    def __init__(self):
        self.call = None

    def __getattr__(self, name):
        def f(*a, **kw):
            self.call = (name, a, kw)
            return self
        return f


class Plan:
    ENGS = ("sync", "scalar", "vector", "gpsimd", "tensor")

    def __init__(self):
        self.streams = {e: [] for e in self.ENGS}
        self.all_ops = []
        self.out_dmas = []

    def op(self, eng, fn, reads=(), writes=(), dma=False, out=False):
        rec = _Rec()
        fn(rec)
        o = _Op(eng, rec.call, dma)
        o.idx = len(self.all_ops)
        deps = set()
        for b in reads:
            deps.update(b.w.values())
        for b in writes:
            deps.update(b.w.values())
            deps.update(b.r.values())
        key = ("dma", o.idx) if dma else eng
        fdeps = []
        for d in deps:
            dop = self.all_ops[d]
            if (not dop.dma) and dop.eng == eng and (not dma) and eng == "tensor":
                continue
            fdeps.append(d)
        o.deps = fdeps
        for d in fdeps:
            self.all_ops[d].signal = True
        for b in reads:
            b.r[key] = o.idx
        for b in writes:
            b.w = {key: o.idx}
            b.r = {}
        self.streams[eng].append(o)
        self.all_ops.append(o)
        if out:
            self.out_dmas.append(o.idx)
        return o

    def finalize(self, n_dma_sems):
        if self.out_dmas:
            fin = _Op("sync", None, False)
            fin.idx = len(self.all_ops)
            fin.deps = list(self.out_dmas)
            for d in fin.deps:
                self.all_ops[d].signal = True
            self.streams["sync"].append(fin)
            self.all_ops.append(fin)
        cnt = {e: [0, 0] for e in self.ENGS}
        dma_vals = [0] * n_dma_sems
        dma_used = [False] * n_dma_sems
        rr = 0
        for o in self.all_ops:
            if o.dma:
                s = rr
                rr = (rr + 1) % n_dma_sems
                if dma_used[s]:
                    o.presem = ("dma", s, dma_vals[s])
                dma_vals[s] += 16
                dma_used[s] = True
                o.sig = ("dma", s, dma_vals[s])
            elif o.signal:
                c = cnt[o.eng]
                if c[1] >= SEM_CAP:
                    c[0] += 1
                    c[1] = 0
                c[1] += 1
                o.sig = (o.eng, c[0], c[1])
        return {e: cnt[e][0] + 1 for e in self.ENGS}

    def run_stream(self, eng_name, e, sems):
        seen = {}
        for o in self.streams[eng_name]:
            waits = {}
            if o.presem is not None:
                waits[o.presem[:2]] = o.presem[2]
            for d in o.deps:
                sg = self.all_ops[d].sig
                k = sg[:2]
                if waits.get(k, 0) < sg[2]:
                    waits[k] = sg[2]
            for k, v in waits.items():
                if seen.get(k, 0) >= v:
                    continue
                seen[k] = v
                e.wait_ge(sems[k[0]][k[1]], v)
            if o.fn is None:
                continue
            name, a, kw = o.fn
            ins = getattr(e, name)(*a, **kw)
            if o.sig is not None:
                if o.dma:
                    ins.then_inc(sems["dma"][o.sig[1]], 16)
                else:
                    ins.then_inc(sems[o.sig[0]][o.sig[1]], 1)


def build_program(nst=NST, debug=False, carry=True):
    nc = bass.Bass("TRN2", target_bir_lowering=False)
    P = Plan()
    nrows = 128 + nst * ST * 128

    def din(name, shape, dt=F32):
        return nc.dram_tensor(name, list(shape), dt, kind="ExternalInput").ap()

    x_d = din("x", [128 + T_CORE, D])
    vecsA_d = din("vecsA", [84, 128])
    vecsB_d = din("vecsB", [67, 128])
    wada_d = din("w_ada", [D, 3 * D])
    bada_d = din("b_ada", [1, 3 * D])
    win_d = din("w_in", [D, 4480])
    bin_d = din("b_in", [1, 4480])
    wout_d = din("w_out", [D, D])
    glu_w_d = din("glu_w", [512, 512])
    glu_b_d = din("glu_b", [1, 512])
    fg_d = din("final_gain", [1, D])
    sinks_d = din("sinks", [1, 24])
    bre_d = din("b_re", [32, 64, 16])
    bim_d = din("b_im", [32, 64, 16])
    cre_d = din("c_re", [32, 16, 64])
    cim_d = din("c_im", [32, 16, 64])
    flags_d = din("flags", [128, 4])
    xprev_d = din("xprev", [NPRE_ST * ST * 128, D])
    cmask_d = din("cmask", [128, NPRE_ST * ST])
    out_d = nc.dram_tensor("out", [T_CORE, D], F32, kind="ExternalOutput").ap()
    dbg = {}

    def dout(name, shape):
        dbg[name] = nc.dram_tensor(name, list(shape), F32, kind="ExternalOutput").ap()
        return dbg[name]

    wfm = nc.dram_tensor("wfm", [D, 2432], BF16).ap()
    wtm = nc.dram_tensor("wtm", [D, 2240], BF16).ap()
    wo = nc.dram_tensor("wo", [D, D], BF16).ap()
    B_wfm, B_wtm, B_wo = Buf("wfm"), Buf("wtm"), Buf("wo")

    es = ExitStack()
    with es:
        def sb(name, shape, dt=F32):
            return es.enter_context(nc.sbuf_tensor("sb_" + name, list(shape), dt))

        banks = [es.enter_context(nc.psum_tensor("bank%d" % i, [128, 512], F32)) for i in range(8)]
        B_bank = [Buf("bank%d" % i) for i in range(8)]
        bank_rr = [0]

        bank_pool = [list(range(7))]

        def nb():
            pool = bank_pool[0]
            i = pool[bank_rr[0] % len(pool)]
            bank_rr[0] += 1
            return banks[i], B_bank[i]

        ident_f = sb("ident_f", [128, 128]); B_idf = Buf("idf")
        ident_b = sb("ident_b", [128, 128], BF16); B_idb = Buf("idb")
        ones_b = sb("ones_b", [1, 512], BF16); B_ones = Buf("ones")
        epsc = sb("epsc", [128, 1]); B_epsc = Buf("epsc")
        vt = sb("vt", [128, 84]); B_vt = Buf("vt")
        vtb = sb("vtb", [128, 67]); B_vtb = Buf("vtb")
        vstage = sb("vstage", [84, 128]); B_vst = Buf("vstage")
        vstageb = sb("vstageb", [67, 128]); B_vstb = Buf("vstageb")
        scfm = sb("scfm", [128, 16]); B_scfm = Buf("scfm")
        modfm = sb("modfm", [128, 32]); B_modfm = Buf("modfm")
        gsfm = sb("gsfm", [128, 16]); B_gsfm = Buf("gsfm")
        gateB = sb("gateB", [128, D]); B_gateB = Buf("gateB")
        fgB = sb("fgB", [128, D]); B_fgB = Buf("fgB")
        esink = sb("esink", [128, 24]); B_esink = Buf("esink")
        flags = sb("flags", [128, 4]); B_flags = Buf("flags")
        maskc = sb("maskc", [128, 128], BF16); B_maskc = Buf("maskc")
        maskp = sb("maskp", [128, 128], BF16); B_maskp = Buf("maskp")
        maskp0 = sb("maskp0", [128, 128], BF16); B_maskp0 = Buf("maskp0")
        brow_tm = sb("brow_tm", [1, 2240], BF16); B_btm = Buf("brow_tm")
        brow_glu = sb("brow_glu", [1, 512], BF16); B_bglu = Buf("brow_glu")
        wglu = sb("wglu", [128, 4, 512], BF16); B_wglu = Buf("wglu")
        xt = sb("xt", [128, D]); B_xt = Buf("xt")
        xn = sb("xn", [128, D], BF16); B_xn = Buf("xn")
        junk = xn; B_junk = B_xn
        hT = sb("hT", [128, 16, 512], BF16); B_hT = [Buf("hT%d" % i) for i in range(ST)]
        wsl = [sb("wsl%d" % i, [128, 16, 256], BF16) for i in range(2)]
        B_wsl = [Buf("wsl%d" % i) for i in range(2)]
        wsl_rr = [0]
        arena = sb("arena", [128, 8192])
        ab = arena[:, :].bitcast(BF16)
        QT = ab[:, 0:6144].rearrange("p (c t) -> p c t", c=12)
        za = ab[:, 6144:12288].rearrange("p (t n) -> p t n", t=4)
        UT = ab[:, 12288:14336].rearrange("p (c t) -> p c t", c=4)
        zs = ab[:, 14336:16384].rearrange("p (t n) -> p t n", t=4)
        cells = []
        B_QT = [Buf("QT%d" % c) for c in range(12)]
        B_za = [Buf("za%d" % t) for t in range(4)]
        B_UT = [Buf("UT%d" % c) for c in range(4)]
        B_zs = [Buf("zs%d" % t) for t in range(4)]
        for c in range(12):
            cells.append((c * 1024, (c + 1) * 1024, B_QT[c]))
        for t in range(4):
            cells.append((12288 + t * 3072, 12288 + (t + 1) * 3072, B_za[t]))
        for c in range(4):
            cells.append((24576 + c * 1024, 24576 + (c + 1) * 1024, B_UT[c]))
        for t in range(4):
            cells.append((28672 + t * 1024, 28672 + (t + 1) * 1024, B_zs[t]))

        def xr_cells(t):
            lo, hi = t * 8192, (t + 1) * 8192
            return [b for (l, h, b) in cells if l < hi and h > lo]
        all_cells = [b for (_, _, b) in cells]

        def xr(t):
            return arena[:, t * 2048:(t + 1) * 2048]

        KT = sb("KT", [128, 3, 640], BF16); B_KT = [Buf("KT%d" % g) for g in range(3)]
        Vt = sb("Vt", [128, 5, 3, 65], BF16); B_Vt = [Buf("Vt%d" % i) for i in range(5)]
        Eb = [sb("Eb%d" % i, [128, 512], BF16) for i in range(8)]; B_Eb = [Buf("Eb%d" % i) for i in range(8)]
        Mc = sb("Mc", [128, 512], BF16); Mp = sb("Mp", [128, 512], BF16); Mp0 = sb("Mp0", [128, 512], BF16)
        B_Mc, B_Mp, B_Mp0 = Buf("Mc"), Buf("Mp"), Buf("Mp0")
        attn_f = sb("attn_f", [128, 1536]); B_attn = [Buf("attn%d" % g) for g in range(3)]
        den = sb("den", [128, 3, 8]); B_den = [Buf("den%d" % g) for g in range(3)]
        rden = sb("rden", [128, 3, 8]); B_rden = [Buf("rden%d" % g) for g in range(3)]
        stat = sb("stat", [128, 8]); B_stat = [Buf("stat%d" % i) for i in range(8)]
        mixed = sb("mixed", [128, D], BF16); B_mixed = Buf("mixed"); B_mixA = Buf("mixA"); B_mixS = Buf("mixS")
        CT = sb("CT", [128, 16, 128]); SN = sb("SN", [128, 16, 128]); B_tab = Buf("tab")
        BL = sb("BL", [128, 32, 128], BF16); B_BL = Buf("BL")
        Cm = sb("Cm", [128, 48, 32], BF16); B_Cm = Buf("Cm")
        p24 = sb("p24", [128, 2, 512], BF16); B_p24 = [Buf("p24a"), Buf("p24b")]
        Dd = sb("Dd", [128, 4, 128], BF16); B_Dd = Buf("Dd")
        rho = sb("rho", [128, 16]); c128 = sb("c128", [128, 16]); s128 = sb("s128", [128, 16]); B_par = Buf("par")
        gin_r = sb("gin_r", [128, 16]); gin_i = sb("gin_i", [128, 16]); B_gin = [Buf("gin%d" % q) for q in range(4)]
        st1 = sb("st1", [128, 512]); st2 = sb("st2", [128, 512]); st3 = sb("st3", [128, 512]); st4 = sb("st4", [128, 512])
        B_st = [Buf("st%d" % i) for i in range(4)]
        st5 = sb("st5", [128, 512]); st6 = sb("st6", [128, 512]); st7 = sb("st7", [128, 512]); st8 = sb("st8", [128, 512])
        B_stb = [Buf("stb%d" % i) for i in range(4)]
        spool2 = sb("spool2", [128, 1536])
        p24b = sb("p24b", [128, 2, 512], BF16)
        wr_ = st1; wi_ = st3; B_wr, B_wi = B_st[0], B_st[2]
        spool = sb("spool", [128, 2048])
        gr_ = spool[:, 0:512].rearrange("p (a b) -> p a b", a=4); gi_ = spool[:, 512:1024].rearrange("p (a b) -> p a b", a=4)
        B_gr, B_gi = Buf("gr"), Buf("gi")
        hr_ = spool[:, 1024:1280].bitcast(BF16).rearrange("p (a b) -> p a b", a=4)
        hi_ = spool[:, 1280:1536].bitcast(BF16).rearrange("p (a b) -> p a b", a=4); B_hr, B_hi = Buf("hr"), Buf("hi")
        ctmp = sb("ctmp", [128, 8, 4]); B_ctmp = Buf("ctmp")
        g1 = spool[:, 1536:2048]; B_g1 = Buf("g1")
        g1b = sb("g1b", [128, 512], BF16); B_g1b = Buf("g1b")
        g1T = sb("g1T", [128, 4, 128], BF16); B_g1T = Buf("g1T")
        sg = sb("sg", [128, 512]); B_sg = Buf("sg")
        s2 = sg; B_s2 = B_sg
        otmp = st2; B_otmp = B_st[1]
        sm = sb("sm", [128, 64]); B_sm = Buf("sm")
        par2 = sb("par2", [128, 64]); B_par2 = Buf("par2")
        bbar = sb("bbar", [128, 2, 16, 16]); B_bbar = Buf("bbar")
        decT = spool[:, :].bitcast(BF16).rearrange("p (q i n) -> p q i n", q=16, i=2); B_decT = Buf("decT")
        ssA = sb("ssA", [128, 4]); B_ssA = [Buf("ssA%d" % i) for i in range(3)]
        brow_u = sb("brow_u", [1, 512], BF16); B_bu = Buf("brow_u")
        u0B = sb("u0B", [128, 512]); B_u0B = Buf("u0B")
        rsA = sb("rsA", [128, 4]); B_rsA = [Buf("rsA%d" % i) for i in range(4)]
        cmask = sb("cmask", [128, NPRE_ST * ST]); B_cmask = Buf("cmask")
        hloc = sb("hloc", [128, 8, 16]); B_hloc = Buf("hloc")
        uTM = mixed[:, :].rearrange("p (t n) -> p t n", t=4)
        SKr = attn_f[:, 0:512].rearrange("p (q k) -> p q k", q=16)
        SKi = attn_f[:, 512:1024].rearrange("p (q k) -> p q k", q=16)

        def V(fn, r=(), w=()): return P.op("vector", fn, r, w)
        def A(fn, r=(), w=()): return P.op("scalar", fn, r, w)
        def G(fn, r=(), w=()): return P.op("gpsimd", fn, r, w)
        def T(fn, r=(), w=()): return P.op("tensor", fn, r, w)
        B_swq = Buf("swdge_chain")

        def DMA(fn, r=(), w=(), q="sync", out=False):
            if q == "gpsimd":
                return P.op(q, fn, tuple(r), tuple(w) + (B_swq,), dma=True, out=out)
            return P.op(q, fn, r, w, dma=True, out=out)

        def cast(dst, src, wbuf):
            DMA(lambda e: e.dma_start(out=dst, in_=src), (), (wbuf,), q="gpsimd")
        cast(wfm[:, 0:1536], win_d[:, 0:1536], B_wfm)
        for g in range(3):
            cast(wfm[:, 1536 + 128 * g:1536 + 128 * g + 64], win_d[:, 1536 + 64 * g:1600 + 64 * g], B_wfm)
            cast(wfm[:, 1536 + 128 * g + 64:1536 + 128 * g + 128], win_d[:, 1536 + 64 * g:1600 + 64 * g], B_wfm)
        cast(wfm[:, 1920:2432], win_d[:, 3456:3968], B_wfm)
        cast(wtm[:, 0:192], win_d[:, 1728:1920], B_wtm)
        cast(wtm[:, 192:1728], win_d[:, 1920:3456], B_wtm)
        cast(wtm[:, 1728:2240], win_d[:, 3968:4480], B_wtm)
        cast(wo[:, :], wout_d[:, :], B_wo)
        DMA(lambda e: e.dma_start(out=brow_tm[0:1, 0:192], in_=bin_d[0:1, 1728:1920]), (), (B_btm,), q="gpsimd")
        DMA(lambda e: e.dma_start(out=brow_tm[0:1, 192:1728], in_=bin_d[0:1, 1920:3456]), (), (B_btm,), q="gpsimd")
        DMA(lambda e: e.dma_start(out=brow_tm[0:1, 1728:2240], in_=bin_d[0:1, 3968:4480]), (), (B_btm,), q="gpsimd")
        DMA(lambda e: e.dma_start(out=brow_glu[0:1, :], in_=glu_b_d[0:1, :]), (), (B_bglu,), q="gpsimd")
        DMA(lambda e: e.dma_start(out=brow_u[0:1, :], in_=bin_d[0:1, 3456:3968]), (), (B_bu,), q="gpsimd")
        DMA(lambda e: e.dma_start(out=cmask[:, :], in_=cmask_d), (), (B_cmask,))
        DMA(lambda e: e.dma_start(out=wglu[:, :, :], in_=glu_w_d.rearrange("(k p) n -> p k n", p=128)), (), (B_wglu,), q="gpsimd")

        DMA(lambda e: e.dma_start(out=vstage[:, :], in_=vecsA_d), (), (B_vst,))
        DMA(lambda e: e.dma_start(out=vstageb[:, :], in_=vecsB_d), (), (B_vstb,))
        DMA(lambda e: e.dma_start(out=flags[:, :], in_=flags_d), (), (B_flags,))
        DMA(lambda e: e.dma_start(out=fgB[:, :], in_=fg_d[0:1, :].partition_broadcast(128)), (), (B_fgB,))
        DMA(lambda e: e.dma_start(out=gateB[:, :], in_=bada_d[0:1, 2 * D:3 * D].partition_broadcast(128)), (), (B_gateB,))
        DMA(lambda e: e.dma_start(out=esink[:, :], in_=sinks_d[0:1, :].partition_broadcast(128)), (), (B_esink,))

        G(lambda e: e.memset(ident_f[:, :], 1.0), (), (B_idf,))
        G(lambda e: e.affine_select(out=ident_f[:, :], in_=ident_f[:, :], pattern=[[-1, 128]], compare_op=ALU.is_equal,
                                    fill=0.0, base=0, channel_multiplier=1), (B_idf,), (B_idf,))
        V(lambda e: e.tensor_copy(out=ident_b[:, :], in_=ident_f[:, :]), (B_idf,), (B_idb,))
        V(lambda e: e.memset(ones_b[:, :], 1.0), (), (B_ones,))
        V(lambda e: e.memset(epsc[:, :], EPS), (), (B_epsc,))
        G(lambda e: e.memset(maskc[:, :], 1.0), (), (B_maskc,))
        G(lambda e: e.affine_select(out=maskc[:, :], in_=maskc[:, :], pattern=[[1, 128]], compare_op=ALU.is_ge,
                                    fill=0.0, base=0, channel_multiplier=-1), (B_maskc,), (B_maskc,))
        G(lambda e: e.memset(maskp[:, :], 1.0), (), (B_maskp,))
        G(lambda e: e.affine_select(out=maskp[:, :], in_=maskp[:, :], pattern=[[-1, 128]], compare_op=ALU.is_gt,
                                    fill=0.0, base=0, channel_multiplier=1), (B_maskp,), (B_maskp,))
        V(lambda e: e.tensor_scalar(out=maskp0[:, :], in0=maskp[:, :], scalar1=flags[:, 0:1], scalar2=None, op0=ALU.mult),
          (B_maskp, B_flags), (B_maskp0,))
        A(lambda e: e.activation(out=esink[:, :], in_=esink[:, :], func=AF.Exp), (B_esink,), (B_esink,))
        for (Mx, Bx, m01, Bm) in ((Mc, B_Mc, maskc, B_maskc), (Mp, B_Mp, maskp, B_maskp), (Mp0, B_Mp0, maskp0, B_maskp0)):
            V((lambda Mx, m01: lambda e: e.tensor_scalar(out=Mx[:, :].rearrange("p (j q) -> p j q", j=4),
                                                         in0=m01[:, :].unsqueeze(1).to_broadcast([128, 4, 128]),
                                                         scalar1=-1.0, scalar2=30000.0, op0=ALU.add, op1=ALU.mult))(Mx, m01), (Bm,), (Bx,))
        V(lambda e: e.memset(Vt[:, :, :, 64:65], 1.0), (), tuple(B_Vt))

        bk, Bb = nb()
        T(lambda e: e.transpose(bk[:, 0:84], vstage[:, :], ident_f[0:84, 0:84]), (B_vst, B_idf), (Bb,))
        V(lambda e: e.tensor_copy(out=vt[:, :], in_=bk[:, 0:84]), (Bb,), (B_vt,))
        bk2, Bb2 = nb()
        T(lambda e: e.transpose(bk2[:, 0:67], vstageb[:, :], ident_f[0:67, 0:67]), (B_vstb, B_idf), (Bb2,))
        V(lambda e: e.tensor_copy(out=vtb[:, :], in_=bk2[:, 0:67]), (Bb2,), (B_vtb,))
        A(lambda e: e.activation(out=scfm[:, :], in_=vt[:, 64:80], func=AF.Silu), (B_vt,), (B_scfm,))

        B_zc = Buf("zc")

        def ssm_setup_gen():
            step = sm[:, 0:16]; lrs = sm[:, 16:32]; th = sm[:, 32:48]; tmpa = sm[:, 48:64]
            A(lambda e: e.activation(out=step, in_=vtb[:, 32:48], func=AF.Exp), (B_vtb,), (B_sm,))
            V(lambda e: e.tensor_tensor(out=lrs, in0=vtb[:, 0:16], in1=step, op=ALU.mult), (B_vtb, B_sm), (B_sm,))
            V(lambda e: e.tensor_tensor(out=th, in0=vtb[:, 16:32], in1=step, op=ALU.mult), (B_vtb, B_sm), (B_sm,))
            A(lambda e: e.activation(out=rho[:, :], in_=lrs, func=AF.Exp), (B_sm,), (B_par,))

            def sincos(out_sin, out_cos, ang, tmps, rbufs, wbufs):
                tf, tf2 = tmps
                ti = tf2.bitcast(I32)
                allb = tuple(rbufs) + tuple(wbufs)
                V(lambda e: e.tensor_scalar(out=tf, in0=ang, scalar1=1.0 / TWO_PI, scalar2=None, op0=ALU.mult), allb, wbufs)
                for k, dst in enumerate((out_sin, out_cos)):
                    if k == 1:
                        V(lambda e: e.tensor_scalar(out=tf, in0=tf, scalar1=0.25, scalar2=None, op0=ALU.add), wbufs, wbufs)
                    V(lambda e: e.tensor_copy(out=ti, in_=tf), wbufs, wbufs)
                    V(lambda e: e.tensor_copy(out=tf2, in_=ti), wbufs, wbufs)
                    V(lambda e: e.tensor_tensor(out=tf2, in0=tf, in1=tf2, op=ALU.subtract), wbufs, wbufs)
                    A(lambda e: e.activation(out=dst, in_=tf2, func=AF.Sin, scale=TWO_PI), wbufs, wbufs)

            mixf = mixed[:, :].bitcast(F32)
            sm2 = mixf[:, 256:384]; B_sm2 = B_mixed
            sth = sm2[:, 0:16]; cth = sm2[:, 16:32]; nr = sm2[:, 32:48]; ni = sm2[:, 48:64]
            dn = sm2[:, 64:80]; cfr = sm2[:, 80:96]; cfi = sm2[:, 96:112]; t5 = sm2[:, 112:128]
            sincos(sth, cth, th, (t5, dn), (B_sm,), (B_sm2,))
            V(lambda e: e.tensor_tensor(out=nr, in0=cth, in1=rho[:, :], op=ALU.mult), (B_sm2, B_par), (B_sm2,))
            V(lambda e: e.tensor_scalar(out=nr, in0=nr, scalar1=-1.0, scalar2=None, op0=ALU.add), (B_sm2,), (B_sm2,))
            V(lambda e: e.tensor_tensor(out=ni, in0=sth, in1=rho[:, :], op=ALU.mult), (B_sm2, B_par), (B_sm2,))
            lre = vtb[:, 0:16]; lim = vtb[:, 16:32]
            V(lambda e: e.tensor_tensor(out=dn, in0=lre, in1=lre, op=ALU.mult), (B_vtb,), (B_sm2,))
            V(lambda e: e.tensor_tensor(out=t5, in0=lim, in1=lim, op=ALU.mult), (B_vtb,), (B_sm2,))
            V(lambda e: e.tensor_tensor(out=dn, in0=dn, in1=t5, op=ALU.add), (B_sm2,), (B_sm2,))
            V(lambda e: e.reciprocal(out=dn, in_=dn), (B_sm2,), (B_sm2,))
            V(lambda e: e.tensor_tensor(out=cfr, in0=nr, in1=lre, op=ALU.mult), (B_sm2, B_vtb), (B_sm2,))
            V(lambda e: e.tensor_tensor(out=t5, in0=ni, in1=lim, op=ALU.mult), (B_sm2, B_vtb), (B_sm2,))
            V(lambda e: e.tensor_tensor(out=cfr, in0=cfr, in1=t5, op=ALU.add), (B_sm2,), (B_sm2,))
            V(lambda e: e.tensor_tensor(out=cfr, in0=cfr, in1=dn, op=ALU.mult), (B_sm2,), (B_sm2,))
            V(lambda e: e.tensor_tensor(out=cfi, in0=ni, in1=lre, op=ALU.mult), (B_sm2, B_vtb), (B_sm2,))
            V(lambda e: e.tensor_tensor(out=t5, in0=nr, in1=lim, op=ALU.mult), (B_sm2, B_vtb), (B_sm2,))
            V(lambda e: e.tensor_tensor(out=cfi, in0=cfi, in1=t5, op=ALU.subtract), (B_sm2,), (B_sm2,))
            V(lambda e: e.tensor_tensor(out=cfi, in0=cfi, in1=dn, op=ALU.mult), (B_sm2,), (B_sm2,))
            iot = mixf[:, 0:128]; B_iot = B_mixed
            G(lambda e: e.iota(iot[:, :], pattern=[[1, 128]], base=0, channel_multiplier=0,
                               allow_small_or_imprecise_dtypes=True), (), (B_iot,))
            ang = st1[:, 0:128]; tmpw = st2[:, 0:128]; tmpw2 = st2[:, 128:256]
            for q in range(16):
                V((lambda q: lambda e: e.tensor_scalar(out=ang, in0=iot[:, :], scalar1=th[:, q:q + 1], scalar2=None, op0=ALU.mult))(q),
                  (B_iot, B_sm), (B_st[0],))
                sincos(SN[:, q, :], CT[:, q, :], ang, (tmpw, tmpw2), (B_st[0],), (B_st[1], B_tab))
                yield
            V(lambda e: e.tensor_scalar(out=tmpa, in0=th, scalar1=128.0, scalar2=None, op0=ALU.mult), (B_sm,), (B_sm,))
            sincos(s128[:, :], c128[:, :], tmpa, (t5, nr), (B_sm,), (B_sm2, B_par))
            V(lambda e: e.tensor_copy(out=par2[:, 0:32], in_=sm2[:, 0:32]), (B_sm2,), (B_par2,))
            A(lambda e: e.activation(out=t5, in_=lrs, func=AF.Exp, scale=128.0), (B_sm,), (B_sm2,))
            V(lambda e: e.tensor_tensor(out=par2[:, 32:48], in0=c128[:, :], in1=t5, op=ALU.mult), (B_par, B_sm2), (B_par2,))
            V(lambda e: e.tensor_tensor(out=par2[:, 48:64], in0=s128[:, :], in1=t5, op=ALU.mult), (B_par, B_sm2), (B_par2,))
            iotr = mixf[:, 384:512]
            G(lambda e: e.iota(iotr, pattern=[[-1, 128]], base=127, channel_multiplier=0, allow_small_or_imprecise_dtypes=True), (), (B_iot,))
            for q in range(16):
                V((lambda q: lambda e: e.tensor_scalar(out=ang, in0=iotr, scalar1=th[:, q:q + 1], scalar2=None, op0=ALU.mult))(q),
                  (B_iot, B_sm), (B_st[0],))
                sincos(st3[:, 0:128], st4[:, 0:128], ang, (tmpw, tmpw2), (B_st[0],), (B_st[1], B_st[2], B_st[3]))
                yield
                A((lambda q: lambda e: e.activation(out=st1[:, 128:256], in_=iotr, func=AF.Exp, scale=lrs[:, q:q + 1]))(q), (B_iot, B_sm), (B_st[0],))
                V(lambda e: e.tensor_tensor(out=st4[:, 0:128], in0=st4[:, 0:128], in1=st1[:, 128:256], op=ALU.mult), (B_st[0], B_st[3]), (B_st[3],))
                V(lambda e: e.tensor_tensor(out=st3[:, 0:128], in0=st3[:, 0:128], in1=st1[:, 128:256], op=ALU.mult), (B_st[0], B_st[2]), (B_st[2],))
                for part, srct, Bs in ((0, st4, B_st[3]), (1, st3, B_st[2])):
                    bkq, Bbq = nb()
                    T((lambda bkq, srct: lambda e: e.transpose(bkq[:, 0:128], srct[:, 0:128], ident_f[:, :]))(bkq, srct), (Bs, B_idf), (Bbq,))
                    A((lambda bkq, q, part: lambda e: e.copy(out=decT[:, q, part, :], in_=bkq[:, 0:128]))(bkq, q, part), (Bbq,), (B_decT,))
                    yield

            xnf = xn[:, :].bitcast(F32)
            braw = xnf[:, 0:512].rearrange("p (i q c) -> p i q c", i=2, q=16); B_braw = B_xn
            for i, src in enumerate((bre_d, bim_d)):
                srcv = src.rearrange("(q l) p c -> (l p) q c", l=2)
                DMA((lambda i, srcv: lambda e: e.dma_start(out=braw[:, i, :, :], in_=srcv))(i, srcv), (), (B_braw,))
            cfr_b = cfr.unsqueeze(2).to_broadcast([128, 16, 16]); cfi_b = cfi.unsqueeze(2).to_broadcast([128, 16, 16])
            tb1 = st3[:, 0:256].rearrange("p (q c) -> p q c", q=16); tb2 = st4[:, 0:256].rearrange("p (q c) -> p q c", q=16)
            V(lambda e: e.tensor_tensor(out=tb1, in0=braw[:, 0, :, :], in1=cfr_b, op=ALU.mult), (B_braw, B_sm2), (B_st[2],))
            V(lambda e: e.tensor_tensor(out=tb2, in0=braw[:, 1, :, :], in1=cfi_b, op=ALU.mult), (B_braw, B_sm2), (B_st[3],))
            V(lambda e: e.tensor_tensor(out=bbar[:, 0, :, :], in0=tb1, in1=tb2, op=ALU.subtract), (B_st[2], B_st[3]), (B_bbar,))
            V(lambda e: e.tensor_tensor(out=tb1, in0=braw[:, 1, :, :], in1=cfr_b, op=ALU.mult), (B_braw, B_sm2), (B_st[2],))
            V(lambda e: e.tensor_tensor(out=tb2, in0=braw[:, 0, :, :], in1=cfi_b, op=ALU.mult), (B_braw, B_sm2), (B_st[3],))
            V(lambda e: e.tensor_tensor(out=bbar[:, 1, :, :], in0=tb1, in1=tb2, op=ALU.add), (B_st[2], B_st[3]), (B_bbar,))
            xq = mixf[:, 128:256]; B_xq = B_mixed
            for i in range(2):
                for q in range(16):
                    r0 = ((2 * q) % 8) * 16
                    V(lambda e: e.memset(xq[:, :], 0.0), (), (B_xq,))
                    V((lambda i, q, r0: lambda e: e.tensor_copy(out=xq[0:64, r0:r0 + 16], in_=bbar[0:64, i, q, :]))(i, q, r0), (B_bbar,), (B_xq,))
                    V((lambda i, q, r0: lambda e: e.tensor_copy(out=xq[64:128, r0 + 16:r0 + 32], in_=bbar[64:128, i, q, :]))(i, q, r0), (B_bbar,), (B_xq,))
                    bkq, Bbq = nb()
                    T((lambda bkq: lambda e: e.transpose(bkq[:, 0:128], xq[:, :], ident_f[:, :]))(bkq), (B_xq, B_idf), (Bbq,))
                    A((lambda i, q, bkq: lambda e: e.copy(out=BL[:, 16 * i + q, :], in_=bkq[:, 0:128]))(i, q, bkq), (Bbq,), (B_BL,))
                    yield
            zc = hT[0:32, :, :].bitcast(F32).rearrange("p k n -> p (k n)").rearrange("p (i q n) -> p i q n", i=2, q=16)
            V(lambda e: e.memset(zc[:, :, :, :], 0.0), (), (B_zc,))
            for i, src in enumerate((cre_d, cim_d)):
                for l in range(2):
                    srcv = src.rearrange("(q l) c p -> l c q p", l=2)[l]
                    DMA((lambda i, l, srcv: lambda e: e.dma_start(out=zc[16 * l:16 * l + 16, i, :, 64 * l:64 * l + 64], in_=srcv))(i, l, srcv),
                        (B_zc,), (B_zc,))
            for i in range(2):
                for q in range(16):
                    bkq, Bbq = nb()
                    T((lambda i, q, bkq: lambda e: e.transpose(bkq[:, 0:32], zc[:, i, q, :], ident_f[0:32, 0:32]))(i, q, bkq), (B_zc, B_idf), (Bbq,))
                    if i == 0:
                        A((lambda q, bkq: lambda e: e.copy(out=Cm[:, q, :], in_=bkq[:, 0:32]))(q, bkq), (Bbq,), (B_Cm,))
                        A((lambda q, bkq: lambda e: e.mul(out=Cm[:, 32 + q, :], in_=bkq[:, 0:32], mul=-1.0))(q, bkq), (Bbq,), (B_Cm,))
                    else:
                        A((lambda q, bkq: lambda e: e.mul(out=Cm[:, 16 + q, :], in_=bkq[:, 0:32], mul=-1.0))(q, bkq), (Bbq,), (B_Cm,))
            for c in range(4):
                V((lambda c: lambda e: e.tensor_scalar(out=Dd[:, c, :], in0=ident_f[:, :], scalar1=vt[:, 80 + c:81 + c], scalar2=None,
                                                       op0=ALU.mult))(c), (B_idf, B_vt), (B_Dd,))

            yield

        screp = xt[:, :].rearrange("p (k m) -> p k m", k=16)
        V(lambda e: e.tensor_copy(out=screp, in_=scfm[:, :].unsqueeze(2).to_broadcast([128, 16, 128])), (B_scfm,), (B_xt,))
        wada_v = wada_d.rearrange("(k p) n -> p k n", p=128)
        stg = [arena[:, 0:4096].rearrange("p (k n) -> p k n", k=16), arena[:, 4096:8192].rearrange("p (k n) -> p k n", k=16)]
        B_stg = [Buf("stg0"), Buf("stg1")]
        modbk, B_modbk = banks[7], B_bank[7]
        ssg = ssm_setup_gen()
        ssg_alive = [True]

        def ssg_step(n):
            for _ in range(n):
                if not ssg_alive[0]:
                    return
                try:
                    next(ssg)
                except StopIteration:
                    ssg_alive[0] = False

        for pc in range(24):
            ssg_step(5)
            s = pc % 2
            DMA((lambda pc, s: lambda e: e.dma_start(out=stg[s], in_=wada_v[:, :, pc * 256:(pc + 1) * 256]))(pc, s),
                (), (B_stg[s],))
            if pc < 16:
                for cc in range(2):
                    j = pc * 2 + cc
                    for k in range(16):
                        T((lambda s, cc, k, j: lambda e: e.matmul(modbk[:, j:j + 1], lhsT=stg[s][:, k, cc * 128:(cc + 1) * 128],
                                                                  rhs=scfm[:, k:k + 1], start=(k == 0), stop=(k == 15)))(s, cc, k, j),
                          (B_stg[s], B_scfm), (B_modbk,))
            else:
                gb, B_gb = nb()
                for k in range(16):
                    T((lambda s, k, gb: lambda e: e.matmul(gb[:, 0:256], lhsT=screp[:, k, :], rhs=stg[s][:, k, :],
                                                           start=(k == 0), stop=(k == 15)))(s, k, gb),
                      (B_stg[s], B_xt), (B_gb,))
                c0 = (pc - 16) * 256
                V((lambda gb, c0: lambda e: e.tensor_tensor(out=gateB[:, c0:c0 + 256], in0=gb[:, 0:256], in1=gateB[:, c0:c0 + 256],
                                                            op=ALU.add))(gb, c0), (B_gb, B_gateB), (B_gateB,))
        V(lambda e: e.tensor_tensor(out=modfm[:, :], in0=modbk[:, 0:32], in1=vt[:, 0:32], op=ALU.add), (B_modbk, B_vt), (B_modfm,))
        V(lambda e: e.scalar_tensor_tensor(out=gsfm[:, :], in0=modfm[:, 16:32], scalar=1.0, in1=vt[:, 32:48],
                                           op0=ALU.add, op1=ALU.mult), (B_modfm, B_vt), (B_gsfm,))
        for b in all_cells:
            b.w = dict(B_stg[0].w); b.w.update(B_stg[1].w)
            b.r = dict(B_stg[0].r); b.r.update(B_stg[1].r)

        ssg_step(100000)

        for b in B_hT:
            b.w = dict(B_zc.w); b.r = dict(B_zc.r)

        def rstd_from(ss_ap, out_ap, n, rb, wb):
            A(lambda e: e.activation(out=out_ap, in_=ss_ap, func=AF.Sqrt, bias=epsc[:, 0:1], scale=1.0 / n), tuple(rb) + (B_epsc,), wb)
            V(lambda e: e.reciprocal(out=out_ap, in_=out_ap), wb, wb)

        def load_norm_transpose(row0, slot):
            DMA(lambda e: e.dma_start(out=xt[:, :], in_=x_d[row0:row0 + 128, :]), (), (B_xt,))
            A(lambda e: e.activation(out=junk[:, :], in_=xt[:, :], func=AF.Square, accum_out=stat[:, 0:1]),
              (B_xt,), (B_junk, B_stat[0]))
            rstd_from(stat[:, 0:1], stat[:, 1:2], float(D), (B_stat[0],), (B_stat[1],))
            A(lambda e: e.activation(out=xn[:, :], in_=xt[:, :], func=AF.Identity, scale=stat[:, 1:2]), (B_xt, B_stat[1]), (B_xn,))
            for half in range(2):
                bk_, Bb_ = nb()
                bkv = bk_[:, :].bitcast(BF16)
                for j in range(8):
                    k = half * 8 + j
                    T((lambda bkv, j, k: lambda e: e.transpose(bkv[:, j * 128:(j + 1) * 128], xn[:, k * 128:(k + 1) * 128], ident_b[:, :]))(bkv, j, k),
                      (B_xn, B_idb), (Bb_,))
                for j in range(8):
                    k = half * 8 + j
                    if j % 2 == 0:
                        A((lambda bkv, j, k: lambda e: e.activation(out=hT[:, k, slot * 128:(slot + 1) * 128], in_=bkv[:, j * 128:(j + 1) * 128],
                                                                    func=AF.Identity, scale=gsfm[:, k:k + 1], bias=modfm[:, k:k + 1]))(bkv, j, k),
                          (Bb_, B_gsfm, B_modfm), (B_hT[slot],))
                    else:
                        V((lambda bkv, j, k: lambda e: e.tensor_scalar(out=hT[:, k, slot * 128:(slot + 1) * 128], in0=bkv[:, j * 128:(j + 1) * 128],
                                                                       scalar1=gsfm[:, k:k + 1], scalar2=modfm[:, k:k + 1],
                                                                       op0=ALU.mult, op1=ALU.add))(bkv, j, k),
                          (Bb_, B_gsfm, B_modfm), (B_hT[slot],))

        def load_w(src_ap, ncols, srcbuf):
            s = wsl_rr[0]
            wsl_rr[0] = 1 - s
            DMA(lambda e: e.dma_start(out=wsl[s][:, :, 0:ncols], in_=src_ap), (srcbuf,), (B_wsl[s],))
            return wsl[s], B_wsl[s]

        evac_rr = [0]

        def evac_copy(out_ap, in_ap, rb, wb):
            if evac_rr[0] % 2 == 0:
                A(lambda e: e.copy(out=out_ap, in_=in_ap), rb, wb)
            else:
                V(lambda e: e.tensor_copy(out=out_ap, in_=in_ap), rb, wb)
            evac_rr[0] += 1

        wfm_v = wfm.rearrange("(k p) n -> p k n", p=128)
        wtm_v = wtm.rearrange("(k p) n -> p k n", p=128)
        wo_v = wo.rearrange("(k p) n -> p k n", p=128)

        def fm_chunk(w, Bw, cc, col0, ntok, tok0, dst_ap, dst_buf, hbufs):
            bk_, Bb_ = nb()
            ch = col0 // 128
            for k in range(16):
                T((lambda k: lambda e: e.matmul(bk_[:, 0:ntok], lhsT=w[:, k, cc * 128:(cc + 1) * 128], rhs=hT[:, k, tok0:tok0 + ntok],
                                                start=(k == 0), stop=(k == 15)))(k), (Bw,) + tuple(hbufs), (Bb_,))
            bcol = vtb[:, 48 + ch:49 + ch]
            if evac_rr[0] % 2 == 0:
                A(lambda e: e.activation(out=dst_ap, in_=bk_[:, 0:ntok], func=AF.Identity, bias=bcol, scale=1.0), (Bb_, B_vtb), (dst_buf,))
            else:
                V(lambda e: e.tensor_scalar(out=dst_ap, in0=bk_[:, 0:ntok], scalar1=bcol, scalar2=None, op0=ALU.add), (Bb_, B_vtb), (dst_buf,))
            evac_rr[0] += 1

        def tm_unit(w, Bw, ncols, col0, slot, hbuf):
            bk_, Bb_ = nb()
            for k in range(16):
                T((lambda k: lambda e: e.matmul(bk_[:, 0:ncols], lhsT=hT[:, k, slot * 128:(slot + 1) * 128], rhs=w[:, k, 0:ncols],
                                                start=(k == 0), stop=False))(k), (Bw, hbuf), (Bb_,))
            T(lambda e: e.matmul(bk_[:, 0:ncols], lhsT=ones_b[0:1, 0:128], rhs=brow_tm[0:1, col0:col0 + ncols], start=False, stop=True),
              (B_btm, B_ones), (Bb_,))
            return bk_, Bb_

        if carry:
            npre = NPRE_ST if nst == NST else 2
            win_v = win_d.rearrange("(k p) n -> p k n", p=128)
            wpu = ab[:, 0:8192].rearrange("p (k n) -> p k n", k=16)
            Bwpu = Buf("wpu")
            for cb in all_cells:
                Bwpu.w.update(cb.w); Bwpu.r.update(cb.r)
            hTf = hT[:, :, :].bitcast(F32).rearrange("p k n -> p (k n)").rearrange("p (k n) -> p k n", k=8)
            shrep = xt[:, :].rearrange("p (k m) -> p k m", k=16)
            V(lambda e: e.tensor_copy(out=shrep, in_=modfm[:, 0:16].unsqueeze(2).to_broadcast([128, 16, 128])), (B_modfm,), (B_xt,))
            DMA(lambda e: e.dma_start(out=u0B[:, :], in_=bin_d[0:1, 3456:3968].partition_broadcast(128)), (), (B_u0B,))
            ub, Bub = nb()
            for pc in range(2):
                DMA(lambda e: e.dma_start(out=hTf, in_=win_v[:, 8 * pc:8 * pc + 8, 3456:3968]), (), tuple(B_hT))
                for kk in range(8):
                    k = 8 * pc + kk
                    if k % 2 == 0:
                        V(lambda e: e.tensor_scalar(out=wpu[:, k, :], in0=hTf[:, kk, :], scalar1=gsfm[:, k:k + 1], scalar2=None, op0=ALU.mult),
                          tuple(B_hT) + (B_gsfm,), (Bwpu,))
                    else:
                        A(lambda e: e.activation(out=wpu[:, k, :], in_=hTf[:, kk, :], func=AF.Identity, scale=gsfm[:, k:k + 1]),
                          tuple(B_hT) + (B_gsfm,), (Bwpu,))
                    T(lambda e: e.matmul(ub[:, :], lhsT=shrep[:, k, :], rhs=hTf[:, kk, :], start=(k == 0), stop=(k == 15)),
                      (B_xt,) + tuple(B_hT), (Bub,))
            V(lambda e: e.tensor_tensor(out=u0B[:, :], in0=ub[:, :], in1=u0B[:, :], op=ALU.add), (Bub, B_u0B), (B_u0B,))
            Hr, Hi = hloc[:, 0, :], hloc[:, 1, :]
            c1, c2, c3, c4 = hloc[:, 2, :], hloc[:, 3, :], hloc[:, 4, :], hloc[:, 5, :]
            a128r, a128i = par2[:, 32:48], par2[:, 48:64]
            V(lambda e: e.memset(hloc[:, :, :], 0.0), (), (B_hloc,))
            B_SK = tuple(B_attn)
            SK4r = attn_f[:, 0:64].rearrange("p (q k) -> p q k", q=16)
            SK4i = attn_f[:, 64:128].rearrange("p (q k) -> p q k", q=16)
            xtA = [xt[:, :], arena[:, 4096:6144]]
            B_xtA = [B_xt, Buf("xa1")]
            xnA = [xn[:, :], ab[:, 12288:14336], ab[:, 14336:16384]]
            B_xnA = [B_xn, Buf("xna1"), Buf("xna2")]
            for bb in B_xtA[1:] + B_xnA[1:]:
                for cb in all_cells:
                    bb.w.update(cb.w); bb.r.update(cb.r)
            def stA(n):
                row0 = n * 128
                sl, t = n % 3, n % ST
                xts, Bxts, xns, Bxns = xtA[n % 2], B_xtA[n % 2], xnA[sl], B_xnA[sl]
                DMA(lambda e: e.dma_start(out=xts, in_=xprev_d[row0:row0 + 128, :]), (), (Bxts,))
                A(lambda e: e.activation(out=xns, in_=xts, func=AF.Square, accum_out=ssA[:, sl:sl + 1]), (Bxts,), (Bxns, B_ssA[sl]))
                rstd_from(ssA[:, sl:sl + 1], rsA[:, t:t + 1], float(D), (B_ssA[sl],), (B_rsA[t],))
                A(lambda e: e.activation(out=xns, in_=xts, func=AF.Identity, scale=1.0), (Bxts,), (Bxns,))

            def stB(n):
                sl, t = n % 3, n % ST
                xns, Bxns = xnA[sl], B_xnA[sl]
                for half in range(2):
                    bk_, Bb_ = nb()
                    bkv = bk_[:, :].bitcast(BF16)
                    for j in range(8):
                        k = half * 8 + j
                        T(lambda e: e.transpose(bkv[:, j * 128:(j + 1) * 128], xns[:, k * 128:(k + 1) * 128], ident_b[:, :]), (Bxns, B_idb), (Bb_,))
                    if half == 0:
                        V(lambda e: e.tensor_copy(out=hT[:, half * 8:(half + 1) * 8, t * 128:(t + 1) * 128],
                                                  in_=bkv.rearrange("p (j m) -> p j m", j=8)), (Bb_,), (B_hT[t],))
                    else:
                        A(lambda e: e.copy(out=hT[:, half * 8:(half + 1) * 8, t * 128:(t + 1) * 128],
                                           in_=bkv.rearrange("p (j m) -> p j m", j=8)), (Bb_,), (B_hT[t],))

            def stC(n):
                t = n % ST
                bk_, Bb_ = nb()
                for k in range(16):
                    T(lambda e: e.matmul(bk_[:, :], lhsT=hT[:, k, t * 128:(t + 1) * 128], rhs=wpu[:, k, :], start=(k == 0), stop=(k == 15)),
                      (Bwpu, B_hT[t]), (Bb_,))
                V(lambda e: e.scalar_tensor_tensor(out=uTM[:, t, :], in0=bk_[:, :], scalar=rsA[:, t:t + 1], in1=u0B[:, :],
                                                   op0=ALU.mult, op1=ALU.add), (Bb_, B_rsA[t], B_u0B), (B_mixed,))

            ntile = npre * ST
            for it in range(ntile + 2):
                if it < ntile:
                    stA(it)
                if 0 <= it - 1 < ntile:
                    stB(it - 1)
                if not (0 <= it - 2 < ntile):
                    continue
                stC(it - 2)
                if (it - 2) % ST != ST - 1:
                    continue
                s = (it - 2) // ST
                for hb in range(2):
                    vb = [nb(), nb()]
                    for qq in range(8):
                        q = hb * 8 + qq
                        for gl in range(2):
                            g = 2 * q + gl
                            for part in range(2):
                                T((lambda vb, qq, q, gl, g, part: lambda e: e.matmul(
                                    vb[part][0][64 * gl:64 * gl + 64, qq * 64:(qq + 1) * 64],
                                    lhsT=decT[:, q, part, 64 * gl:64 * gl + 64], rhs=uTM[:, :, g * 16:(g + 1) * 16],
                                    start=True, stop=True))(vb, qq, q, gl, g, part), (B_decT, B_mixed), (vb[part][1],))
                    vr4 = vb[0][0][:, :].rearrange("p (q k c) -> p q k c", q=8, k=4)
                    vi4 = vb[1][0][:, :].rearrange("p (q k c) -> p q k c", q=8, k=4)
                    bbr = bbar[:, 0, hb * 8:hb * 8 + 8, :].unsqueeze(2).to_broadcast([128, 8, 4, 16])
                    bbi = bbar[:, 1, hb * 8:hb * 8 + 8, :].unsqueeze(2).to_broadcast([128, 8, 4, 16])
                    v4 = lambda tl: tl[:, :].rearrange("p (q k c) -> p q k c", q=8, k=4)
                    V((lambda vr4, bbr: lambda e: e.tensor_tensor(out=v4(st1), in0=vr4, in1=bbr, op=ALU.mult))(vr4, bbr), (vb[0][1], B_bbar), (B_st[0],))
                    V((lambda vi4, bbi: lambda e: e.tensor_tensor(out=v4(st2), in0=vi4, in1=bbi, op=ALU.mult))(vi4, bbi), (vb[1][1], B_bbar), (B_st[1],))
                    G(lambda e: e.tensor_tensor(out=st1[:, :], in0=st1[:, :], in1=st2[:, :], op=ALU.subtract), (B_st[0], B_st[1]), (B_st[0],))
                    V((lambda hb: lambda e: e.tensor_reduce(out=SK4r[:, hb * 8:hb * 8 + 8, :], in_=v4(st1), axis=AX.X, op=ALU.add))(hb), (B_st[0],), B_SK)
                    V((lambda vi4, bbr: lambda e: e.tensor_tensor(out=v4(st3), in0=vi4, in1=bbr, op=ALU.mult))(vi4, bbr), (vb[1][1], B_bbar), (B_st[2],))
                    V((lambda vr4, bbi: lambda e: e.tensor_tensor(out=v4(st4), in0=vr4, in1=bbi, op=ALU.mult))(vr4, bbi), (vb[0][1], B_bbar), (B_st[3],))
                    G(lambda e: e.tensor_tensor(out=st3[:, :], in0=st3[:, :], in1=st4[:, :], op=ALU.add), (B_st[2], B_st[3]), (B_st[2],))
                    V((lambda hb: lambda e: e.tensor_reduce(out=SK4i[:, hb * 8:hb * 8 + 8, :], in_=v4(st3), axis=AX.X, op=ALU.add))(hb), (B_st[2],), B_SK)
                for kk in range(ST):
                    ch = s * ST + kk
                    G(lambda e: e.tensor_tensor(out=c1, in0=Hr, in1=a128r, op=ALU.mult), (B_hloc, B_par2), (B_hloc,))
                    G(lambda e: e.tensor_tensor(out=c2, in0=Hi, in1=a128i, op=ALU.mult), (B_hloc, B_par2), (B_hloc,))
                    G(lambda e: e.tensor_tensor(out=c3, in0=Hi, in1=a128r, op=ALU.mult), (B_hloc, B_par2), (B_hloc,))
                    G(lambda e: e.tensor_tensor(out=c4, in0=Hr, in1=a128i, op=ALU.mult), (B_hloc, B_par2), (B_hloc,))
                    G(lambda e: e.tensor_tensor(out=c1, in0=c1, in1=c2, op=ALU.subtract), (B_hloc,), (B_hloc,))
                    G(lambda e: e.tensor_tensor(out=c3, in0=c3, in1=c4, op=ALU.add), (B_hloc,), (B_hloc,))
                    V((lambda kk, ch: lambda e: e.scalar_tensor_tensor(out=Hr, in0=SK4r[:, :, kk], scalar=cmask[:, ch:ch + 1], in1=c1,
                                                                       op0=ALU.mult, op1=ALU.add))(kk, ch), (B_hloc, B_cmask) + B_SK, (B_hloc,))
                    V((lambda kk, ch: lambda e: e.scalar_tensor_tensor(out=Hi, in0=SK4i[:, :, kk], scalar=cmask[:, ch:ch + 1], in1=c3,
                                                                       op0=ALU.mult, op1=ALU.add))(kk, ch), (B_hloc, B_cmask) + B_SK, (B_hloc,))
            for cb in all_cells:
                for bb in B_xtA[1:] + B_xnA[1:] + [Bwpu]:
                    cb.w.update(bb.w); cb.r.update(bb.r)
            for bb in (B_gr, B_gi, B_hr, B_hi, B_g1):
                bb.w = dict(B_decT.w); bb.r = dict(B_decT.r)
            w4 = sm[:, 32:64]
            sth_, cth_ = par2[:, 0:16], par2[:, 16:32]
            V(lambda e: e.tensor_tensor(out=w4[:, 0:16], in0=Hr, in1=cth_, op=ALU.mult), (B_hloc, B_par2), (B_sm,))
            V(lambda e: e.tensor_tensor(out=w4[:, 16:32], in0=Hi, in1=sth_, op=ALU.mult), (B_hloc, B_par2), (B_sm,))
            V(lambda e: e.tensor_tensor(out=gin_r[:, :], in0=w4[:, 0:16], in1=w4[:, 16:32], op=ALU.subtract), (B_sm,), tuple(B_gin))
            V(lambda e: e.tensor_tensor(out=w4[:, 0:16], in0=Hi, in1=cth_, op=ALU.mult), (B_hloc, B_par2), (B_sm,))
            V(lambda e: e.tensor_tensor(out=w4[:, 16:32], in0=Hr, in1=sth_, op=ALU.mult), (B_hloc, B_par2), (B_sm,))
            V(lambda e: e.tensor_tensor(out=gin_i[:, :], in0=w4[:, 0:16], in1=w4[:, 16:32], op=ALU.add), (B_sm,), tuple(B_gin))
        else:
            V(lambda e: e.memset(gin_r[:, :], 0.0), (), tuple(B_gin))
            V(lambda e: e.memset(gin_i[:, :], 0.0), (), tuple(B_gin))

        for bb in (B_mixA, B_mixS):
            bb.w = dict(B_mixed.w); bb.r = dict(B_mixed.r)

        load_norm_transpose(0, 0)
        for g in range(3):
            w, Bw = load_w(wfm_v[:, :, 1536 + 128 * g:1664 + 128 * g], 128, B_wfm)
            fm_chunk(w, Bw, 0, 1536 + 128 * g, 128, 0, KT[:, g, 0:128], B_KT[g], (B_hT[0],))
        w, Bw = load_w(wtm_v[:, :, 0:192], 192, B_wtm)
        bk_, Bb_ = tm_unit(w, Bw, 192, 0, 0, B_hT[0])
        V((lambda bk_: lambda e: e.tensor_copy(out=Vt[:, 0, :, 0:64], in_=bk_[:, 0:192].rearrange("p (g d) -> p g d", g=3)))(bk_),
          (Bb_,), (B_Vt[0],))

        stsets = [[st1, st2, st3, st4], [st5, st6, st7, st8]]
        Bstsets = [B_st, B_stb]
        gsets = [[gr_, gi_, hr_, hi_, p24, ctmp[:, 0:4, :]],
                 [spool2[:, 0:512].rearrange("p (a b) -> p a b", a=4), spool2[:, 512:1024].rearrange("p (a b) -> p a b", a=4),
                  spool2[:, 1024:1280].bitcast(BF16).rearrange("p (a b) -> p a b", a=4),
                  spool2[:, 1280:1536].bitcast(BF16).rearrange("p (a b) -> p a b", a=4), p24b, ctmp[:, 4:8, :]]]
        Bgsets = [[B_gr, B_gi, B_hr, B_hi, B_p24, B_ctmp], [Buf("gr2"), Buf("gi2"), Buf("hr2"), Buf("hi2"), [Buf("p24c"), Buf("p24d")], Buf("ctmp2")]]

        for s in range(nst):
            for t in range(ST):
                load_norm_transpose(128 + (s * ST + t) * 128, t)
            fm_units = [(c0, min(256, 2432 - c0)) for c0 in range(0, 2432, 256)]
            for (c0, ncols) in fm_units:
                w, Bw = load_w(wfm_v[:, :, c0:c0 + ncols], ncols, B_wfm)
                for cc in range(ncols // 128):
                    ch = c0 // 128 + cc
                    if ch < 12:
                        dst, db = QT[:, ch, :], B_QT[ch]
                    elif ch < 15:
                        dst, db = KT[:, ch - 12, 128:640], B_KT[ch - 12]
                    else:
                        dst, db = UT[:, ch - 15, :], B_UT[ch - 15]
                    fm_chunk(w, Bw, cc, c0 + cc * 128, 512, 0, dst, db, B_hT)
            tm_units = [(0, 192)] + [(192 + 256 * i, 256) for i in range(8)]
            for ui, (c0, ncols) in enumerate(tm_units):
                w, Bw = load_w(wtm_v[:, :, c0:c0 + ncols], ncols, B_wtm)
                for t in range(ST):
                    bk_, Bb_ = tm_unit(w, Bw, ncols, c0, t, B_hT[t])
                    if ui == 0:
                        V((lambda bk_, t: lambda e: e.tensor_copy(out=Vt[:, t + 1, :, 0:64],
                                                                  in_=bk_[:, 0:192].rearrange("p (g d) -> p g d", g=3)))(bk_, t),
                          (Bb_,), (B_Vt[t + 1],))
                    elif c0 < 1728:
                        zc0 = c0 - 192
                        A((lambda bk_, t, zc0: lambda e: e.activation(out=za[:, t, zc0:zc0 + 256], in_=bk_[:, 0:256], func=AF.Silu))(bk_, t, zc0),
                          (Bb_,), (B_za[t],))
                    else:
                        zc0 = c0 - 1728
                        A((lambda bk_, t, zc0: lambda e: e.activation(out=zs[:, t, zc0:zc0 + 256], in_=bk_[:, 0:256], func=AF.Silu))(bk_, t, zc0),
                          (Bb_,), (B_zs[t],))

            sst, ast = {}, {}

            def AA(j):
                t, g = divmod(j, 3)
                gt = s * ST + t
                pts = {}
                for ci, (kc0, Mx, BMx) in enumerate(((t * 128, Mp0 if gt == 0 else Mp, B_Mp0 if gt == 0 else B_Mp),
                                                     ((t + 1) * 128, Mc, B_Mc))):
                    for base in (0, 64):
                        idx = (j % 2) * 4 + ci * 2 + base // 64
                        bk_, Bb_ = nb()
                        T(lambda e: e.matmul(bk_[:, 0:512], lhsT=KT[base:base + 64, g, kc0:kc0 + 128],
                                             rhs=QT[base:base + 64, 4 * g:4 * g + 4, t * 128:(t + 1) * 128], start=True, stop=False),
                          (B_KT[g],) + tuple(B_QT[4 * g:4 * g + 4]), (Bb_,))
                        T(lambda e: e.matmul(bk_[:, 0:512], lhsT=ident_b[:, :], rhs=Mx[:, :], start=False, stop=True), (B_idb, BMx), (Bb_,))
                        A(lambda e: e.activation(out=Eb[idx][:, :], in_=bk_[:, 0:512], func=AF.Exp, scale=0.125), (Bb_,), (B_Eb[idx],))
                        pts[(ci, base)] = idx
                ast[j] = pts

            def AB(j):
                t, g = divmod(j, 3)
                pts = ast.pop(j)
                bo = [nb(), nb()]
                for hl in range(8):
                    jj, base = hl // 2, 64 * (hl % 2)
                    ob, Bob = bo[hl // 4]
                    oc = (hl % 4) * 65
                    for ci in range(2):
                        pidx = pts[(ci, base)]
                        vslot = t + ci
                        T(lambda e: e.matmul(ob[:, oc:oc + 65], lhsT=Eb[pidx][:, jj * 128:(jj + 1) * 128], rhs=Vt[:, vslot, g, :],
                                             start=(ci == 0), stop=(ci == 1)), (B_Eb[pidx], B_Vt[vslot]), (Bob,))
                for half in range(2):
                    ob, Bob = bo[half]
                    ov = ob[:, 0:260].rearrange("p (h d) -> p h d", h=4)
                    hs = slice(half * 4, half * 4 + 4)
                    V(lambda e: e.tensor_tensor(out=den[:, g, hs], in0=ov[:, :, 64], in1=esink[:, 8 * g + 4 * half:8 * g + 4 * half + 4], op=ALU.add),
                      (Bob, B_esink), (B_den[g],))
                    V(lambda e: e.reciprocal(out=rden[:, g, hs], in_=den[:, g, hs]), (B_den[g],), (B_rden[g],))
                    a0 = g * 512 + half * 256
                    V(lambda e: e.tensor_tensor(out=attn_f[:, a0:a0 + 256].rearrange("p (h d) -> p h d", h=4), in0=ov[:, :, 0:64],
                                                in1=rden[:, g, hs].unsqueeze(2).to_broadcast([128, 4, 64]), op=ALU.mult),
                      (Bob, B_rden[g]), (B_attn[g],))
                if g == 2:
                    A(lambda e: e.activation(out=junk[:, 0:1536], in_=attn_f[:, :], func=AF.Square, accum_out=stat[:, 2:3]),
                      tuple(B_attn), (B_junk, B_stat[2]))
                    rstd_from(stat[:, 2:3], stat[:, 3:4], 1536.0, (B_stat[2],), (B_stat[3],))
                    V(lambda e: e.scalar_tensor_tensor(out=mixed[:, 0:1536], in0=attn_f[:, :], scalar=stat[:, 3:4], in1=za[:, t, :],
                                                       op0=ALU.mult, op1=ALU.mult), tuple(B_attn) + (B_stat[3], B_za[t]), (B_mixA,))

            def sbufs(qi):
                par = qi % 2
                return stsets[par], Bstsets[par], gsets[par], Bgsets[par]

            def SA(qi):
                t, Q = divmod(qi, 4)
                pr, Bpr = banks[5], B_bank[5]
                pi_, Bpi = banks[6], B_bank[6]
                for i in range(4):
                    T(lambda e: e.matmul(pr[:, i * 128:(i + 1) * 128], lhsT=BL[:, 4 * Q + i, :], rhs=UT[:, Q, t * 128:(t + 1) * 128],
                                         start=True, stop=True), (B_BL, B_UT[Q]), (Bpr,))
                for i in range(4):
                    T(lambda e: e.matmul(pi_[:, i * 128:(i + 1) * 128], lhsT=BL[:, 16 + 4 * Q + i, :], rhs=UT[:, Q, t * 128:(t + 1) * 128],
                                         start=True, stop=True), (B_BL, B_UT[Q]), (Bpi,))
                sst[qi] = (pr, Bpr, pi_, Bpi)

            def SB(qi):
                t, Q = divmod(qi, 4)
                (s1, s2, s3, s4), Bs, (gr, gi, hr, hi, pq, ct_), (Bgr, Bgi, Bhr, Bhi, Bpq, Bct) = sbufs(qi)
                pr, Bpr, pi_, Bpi = sst.pop(qi)
                ctq = CT[:, 4 * Q:4 * Q + 4, :].rearrange("p a b -> p (a b)")
                snq = SN[:, 4 * Q:4 * Q + 4, :].rearrange("p a b -> p (a b)")
                V(lambda e: e.tensor_tensor(out=s1[:, :], in0=pr[:, :], in1=ctq, op=ALU.mult), (Bpr, B_tab), (Bs[0],))
                V(lambda e: e.tensor_tensor(out=s2[:, :], in0=pi_[:, :], in1=snq, op=ALU.mult), (Bpi, B_tab), (Bs[1],))
                V(lambda e: e.tensor_tensor(out=s3[:, :], in0=pi_[:, :], in1=ctq, op=ALU.mult), (Bpi, B_tab), (Bs[2],))
                V(lambda e: e.tensor_tensor(out=s4[:, :], in0=pr[:, :], in1=snq, op=ALU.mult), (Bpr, B_tab), (Bs[3],))
                V(lambda e: e.tensor_tensor(out=s1[:, :], in0=s1[:, :], in1=s2[:, :], op=ALU.add), (Bs[0], Bs[1]), (Bs[0],))
                V(lambda e: e.tensor_tensor(out=s3[:, :], in0=s3[:, :], in1=s4[:, :], op=ALU.subtract), (Bs[2], Bs[3]), (Bs[2],))
                for i in range(4):
                    q = 4 * Q + i
                    V(lambda e: e.tensor_tensor_scan(out=gr[:, i, :], data0=rho[:, q:q + 1].to_broadcast([128, 128]),
                                                     data1=s1[:, i * 128:(i + 1) * 128], initial=gin_r[:, q:q + 1], op0=ALU.mult, op1=ALU.add),
                      (B_par, Bs[0], B_gin[Q]), (Bgr,))
                    V(lambda e: e.tensor_tensor_scan(out=gi[:, i, :], data0=rho[:, q:q + 1].to_broadcast([128, 128]),
                                                     data1=s3[:, i * 128:(i + 1) * 128], initial=gin_i[:, q:q + 1], op0=ALU.mult, op1=ALU.add),
                      (B_par, Bs[2], B_gin[Q]), (Bgi,))

            def SC(qi):
                t, Q = divmod(qi, 4)
                _, _, (gr, gi, hr, hi, pq, ct_), (Bgr, Bgi, Bhr, Bhi, Bpq, Bct) = sbufs(qi)
                ctq = CT[:, 4 * Q:4 * Q + 4, :].rearrange("p a b -> p (a b)")
                snq = SN[:, 4 * Q:4 * Q + 4, :].rearrange("p a b -> p (a b)")
                qs = slice(4 * Q, 4 * Q + 4)
                glr = gr[:, :, 127]; gli = gi[:, :, 127]
                G(lambda e: e.tensor_tensor(out=ct_[:, 0, :], in0=glr, in1=c128[:, qs], op=ALU.mult), (Bgr, B_par), (Bct,))
                G(lambda e: e.tensor_tensor(out=ct_[:, 1, :], in0=gli, in1=s128[:, qs], op=ALU.mult), (Bgi, B_par), (Bct,))
                G(lambda e: e.tensor_tensor(out=ct_[:, 2, :], in0=gli, in1=c128[:, qs], op=ALU.mult), (Bgi, B_par), (Bct,))
                G(lambda e: e.tensor_tensor(out=ct_[:, 3, :], in0=glr, in1=s128[:, qs], op=ALU.mult), (Bgr, B_par), (Bct,))
                G(lambda e: e.tensor_tensor(out=gin_r[:, qs], in0=ct_[:, 0, :], in1=ct_[:, 1, :], op=ALU.subtract), (Bct,), (B_gin[Q],))
                G(lambda e: e.tensor_tensor(out=gin_i[:, qs], in0=ct_[:, 2, :], in1=ct_[:, 3, :], op=ALU.add), (Bct,), (B_gin[Q],))
                grf = gr.rearrange("p a b -> p (a b)"); gif = gi.rearrange("p a b -> p (a b)")
                G(lambda e: e.tensor_tensor(out=hr.rearrange("p a b -> p (a b)"), in0=grf, in1=ctq, op=ALU.mult), (Bgr, B_tab), (Bhr,))
                G(lambda e: e.tensor_tensor(out=pq[:, 0, :], in0=gif, in1=snq, op=ALU.mult), (Bgi, B_tab), (Bpq[0],))
                G(lambda e: e.tensor_tensor(out=hi.rearrange("p a b -> p (a b)"), in0=gif, in1=ctq, op=ALU.mult), (Bgi, B_tab), (Bhi,))
                G(lambda e: e.tensor_tensor(out=pq[:, 1, :], in0=grf, in1=snq, op=ALU.mult), (Bgr, B_tab), (Bpq[1],))

            def SD(qi):
                t, Q = divmod(qi, 4)
                _, _, (gr, gi, hr, hi, pq, ct_), (Bgr, Bgi, Bhr, Bhi, Bpq, Bct) = sbufs(qi)
                ybk, Bybk = banks[7], B_bank[7]
                for i in range(4):
                    q = 4 * Q + i
                    yc = q * 32
                    T(lambda e: e.matmul(ybk[:, yc:yc + 32], lhsT=hr[:, i, :], rhs=Cm[:, q, :], start=True, stop=False), (Bhr, B_Cm), (Bybk,))
                    T(lambda e: e.matmul(ybk[:, yc:yc + 32], lhsT=pq[:, 0, i * 128:(i + 1) * 128], rhs=Cm[:, 32 + q, :], start=False, stop=False),
                      (Bpq[0], B_Cm), (Bybk,))
                    T(lambda e: e.matmul(ybk[:, yc:yc + 32], lhsT=hi[:, i, :], rhs=Cm[:, 16 + q, :], start=False, stop=False), (Bhi, B_Cm), (Bybk,))
                    T(lambda e: e.matmul(ybk[:, yc:yc + 32], lhsT=pq[:, 1, i * 128:(i + 1) * 128], rhs=Cm[:, 16 + q, :], start=False, stop=False),
                      (Bpq[1], B_Cm), (Bybk,))
                    T(lambda e: e.matmul(ybk[:, yc:yc + 32], lhsT=UT[:, Q, t * 128:(t + 1) * 128], rhs=Dd[:, Q, i * 32:(i + 1) * 32],
                                         start=False, stop=True), (B_UT[Q], B_Dd), (Bybk,))
                if Q != 3:
                    return
                A(lambda e: e.activation(out=g1, in_=ybk[:, :], func=AF.Gelu), (Bybk,), (B_g1,))
                V(lambda e: e.tensor_copy(out=g1b[:, :], in_=g1), (B_g1,), (B_g1b,))
                tb_, Btb_ = nb()
                tbv = tb_[:, :].bitcast(BF16)
                for k in range(4):
                    T(lambda e: e.transpose(tbv[:, k * 128:(k + 1) * 128], g1b[:, k * 128:(k + 1) * 128], ident_b[:, :]), (B_g1b, B_idb), (Btb_,))
                V(lambda e: e.tensor_copy(out=g1T[:, :, :].rearrange("p a b -> p (a b)"), in_=tbv[:, 0:512]), (Btb_,), (B_g1T,))
                gbk, Bgbk = nb()
                for k in range(4):
                    T(lambda e: e.matmul(gbk[:, :], lhsT=g1T[:, k, :], rhs=wglu[:, k, :], start=(k == 0), stop=False), (B_g1T, B_wglu), (Bgbk,))
                T(lambda e: e.matmul(gbk[:, :], lhsT=ones_b[0:1, 0:128], rhs=brow_glu[0:1, :], start=False, stop=True), (B_ones, B_bglu), (Bgbk,))
                A(lambda e: e.activation(out=sg[:, :], in_=gbk[:, :], func=AF.Sigmoid), (Bgbk,), (B_sg,))
                V(lambda e: e.tensor_tensor(out=s2[:, :], in0=g1, in1=sg[:, :], op=ALU.mult), (B_g1, B_sg), (B_s2,))
                A(lambda e: e.activation(out=junk[:, 1536:2048], in_=s2[:, :], func=AF.Square, accum_out=stat[:, 4:5]), (B_s2,), (B_junk, B_stat[4]))
                rstd_from(stat[:, 4:5], stat[:, 5:6], 512.0, (B_stat[4],), (B_stat[5],))
                V(lambda e: e.scalar_tensor_tensor(out=mixed[:, 1536:2048], in0=s2[:, :], scalar=stat[:, 5:6], in1=zs[:, t, :],
                                                   op0=ALU.mult, op1=ALU.mult), (B_s2, B_stat[5], B_zs[t]), (B_mixS,))
                for half in range(2):
                    bk_, Bb_ = nb()
                    bkv = bk_[:, :].bitcast(BF16)
                    for j in range(8):
                        c = half * 8 + j
                        T(lambda e: e.transpose(bkv[:, j * 128:(j + 1) * 128], mixed[:, c * 128:(c + 1) * 128], ident_b[:, :]),
                          (B_mixA, B_mixS, B_idb), (Bb_,))
                    for j in range(8):
                        c = half * 8 + j
                        if j % 2 == 0:
                            A(lambda e: e.activation(out=hT[:, c, t * 128:(t + 1) * 128], in_=bkv[:, j * 128:(j + 1) * 128],
                                                     func=AF.Identity, scale=vt[:, 48 + c:49 + c]), (Bb_, B_vt), (B_hT[t],))
                        else:
                            V(lambda e: e.tensor_scalar(out=hT[:, c, t * 128:(t + 1) * 128], in0=bkv[:, j * 128:(j + 1) * 128],
                                                        scalar1=vt[:, 48 + c:49 + c], scalar2=None, op0=ALU.mult), (Bb_, B_vt), (B_hT[t],))

            NQ = ST * 4
            pend_ab = None
            bank_pool[0] = [0, 1, 2, 3, 4]
            for k in range(NQ + 3):
                if pend_ab is not None:
                    AB(pend_ab)
                    pend_ab = None
                if k < NQ and k % 4 != 3:
                    j = (k // 4) * 3 + k % 4
                    AA(j)
                    pend_ab = j
                if 0 <= k - 1 < NQ:
                    SB(k - 1)
                if k < NQ:
                    SA(k)
                if 0 <= k - 2 < NQ:
                    SC(k - 2)
                if 0 <= k - 3 < NQ:
                    SD(k - 3)
            bank_pool[0] = list(range(7))

            if s + 1 < nst:
                for g in range(3):
                    G((lambda g: lambda e: e.tensor_copy(out=KT[:, g, 0:128], in_=KT[:, g, 512:640]))(g), (B_KT[g],), (B_KT[g],))
                G(lambda e: e.tensor_copy(out=Vt[:, 0, :, 0:64], in_=Vt[:, 4, :, 0:64]), (B_Vt[4],), (B_Vt[0],))

            for t in range(ST):
                row0 = 128 + (s * ST + t) * 128
                DMA((lambda t, row0: lambda e: e.dma_start(out=xr(t), in_=x_d[row0:row0 + 128, :]))(t, row0), (), tuple(xr_cells(t)))
            for u in range(8):
                w, Bw = load_w(wo_v[:, :, u * 256:(u + 1) * 256], 256, B_wo)
                for t in range(ST):
                    bk_, Bb_ = nb()
                    for c in range(16):
                        T(lambda e: e.matmul(bk_[:, 0:256], lhsT=hT[:, c, t * 128:(t + 1) * 128], rhs=w[:, c, :],
                                             start=(c == 0), stop=(c == 15)), (Bw, B_hT[t]), (Bb_,))
                    ot = (st2, st6)[t % 2][:, 0:256]
                    Bot = (B_st[1], B_stb[1])[t % 2]
                    V(lambda e: e.tensor_tensor(out=ot, in0=bk_[:, 0:256], in1=gateB[:, u * 256:(u + 1) * 256], op=ALU.mult),
                      (Bb_, B_gateB), (Bot,))
                    G(lambda e: e.tensor_tensor(out=xr(t)[:, u * 256:(u + 1) * 256], in0=xr(t)[:, u * 256:(u + 1) * 256], in1=ot, op=ALU.add),
                      (Bot,) + tuple(xr_cells(t)), tuple(xr_cells(t)))
            for t in range(ST):
                gt = s * ST + t
                cl = tuple(xr_cells(t))
                A((lambda t: lambda e: e.activation(out=junk[:, :], in_=xr(t), func=AF.Square, accum_out=stat[:, 6:7]))(t), cl, (B_junk, B_stat[6]))
                rstd_from(stat[:, 6:7], stat[:, 7:8], float(D), (B_stat[6],), (B_stat[7],))
                V((lambda t: lambda e: e.scalar_tensor_tensor(out=xr(t), in0=xr(t), scalar=stat[:, 7:8], in1=fgB[:, :],
                                                              op0=ALU.mult, op1=ALU.mult))(t), cl + (B_stat[7], B_fgB), cl)
                DMA((lambda t, gt: lambda e: e.dma_start(out=out_d[gt * 128:(gt + 1) * 128, :], in_=xr(t)))(t, gt), cl, (), out=True)

        nsem = P.finalize(40)
        sems = {e: [es.enter_context(nc.semaphore("s_%s_%d" % (e, i))) for i in range(nsem[e])] for e in Plan.ENGS}
        sems["dma"] = [es.enter_context(nc.semaphore("s_dma_%d" % i)) for i in range(40)]
        with nc.Block() as block:
            @block.sync
            def _(e): P.run_stream("sync", e, sems)

            @block.scalar
            def _(e): P.run_stream("scalar", e, sems)

            @block.vector
            def _(e): P.run_stream("vector", e, sems)

            @block.gpsimd
            def _(e): P.run_stream("gpsimd", e, sems)

            @block.tensor
            def _(e): P.run_stream("tensor", e, sems)
    return nc


def make_in_maps(inp):
    f = lambda a: np.ascontiguousarray(np.asarray(a, dtype=np.float32))
    x = f(inp["x"]); c = f(inp["c"])
    b_ada = f(inp["b_ada"])[0]
    shared = {
        "w_ada": f(inp["w_ada"])[0], "b_ada": b_ada.reshape(1, -1), "w_in": f(inp["w_in"])[0],
        "b_in": f(inp["b_in"])[0].reshape(1, -1), "w_out": f(inp["w_out"])[0], "glu_w": f(inp["glu_w"])[0],
        "glu_b": f(inp["glu_b"])[0].reshape(1, -1), "final_gain": f(inp["final_gain"]).reshape(1, -1),
        "sinks": f(inp["attn_sinks"])[0].reshape(1, -1),
        "b_re": f(inp["ssm_b_re"])[0], "b_im": f(inp["ssm_b_im"])[0],
        "c_re": f(inp["ssm_c_re"])[0], "c_im": f(inp["ssm_c_im"])[0],
    }
    vecsB = np.concatenate([f(inp["ssm_lambda_re"])[0].reshape(16, 128), f(inp["ssm_lambda_im"])[0].reshape(16, 128),
                            np.repeat(f(inp["ssm_log_step"])[0], 64).reshape(16, 128),
                            shared["b_in"][0, 0:1536].reshape(12, 128),
                            np.stack([np.tile(shared["b_in"][0, 1536 + 64 * g:1600 + 64 * g], 2) for g in range(3)], 0),
                            shared["b_in"][0, 3456:3968].reshape(4, 128)], 0)
    maps = []
    for i in range(NCORES):
        b, j = i // 4, i % 4
        t0 = j * T_CORE
        xs = np.zeros((128 + T_CORE, D), np.float32)
        xs[128:] = x[b, t0:t0 + T_CORE]
        if j > 0:
            xs[:128] = x[b, t0 - 128:t0]
        vecsA = np.concatenate([b_ada[0:4096].reshape(32, 128), f(inp["norm_gain"])[0].reshape(16, 128),
                                f(inp["attn_out_gain"])[0].reshape(12, 128), f(inp["ssm_out_gain"])[0].reshape(4, 128),
                                c[b].reshape(16, 128), f(inp["ssm_d"])[0].reshape(4, 128)], 0)
        flags = np.zeros((128, 4), np.float32)
        flags[:, 0] = 1.0 if j > 0 else 0.0
        npre = NPRE_ST * ST * 128
        xprev = np.zeros((npre, D), np.float32)
        cmask = np.zeros((128, NPRE_ST * ST), np.float32)
        if j > 0:
            xprev[npre - t0:] = x[b, 0:t0]
            cmask[:, (npre - t0) // 128:] = 1.0
        m = dict(shared)
        m.update({"x": xs, "vecsA": np.ascontiguousarray(vecsA), "vecsB": np.ascontiguousarray(vecsB), "flags": flags, "xprev": xprev, "cmask": cmask})
        maps.append(m)
    return maps


def kernel(**inputs):
    nc = build_program()
    maps = make_in_maps(inputs)
    res = run_bass_kernel_spmd(nc, maps, core_ids=list(range(NCORES)))
    out = np.zeros((2, 4 * T_CORE, D), np.float32)
    for i in range(NCORES):
        b, j = i // 4, i % 4
        out[b, j * T_CORE:(j + 1) * T_CORE] = res.results[i]["out"]
    return out
```
